# Optimizing a Trainium2 kernel written in Bass

```python
import jax, jax.numpy as jnp
from jax import lax
import numpy as np

D_MODEL = 1024
BATCH = 2
SEQ = 8192
DEPTH = 2
DEC_BATCH = 8
DEC_SEQ = 4096
PAST_LEN = 128

HEAD_DIM = 64
ROPE_DIM = HEAD_DIM // 4
ROPE_THETA = 500000.0
EPS = 1e-6
NEG_INF = -1e30

A_PATTERNS = ((128, 1), (512, 4), (2048, 16))
A_GROUPS = len(A_PATTERNS)
A_HEADS = 8
A_WIDTH = A_HEADS * HEAD_DIM

B_Q_HEADS = 8
B_KV_HEADS = 2
B_HALF_WINDOW = 128
B_BLOCK = 128
B_WIDTH = B_Q_HEADS * HEAD_DIM

C_HEADS = 8
C_DK = 64
C_DV = 64
C_CONV = 5
C_CHUNK = 64
C_WIDTH = C_HEADS * C_DV
C_QKV = 2 * C_HEADS * C_DK + C_HEADS * C_DV

N_BRANCH = 3
FFN_HIDDEN = ((8 * D_MODEL + 3 * 256 - 1) // (3 * 256)) * 256

IN_SPLITS = (
    A_GROUPS * A_HEADS * HEAD_DIM, A_GROUPS * A_HEADS * HEAD_DIM, A_GROUPS * A_HEADS * HEAD_DIM,
    B_Q_HEADS * HEAD_DIM, B_KV_HEADS * HEAD_DIM, B_KV_HEADS * HEAD_DIM,
    C_HEADS * C_DK, C_HEADS * C_DK, C_HEADS * C_DV, C_WIDTH, 2 * C_HEADS, 2 * C_HEADS,
)
IN_WIDTH = sum(IN_SPLITS)

kernel_name = 'hybrid_dilated_window_deltanet_encoder'


def rms_norm(x, gain):
    xf = x.astype(jnp.float32)
    y = xf * lax.rsqrt(jnp.mean(xf * xf, axis=-1, keepdims=True) + EPS)
    return (y * gain.astype(jnp.float32)).astype(x.dtype)


def l2_norm(x):
    return x * lax.rsqrt(jnp.sum(x * x, axis=-1, keepdims=True) + EPS)


def rope_tables(seq_len):
    pos = jnp.arange(seq_len, dtype=jnp.float32)
    inv_freq = jnp.power(jnp.float32(ROPE_THETA), -jnp.arange(0, ROPE_DIM, 2, dtype=jnp.float32) / ROPE_DIM)
    ang = pos[:, None] * inv_freq[None, :]
    return jnp.cos(ang), jnp.sin(ang)


def apply_partial_rope(x, cos, sin):
    half = ROPE_DIM // 2
    xf = x.astype(jnp.float32)
    c, s = cos[:, None, :], sin[:, None, :]
    x1, x2 = xf[..., :half], xf[..., half:ROPE_DIM]
    out = jnp.concatenate([x1 * c - x2 * s, x2 * c + x1 * s, xf[..., ROPE_DIM:]], axis=-1)
    return out.astype(x.dtype)


def band_windows(t, blk, seq_axis):
    ax = t.ndim + seq_axis
    nb = t.shape[ax] // blk
    tb = t.reshape(t.shape[:ax] + (nb, blk) + t.shape[ax + 1:])
    pad = [(0, 0)] * tb.ndim
    pad[ax] = (1, 1)
    tp = jnp.pad(tb, pad)
    return jnp.concatenate([lax.slice_in_dim(tp, i, i + nb, axis=ax) for i in range(3)], axis=ax + 1)


def banded_attention(q, k, v, valid, half_window, blk, sink=None):
    lead = q.shape[:-3]
    L, hq, hd = q.shape[-3:]
    hk = k.shape[-2]
    rep = hq // hk
    nb = L // blk
    qb = q.reshape(lead + (nb, blk, hk, rep, hd))
    kw = band_windows(k, blk, -3)
    vw = band_windows(v, blk, -3)
    kvalid = band_windows(valid, blk, -1)
    s = jnp.einsum('...nqgrd,...nkgd->...ngrqk', qb, kw, preferred_element_type=jnp.float32) * (hd ** -0.5)
    rel = jnp.arange(blk)[:, None] - (jnp.arange(3 * blk)[None, :] - blk)
    mask = (jnp.abs(rel) <= half_window) & kvalid[..., :, None, None, None, :]
    s = jnp.where(mask, s, NEG_INF)
    m = jnp.max(s, axis=-1)
    if sink is not None:
        sink_l = sink.astype(jnp.float32).reshape(hk, rep)[:, :, None]
        m = jnp.maximum(m, sink_l)
    p = jnp.exp(s - m[..., None])
    den = jnp.sum(p, axis=-1)
    if sink is not None:
        den = den + jnp.exp(sink_l - m)
    o = jnp.einsum('...ngrqk,...nkgd->...nqgrd', p, vw.astype(jnp.float32))
    den_t = jnp.moveaxis(den, -1, -3)
    o = (o / den_t[..., None]).reshape(lead + (L, hq, hd))
    lse = (jnp.moveaxis(m, -1, -3) + jnp.log(den_t)).reshape(lead + (L, hq))
    return o.astype(q.dtype), lse


def dilated_group(q, k, v, dilation, half_span):
    B, S, H, hd = q.shape
    blk = half_span
    unit = dilation * blk
    Lp = -(-S // unit) * unit
    Ls = Lp // dilation

    def to_sub(t):
        t = jnp.pad(t, ((0, 0), (0, Lp - S), (0, 0), (0, 0)))
        return t.reshape(B, Ls, dilation, H, hd).transpose(0, 2, 1, 3, 4)

    valid = jnp.arange(Lp).reshape(Ls, dilation).T < S
    o, lse = banded_attention(to_sub(q), to_sub(k), to_sub(v), valid, half_span, blk)
    o = o.transpose(0, 2, 1, 3, 4).reshape(B, Lp, H, hd)[:, :S]
    lse = lse.transpose(0, 2, 1, 3).reshape(B, Lp, H)[:, :S]
    return o, lse


def mixer_a(q, k, v):
    B, S = q.shape[:2]
    outs, lses = [], []
    for gi, (window, dilation) in enumerate(A_PATTERNS):
        sl = slice(gi * A_HEADS, (gi + 1) * A_HEADS)
        o, lse = dilated_group(q[:, :, sl], k[:, :, sl], v[:, :, sl], dilation, (window // 2) // dilation)
        outs.append(o.astype(jnp.float32))
        lses.append(lse)
    wts = jax.nn.softmax(jnp.stack(lses, 0), axis=0)
    o = jnp.sum(wts[..., None] * jnp.stack(outs, 0), axis=0)
    return o.reshape(B, S, A_WIDTH).astype(q.dtype)


def gated_delta_chunked(q, k, v, g, beta):
    P, T, H, dk = q.shape
    dv = v.shape[-1]
    N = T // C_CHUNK

    def chunks(t):
        t = t.astype(jnp.float32).reshape((P, N, C_CHUNK, H) + t.shape[3:])
        return jnp.moveaxis(t, 3, 1)

    qc, kc, vc, gc, bc = chunks(q), chunks(k), chunks(v), chunks(g), chunks(beta)
    gcum = jnp.cumsum(gc, axis=-1)
    idx = jnp.arange(C_CHUNK)
    incl = idx[:, None] >= idx[None, :]
    strict = idx[:, None] > idx[None, :]
    decay = jnp.exp(jnp.where(incl, gcum[..., :, None] - gcum[..., None, :], NEG_INF))
    kb = kc * bc[..., None]
    vb = vc * bc[..., None]
    low = jnp.where(strict, jnp.einsum('phnid,phnjd->phnij', kb, kc) * decay, 0.0)
    a_mat = low + jnp.eye(C_CHUNK, dtype=jnp.float32)
    rhs = jnp.concatenate([vb, kb * jnp.exp(gcum)[..., None]], axis=-1)
    sol = lax.linalg.triangular_solve(a_mat, rhs, left_side=True, lower=True, unit_diagonal=True)
    u, w = sol[..., :dv], sol[..., dv:]
    qk = jnp.where(incl, jnp.einsum('phnid,phnjd->phnij', qc, kc) * decay, 0.0)
    qg = qc * jnp.exp(gcum)[..., None]
    kd = kc * jnp.exp(gcum[..., -1:] - gcum)[..., None]
    glast = jnp.exp(gcum[..., -1])

    def step(state, xs):
        u_i, w_i, qk_i, qg_i, kd_i, gl_i = xs
        v_new = u_i - jnp.einsum('phcd,phde->phce', w_i, state)
        o_i = jnp.einsum('phcd,phde->phce', qg_i, state) + jnp.einsum('phij,phje->phie', qk_i, v_new)
        state = state * gl_i[..., None, None] + jnp.einsum('phcd,phce->phde', kd_i, v_new)
        return state, o_i

    xs = tuple(jnp.moveaxis(t, 2, 0) for t in (u, w, qk, qg, kd, glast))
    state0 = jnp.zeros((P, H, dk, dv), jnp.float32)
    _, o = lax.scan(step, state0, xs)
    return jnp.transpose(o, (1, 0, 3, 2, 4)).reshape(P, T, H, dv)


def short_conv(x, w):
    ch = x.shape[-1]
    return lax.conv_general_dilated(x, w.astype(x.dtype)[:, None, :], window_strides=(1,),
                                    padding=((C_CONV // 2, C_CONV // 2),),
                                    dimension_numbers=('NWC', 'WIO', 'NWC'), feature_group_count=ch)


def mixer_c(qc, kc, vc, zc, b_raw, a_raw, conv_w, a_log, dt_bias, o_gain):
    B, S = qc.shape[:2]
    qkv = jax.nn.silu(short_conv(jnp.concatenate([qc, kc, vc], axis=-1), conv_w).astype(jnp.float32))
    q, k, v = jnp.split(qkv, [C_HEADS * C_DK, 2 * C_HEADS * C_DK], axis=-1)
    q = l2_norm(q.reshape(B, S, C_HEADS, C_DK)) * (C_DK ** -0.5)
    k = l2_norm(k.reshape(B, S, C_HEADS, C_DK))
    v = v.reshape(B, S, C_HEADS, C_DV)
    beta = jax.nn.sigmoid(b_raw.astype(jnp.float32)).reshape(B, S, 2, C_HEADS)
    g = -jnp.exp(a_log.astype(jnp.float32)) * jax.nn.softplus(
        a_raw.astype(jnp.float32).reshape(B, S, 2, C_HEADS) + dt_bias.astype(jnp.float32))
    rev = lambda t: jnp.flip(t, axis=1)
    o = gated_delta_chunked(
        jnp.concatenate([q, rev(q)], 0), jnp.concatenate([k, rev(k)], 0), jnp.concatenate([v, rev(v)], 0),
        jnp.concatenate([g[:, :, 0], rev(g[:, :, 1])], 0), jnp.concatenate([beta[:, :, 0], rev(beta[:, :, 1])], 0))
    o = o[:B] + rev(o[B:])
    o = rms_norm(o, o_gain) * jax.nn.silu(zc.astype(jnp.float32).reshape(B, S, C_HEADS, C_DV))
    return o.reshape(B, S, C_WIDTH).astype(qc.dtype)


def encoder_layer(x, norm1, w_in, qk_gain, sink, conv_w, a_log, dt_bias, o_gain, w_gate,
                  w_br_a, w_br_b, w_br_c, w_out, norm2, w_ffn_in, w_ffn_out):
    B, S, _ = x.shape
    h = rms_norm(x, norm1)
    proj = jnp.einsum('bsd,de->bse', h, w_in)
    qa, ka, va, qb, kb, vb, qc, kc, vc, zc, bc, ac = jnp.split(proj, np.cumsum(IN_SPLITS)[:-1].tolist(), axis=-1)
    cos, sin = rope_tables(S)
    ha = A_GROUPS * A_HEADS
    qa = apply_partial_rope(rms_norm(qa.reshape(B, S, ha, HEAD_DIM), qk_gain[0]), cos, sin)
    ka = apply_partial_rope(rms_norm(ka.reshape(B, S, ha, HEAD_DIM), qk_gain[1]), cos, sin)
    o_a = mixer_a(qa, ka, va.reshape(B, S, ha, HEAD_DIM))
    qb = apply_partial_rope(rms_norm(qb.reshape(B, S, B_Q_HEADS, HEAD_DIM), qk_gain[2]), cos, sin)
    kb = apply_partial_rope(rms_norm(kb.reshape(B, S, B_KV_HEADS, HEAD_DIM), qk_gain[3]), cos, sin)
    o_b, _ = banded_attention(qb, kb, vb.reshape(B, S, B_KV_HEADS, HEAD_DIM), jnp.ones((S,), dtype=bool),
                              B_HALF_WINDOW, B_BLOCK, sink)
    o_b = o_b.reshape(B, S, B_WIDTH)
    o_c = mixer_c(qc, kc, vc, zc, bc, ac, conv_w, a_log, dt_bias, o_gain)
    gates = jax.nn.sigmoid(jnp.einsum('bsd,de->bse', h, w_gate).astype(jnp.float32))
    gates = gates.reshape(B, S, N_BRANCH, D_MODEL).astype(x.dtype)
    merged = (gates[:, :, 0] * jnp.einsum('bsc,cd->bsd', o_a, w_br_a)
              + gates[:, :, 1] * jnp.einsum('bsc,cd->bsd', o_b, w_br_b)
              + gates[:, :, 2] * jnp.einsum('bsc,cd->bsd', o_c, w_br_c))
    x = x + jnp.einsum('bsd,de->bse', merged, w_out)
    h2 = rms_norm(x, norm2)
    gate, up = jnp.split(jnp.einsum('bsd,df->bsf', h2, w_ffn_in), 2, axis=-1)
    return x + jnp.einsum('bsf,fd->bsd', jax.nn.silu(gate) * up, w_ffn_out)


def setup_inputs(seed: int = 0) -> dict:
    key = jax.random.key(seed)
    ks = jax.random.split(key, 20)

    def nrm(k, shape, scale):
        return jax.random.normal(k, shape, jnp.float32) * scale

    dt = jnp.exp(jax.random.uniform(ks[9], (DEPTH, 2, C_HEADS), jnp.float32,
                                    float(np.log(1e-3)), float(np.log(1e-1))))
    return {
        'x_prompt': nrm(ks[0], (BATCH, SEQ, D_MODEL), 1.0),
        'x_sample': nrm(ks[1], (DEC_BATCH, DEC_SEQ, D_MODEL), 1.0),
        'norm1': 1.0 + nrm(ks[2], (DEPTH, D_MODEL), 0.02),
        'w_in': nrm(ks[3], (DEPTH, D_MODEL, IN_WIDTH), D_MODEL ** -0.5),
        'qk_gain': 1.0 + nrm(ks[4], (DEPTH, 4, HEAD_DIM), 0.02),
        'sink': nrm(ks[5], (DEPTH, B_Q_HEADS), 0.5),
        'conv_w': nrm(ks[6], (DEPTH, C_CONV, C_QKV), C_CONV ** -0.5),
        'a_log': jnp.log(jax.random.uniform(ks[7], (DEPTH, 2, C_HEADS), jnp.float32, 1.0, 16.0)),
        'dt_bias': dt + jnp.log(-jnp.expm1(-dt)),
        'o_gain': 1.0 + nrm(ks[8], (DEPTH, C_DV), 0.02),
        'w_gate': nrm(ks[10], (DEPTH, D_MODEL, N_BRANCH * D_MODEL), D_MODEL ** -0.5),
        'w_br_a': nrm(ks[11], (DEPTH, A_WIDTH, D_MODEL), A_WIDTH ** -0.5),
        'w_br_b': nrm(ks[12], (DEPTH, B_WIDTH, D_MODEL), B_WIDTH ** -0.5),
        'w_br_c': nrm(ks[13], (DEPTH, C_WIDTH, D_MODEL), C_WIDTH ** -0.5),
        'w_out': nrm(ks[14], (DEPTH, D_MODEL, D_MODEL), D_MODEL ** -0.5),
        'norm2': 1.0 + nrm(ks[15], (DEPTH, D_MODEL), 0.02),
        'w_ffn_in': nrm(ks[16], (DEPTH, D_MODEL, 2 * FFN_HIDDEN), D_MODEL ** -0.5),
        'w_ffn_out': nrm(ks[17], (DEPTH, FFN_HIDDEN, D_MODEL), FFN_HIDDEN ** -0.5),
    }


def reference(x_prompt, x_sample, norm1, w_in, qk_gain, sink, conv_w, a_log, dt_bias, o_gain, w_gate,
              w_br_a, w_br_b, w_br_c, w_out, norm2, w_ffn_in, w_ffn_out):
    y_prompt, y_sample = x_prompt, x_sample
    for l in range(DEPTH):
        layer_w = (norm1[l], w_in[l], qk_gain[l], sink[l], conv_w[l], a_log[l], dt_bias[l], o_gain[l],
                   w_gate[l], w_br_a[l], w_br_b[l], w_br_c[l], w_out[l], norm2[l], w_ffn_in[l], w_ffn_out[l])
        y_prompt = encoder_layer(y_prompt, *layer_w)
        y_sample = encoder_layer(y_sample, *layer_w)
    return (y_prompt, y_sample)
```

```python
import contextlib
import numpy as np
import concourse.bass as bass
import concourse.mybir as mybir
from concourse.bass_utils import run_bass_kernel_spmd

F32 = mybir.dt.float32
BF16 = mybir.dt.bfloat16
AF = mybir.ActivationFunctionType
ALU = mybir.AluOpType
AX = mybir.AxisListType

D = 1024
INW = 7456
FH = 2816
EPS = 1e-6
DIL = (1, 4, 16)
NCONST = 13
HS = 72


class Res:
    __slots__ = ("w", "r")

    def __init__(self):
        self.w = None
        self.r = {}


class Chan:
    def __init__(self, sem):
        self.sem = sem
        self.cnt = 0


class Eng:
    def __init__(self, e, sem):
        self.e = e
        self.sem = sem
        self.cnt = 0
        self.waited = {}
        self.pend = []

    def wait(self, toks):
        need = {}
        for t in toks:
            if t is None:
                continue
            s, v = t
            if need.get(s, 0) < v:
                need[s] = v
        for s, v in need.items():
            if self.waited.get(s, 0) < v:
                self.e.wait_ge(s, v)
                self.waited[s] = v


def _deps(reads, writes):
    toks = []
    for b in reads:
        toks.append(b.w)
    for b in writes:
        toks.append(b.w)
        toks.extend(b.r.items())
    return toks


def _commit(tok, reads, writes):
    s, v = tok
    for b in reads:
        if b.r.get(s, 0) < v:
            b.r[s] = v
    for b in writes:
        b.w = tok
        b.r = {}


class K:
    def __init__(self, nc):
        self.nc = nc
        self.es = contextlib.ExitStack()
        self.nsem = 0
        self.pe = Eng(nc.tensor, self.sem("pe"))
        self.act = Eng(nc.scalar, self.sem("act"))
        self.dve = Eng(nc.vector, self.sem("dve"))
        self.pool = Eng(nc.gpsimd, self.sem("pool"))
        self.sp = Eng(nc.sync, None)
        self.stopped = False
        self.chans = []
        self.chan_by_name = {}
        self.dram = {}

    def sem(self, name):
        self.nsem += 1
        return self.es.enter_context(self.nc.semaphore(name))

    def chan(self, name):
        c = self.chan_by_name.get(name)
        if c is None:
            c = Chan(self.sem("c_" + name))
            self.chans.append(c)
            self.chan_by_name[name] = c
        return c

    def dres(self, *key):
        r = self.dram.get(key)
        if r is None:
            r = self.dram[key] = Res()
        return r

    def op(self, eng, fn, reads=(), writes=(), inc=True):
        if self.stopped:
            return
        eng.wait(_deps(reads, writes))
        ins = fn()
        if not inc:
            eng.pend.append((reads, writes))
            return
        eng.cnt += 1
        ins.then_inc(eng.sem, 1)
        tok = (eng.sem, eng.cnt)
        for (r, w) in eng.pend:
            _commit(tok, r, w)
        eng.pend = []
        _commit(tok, reads, writes)

    def dma(self, q, ch, out, in_, reads=(), writes=(), slow=False):
        if self.stopped:
            return
        toks = _deps(reads, writes)
        if ch.cnt:
            toks.append((ch.sem, ch.cnt))
        q.wait(toks)
        if slow:
            ins = q.e.dma_start(out=out, in_=in_, allow_slow_non_contiguous=True)
        else:
            ins = q.e.dma_start(out=out, in_=in_)
        ch.cnt += 16
        ins.then_inc(ch.sem, 16)
        _commit((ch.sem, ch.cnt), reads, writes)

    def barrier(self):
        toks = [(e.sem, e.cnt) for e in (self.pe, self.act, self.dve, self.pool) if e.cnt]
        toks += [(c.sem, c.cnt) for c in self.chans if c.cnt]
        for e in (self.pe, self.act, self.dve, self.pool, self.sp):
            e.wait(toks)


class _Stop(Exception):
    pass


class Tl:
    def __init__(self, t):
        self.t = t
        self.res = Res()

    def __getitem__(self, k):
        return self.t[k]


def build_program(T, NL, dbg=None):
    NT = T // 128
    nc = bass.Bass("TRN2", target_bir_lowering=False)
    k = K(nc)
    PE, ACT, DVE, POOL, SP = k.pe, k.act, k.dve, k.pool, k.sp

    def din(name, shape, dt=F32):
        return nc.dram_tensor(name, list(shape), dt, kind="ExternalInput").ap()

    def dscr(name, shape, dt=F32):
        return nc.dram_tensor(name, list(shape), dt, kind="Internal").ap()

    x_in = din("x", [T, D])
    link_in = din("link", [128, 1])
    rope_in = din("rope", [3, T, 32])
    const_in = din("consts", [128, NCONST * 128])
    norm1 = din("norm1", [NL, D]); w_in = din("w_in", [NL, D, INW]); qk_gain = din("qk_gain", [NL, 4, 64])
    sink = din("sink", [NL, 8]); conv_w = din("conv_w", [NL, 5, 1536]); a_log = din("a_log", [NL, 16])
    dt_bias = din("dt_bias", [NL, 16]); o_gain = din("o_gain", [NL, 64]); w_gate = din("w_gate", [NL, D, 3 * D])
    w_br = [din("w_br_a", [NL, 512, D]), din("w_br_b", [NL, 512, D]), din("w_br_c", [NL, 512, D])]
    w_out = din("w_out", [NL, D, D]); norm2 = din("norm2", [NL, D]); w_f1 = din("w_ffn_in", [NL, D, 2 * FH])
    w_f2 = din("w_ffn_out", [NL, FH, D])
    y_out = nc.dram_tensor("y", [T, D], F32, kind="ExternalOutput").ap()

    xmid = dscr("xmid", [T, D]); x1d = dscr("x1d", [T, D])
    qTa = dscr("qTa", [3, 8, 64, T], BF16); kTa = dscr("kTa", [3, 8, 64, T + 128], BF16)
    va = dscr("va", [3, T + 128, 8 * HS], BF16)
    qTb = dscr("qTb", [8, 64, T], BF16); kTb = dscr("kTb", [2, 64, T + 256], BF16); vbd = dscr("vbd", [T + 256, 2 * HS], BF16)
    cx = dscr("cx", [1536, T + 4]); zs = dscr("zs", [T, 512]); bgd = dscr("bgd", [T, 32])
    oA = dscr("oA", [3, T, 520]); obd = dscr("obd", [T, 512], BF16)
    kTc = dscr("kTc", [8, 64, T], BF16); qTc = dscr("qTc", [8, 64, T], BF16)
    ktok = dscr("ktok", [T, 512], BF16); vtok = dscr("vtok", [T, 512], BF16)
    ofd = dscr("ofd", [T, 512]); obw = dscr("obw", [T, 512]); ocd = dscr("ocd", [T, 512], BF16)

    st = contextlib.ExitStack()

    uid = [0]

    def sb(es, name, shape, dt=F32):
        uid[0] += 1
        return Tl(es.enter_context(nc.sbuf_tensor(f"{name}_{uid[0]}", list(shape), dt)))

    def psb(es, name, shape, dt=F32):
        uid[0] += 1
        return Tl(es.enter_context(nc.psum_tensor(f"{name}_{uid[0]}", list(shape), dt)))

    def V(ap, pat, **kw):
        return ap.rearrange(pat, **kw)

    with k.es, st:
        cst = sb(st, "cst", [128, NCONST, 128])
        cbf = sb(st, "cbf", [128, NCONST, 128], BF16)
        linkc = sb(st, "linkc", [128, 1])
        zero_t = sb(st, "zero_t", [128, 1024], BF16)
        zero_f = sb(st, "zero_f", [128, 4])
        rstdP = sb(st, "rstdP", [128, NT])
        rstd2 = sb(st, "rstd2", [128, NT])
        ch_c = k.chan("const")
        k.dma(SP, ch_c, cst[:], V(const_in, "p (c n) -> p c n", c=NCONST), writes=[cst.res])
        ch_c2 = k.chan("const2")
        k.dma(SP, ch_c2, linkc[:], link_in, writes=[linkc.res])
        k.op(DVE, lambda: nc.vector.tensor_copy(out=cbf[:], in_=cst[:]), [cst.res], [cbf.res])
        k.op(POOL, lambda: nc.gpsimd.memset(zero_t[:], 0.0), [], [zero_t.res])
        k.op(POOL, lambda: nc.gpsimd.memset(zero_f[:], 0.0), [], [zero_f.res])
        IDENT, MLO, MHI, MLOE, MHIE, CMi, CMTi, NEGF, NEGB, SMF, SMB, ONES, BONES = range(13)
        ident = cbf[:, IDENT, :]
        def build_masks(es_m):
            m4 = sb(es_m, "m4", [128, 9, 512], BF16)
            m3 = sb(es_m, "m3", [128, 5, 384], BF16)
            mk = sb(es_m, "mk", [128, 8, 128])
            for i, (full, edge) in enumerate(((MLO, MLOE), (MHI, MHIE))):
                k.op(DVE, lambda i=i, full=full, edge=edge: nc.vector.tensor_sub(out=mk[:, 4 + i, :], in0=cst[:, full, :], in1=cst[:, edge, :]),
                     [cst.res], [mk.res])
                k.op(DVE, lambda i=i, edge=edge: nc.vector.scalar_tensor_tensor(out=mk[:, i, :], in0=mk[:, 4 + i, :], scalar=linkc[:, 0:1],
                                                                                  in1=cst[:, edge, :], op0=ALU.mult, op1=ALU.add),
                     [mk.res, linkc.res, cst.res], [mk.res])
                k.op(DVE, lambda i=i, full=full: nc.vector.tensor_scalar(out=mk[:, 2 + i, :], in0=cst[:, full, :], scalar1=linkc[:, 0:1],
                                                                           scalar2=None, op0=ALU.mult),
                     [cst.res, linkc.res], [mk.res])
            lo_src = {"N": cst[:, MLO, :], "E": cst[:, MLOE, :], "S": mk[:, 0, :]}
            hi_src = {"N": cst[:, MHI, :], "E": cst[:, MHIE, :], "S": mk[:, 1, :]}
            M4IDX = {}
            for a_i, a in enumerate("NES"):
                for b_i, b in enumerate("NES"):
                    idx = a_i * 3 + b_i
                    M4IDX[(a, b)] = idx
                    for u in range(4):
                        src = lo_src[a] if u % 2 == 0 else hi_src[b]
                        k.op(DVE, lambda idx=idx, u=u, src=src: nc.vector.tensor_copy(out=m4[:, idx, u * 128:(u + 1) * 128], in_=src),
                             [cst.res, mk.res], [m4.res])
            lo3 = {"N": cst[:, MLO, :], "L": mk[:, 2, :], "Z": None}
            hi3 = {"N": cst[:, MHI, :], "L": mk[:, 3, :], "Z": None}
            M3IDX = {}
            for idx, (a, b) in enumerate((("Z", "N"), ("N", "N"), ("N", "L"), ("L", "N"), ("N", "Z"))):
                M3IDX[(a, b)] = idx
                for u, src in ((0, lo3[a]), (1, cst[:, ONES, :]), (2, hi3[b])):
                    if src is None:
                        k.op(DVE, lambda idx=idx, u=u: nc.vector.memset(m3[:, idx, u * 128:(u + 1) * 128], 0.0), [], [m3.res])
                    else:
                        k.op(DVE, lambda idx=idx, u=u, src=src: nc.vector.tensor_copy(out=m3[:, idx, u * 128:(u + 1) * 128], in_=src),
                             [cst.res, mk.res], [m3.res])
            return m4, m3, M4IDX, M3IDX

        ch_z = k.chan("zpad")
        for g in range(3):
            for e0 in (0, T + 64):
                k.dma(SP, ch_z, V(kTa[g][:, :, e0:e0 + 64], "q p n -> p q n"), V(zero_t[0:64, 0:512], "p (q n) -> p q n", q=8), [zero_t.res], [])
                k.dma(SP, ch_z, va[g][e0:e0 + 64, :], zero_t[0:64, 0:8 * HS], [zero_t.res], [])
        for e0 in (0, T + 128):
            k.dma(SP, ch_z, V(kTb[:, :, e0:e0 + 128], "q p n -> p q n"), V(zero_t[0:64, 0:256], "p (q n) -> p q n", q=2), [zero_t.res], [])
            k.dma(SP, ch_z, vbd[e0:e0 + 128, :], zero_t[:, 0:2 * HS], [zero_t.res], [])
        for cb in range(12):
            for e0 in (0, T + 2):
                k.dma(SP, ch_z, cx[cb * 128:(cb + 1) * 128, e0:e0 + 2], zero_f[:, 0:2], [zero_f.res], [])
        k.barrier()

        def rstd_prepass(es, src, rstd, tag):
            xs = [sb(es, f"rp_x{tag}{i}", [128, D]) for i in range(2)]
            chs = [k.chan(f"rp{i}") for i in range(2)]
            junk = sb(es, f"rp_j{tag}", [128, D])
            ssq = sb(es, f"rp_s{tag}", [128, NT])
            for t in range(NT):
                xt = xs[t % 2]
                k.dma(SP, chs[t % 2], xt[:], src[t * 128:(t + 1) * 128, :], [k.dres(src.tensor.name, t)], [xt.res])
                k.op(ACT, lambda xt=xt, t=t: nc.scalar.activation(out=junk[:], in_=xt[:], func=AF.Square, accum_out=ssq[:, t:t + 1]),
                     [xt.res], [junk.res, ssq.res])
            k.op(DVE, lambda: nc.vector.tensor_scalar(out=ssq[:], in0=ssq[:], scalar1=1.0 / D, scalar2=EPS, op0=ALU.mult, op1=ALU.add),
                 [ssq.res], [ssq.res])
            k.op(ACT, lambda: nc.scalar.activation(out=ssq[:], in_=ssq[:], func=AF.Ln), [ssq.res], [ssq.res])
            k.op(ACT, lambda: nc.scalar.activation(out=rstd[:], in_=ssq[:], func=AF.Exp, scale=-0.5), [ssq.res], [rstd.res])

        def rinv(eng_small, v, n_res):
            k.op(ACT, lambda: nc.scalar.activation(out=v, in_=v, func=AF.Ln), [n_res], [n_res])
            k.op(ACT, lambda: nc.scalar.activation(out=v, in_=v, func=AF.Exp, scale=-0.5), [n_res], [n_res])

        def transposes(src_tl, src_ap_fn, n, ps_tl, dst_tl, dst_ap, evac_eng, extra_reads=()):
            for c in range(n):
                k.op(PE, lambda c=c: nc.tensor.transpose(out=ps_tl[:, c * 128:(c + 1) * 128], in_=src_ap_fn(c), identity=ident),
                     [src_tl.res, cbf.res] + list(extra_reads), [ps_tl.res], inc=(c == n - 1))
            if evac_eng is ACT:
                k.op(ACT, lambda: nc.scalar.copy(out=dst_ap, in_=V(ps_tl[:, 0:n * 128], "p (c n) -> p c n", c=n)), [ps_tl.res], [dst_tl.res])
            else:
                k.op(DVE, lambda: nc.vector.tensor_copy(out=dst_ap, in_=V(ps_tl[:, 0:n * 128], "p (c n) -> p c n", c=n)), [ps_tl.res], [dst_tl.res])

        def load_bc(es, name, src_row, n, ch):
            t = sb(es, name, [128, n])
            k.dma(SP, ch, t[:], src_row.partition_broadcast(128), [], [t.res])
            return t

        def ckpt(name):
            if dbg == name:
                k.stopped = True

        def run_layers():
          for l in range(NL):
            src_x = x_in if l == 0 else xmid
            ckpt("P0")
            dst_y = xmid if l < NL - 1 else y_out

            with contextlib.ExitStack() as es:
                hT = sb(es, "hT", [128, 8, T], BF16)
                rstd = rstdP
                chg = k.chan("gen")
                qkg = load_bc(es, "qkg", V(qk_gain[l:l + 1], "o a d -> o (a d)"), 256, chg)
                alog = load_bc(es, "alog", a_log[l:l + 1, :], 16, chg)
                dtb = load_bc(es, "dtb", dt_bias[l:l + 1, :], 16, chg)
                negA = sb(es, "negA", [128, 16])
                k.op(ACT, lambda: nc.scalar.activation(out=negA[:], in_=alog[:], func=AF.Exp), [alog.res], [negA.res])
                k.op(DVE, lambda: nc.vector.tensor_scalar(out=negA[:], in0=negA[:], scalar1=-1.0, scalar2=None, op0=ALU.mult), [negA.res], [negA.res])
                if l == 0:
                    with contextlib.ExitStack() as es2:
                        rstd_prepass(es2, src_x, rstd, "P")
                        k.barrier()
                ckpt("P1")
                pst = [psb(es, f"P_pst{i}", [128, 1024], BF16) for i in range(2)]
                with contextlib.ExitStack() as es0:
                    gain1 = load_bc(es0, "gain1", norm1[l:l + 1, :], D, chg)
                    xs = [sb(es0, f"P_x{i}", [128, D]) for i in range(2)]
                    chx = [k.chan(f"P_x{i}") for i in range(2)]
                    hbs = [sb(es0, f"P_hb{i}", [128, D], BF16) for i in range(2)]
                    for t in range(NT):
                        xt, hb, pt = xs[t % 2], hbs[t % 2], pst[t % 2]
                        k.dma(SP, chx[t % 2], xt[:], src_x[t * 128:(t + 1) * 128, :], [k.dres(src_x.tensor.name, t)], [xt.res])
                        k.op(DVE, lambda xt=xt, hb=hb, t=t: nc.vector.scalar_tensor_tensor(out=hb[:], in0=xt[:], scalar=rstd[:, t:t + 1], in1=gain1[:],
                                                                                           op0=ALU.mult, op1=ALU.mult),
                             [xt.res, rstd.res, gain1.res], [hb.res])
                        transposes(hb, lambda c, hb=hb: hb[:, c * 128:(c + 1) * 128], 8, pt, hT, hT[:, :, t * 128:(t + 1) * 128], ACT)
                    k.barrier()
                ckpt("P2")
                KP = 4
                wb = [sb(es, f"P_w{i}", [128, 8, 512], BF16) for i in range(2)]
                chw = [k.chan(f"P_w{i}") for i in range(2)]
                psm = [psb(es, f"P_psm{i}", [128, 512]) for i in range(KP)]
                sqs = [sb(es, f"P_sq{i}", [128, 512]) for i in range(KP)]
                sss = [sb(es, f"P_ss{i}", [128, 8]) for i in range(KP)]
                qn = [sb(es, f"P_qn{i}", [128, 512]) for i in range(KP)]
                tAs = [sb(es, f"P_tA{i}", [128, 8, 16]) for i in range(KP)]
                tBs = [sb(es, f"P_tB{i}", [128, 8, 16]) for i in range(KP)]
                qo = [sb(es, f"P_qo{i}", [128, 512], BF16) for i in range(KP)]
                qTs = [sb(es, f"P_qT{i}", [128, 4, 128], BF16) for i in range(KP)]
                chq = [k.chan(f"P_q{i}") for i in range(KP)]
                rtab = sb(es, "P_rtab", [128, NT, 32])
                chr_ = k.chan("P_r0")
                rtab_g = [-1]

                def need_rope(g):
                    if rtab_g[0] != g:
                        k.dma(SP, chr_, rtab[:], V(rope_in[g], "(t p) c -> p t c", p=128), [], [rtab.res])
                        rtab_g[0] = g
                vo = [sb(es, f"P_vo{i}", [128, 8, HS], BF16) for i in range(KP)]
                chv = [k.chan(f"P_v{i}") for i in range(KP)]
                fo = [sb(es, f"P_fo{i}", [128, 512]) for i in range(2)]
                chf = [k.chan(f"P_f{i}") for i in range(2)]
                sm = [sb(es, f"P_sm{i}", [128, 32]) for i in range(3)]
                for v_ in vo:
                    k.op(POOL, lambda v_=v_: nc.gpsimd.memset(v_[:], 1.0), [], [v_.res])
                cnt = {"w": 0, "i": 0}

                def lockstep(gens):
                    gens = list(gens)
                    while gens:
                        nxt = []
                        for g_ in gens:
                            try:
                                next(g_)
                                nxt.append(g_)
                            except StopIteration:
                                pass
                        gens = nxt

                wplan = [(kind_ * 1536 + g_ * 512, 512) for kind_ in range(3) for g_ in range(3)] + [(4608, 512), (5120, 256)] + \
                        [(5376 + kind_ * 512, 512) for kind_ in range(3)] + [(6912, 512), (7424, 32)]

                def issue_w(i):
                    c0, ncols = wplan[i]
                    k.dma(POOL, chw[i % 2], wb[i % 2][:, :, 0:ncols], V(w_in[l][:, c0:c0 + ncols], "(c p) n -> p c n", p=128), [], [wb[i % 2].res])

                def load_w(c0, ncols):
                    i = cnt["w"]
                    assert wplan[i] == (c0, ncols), (i, wplan[i], c0, ncols)
                    if i == 0:
                        issue_w(0)
                    if i + 1 < len(wplan):
                        issue_w(i + 1)
                    cnt["w"] += 1
                    return wb[i % 2]

                def tok_cols(g, pm):
                    d = DIL[g]
                    per = NT // d
                    r, m = pm // per, pm % per
                    s0 = r + d * 128 * m
                    return slice(s0, s0 + d * 127 + 1, d)

                def proj_tm(w, ncols, cols, ps):
                    for c in range(8):
                        k.op(PE, lambda c=c: nc.tensor.matmul(ps[:, 0:ncols], lhsT=hT[:, c, cols], rhs=w[:, c, 0:ncols], start=(c == 0), stop=(c == 7)),
                             [hT.res, w.res], [ps.res], inc=(c == 7))

                def qk_post(j, ps, H, gi, g, pm):
                    n = H * 64
                    qn_t, sq, ss, tA, tB, qo_t = qn[j], sqs[j], sss[j], tAs[j], tBs[j], qo[j]
                    rt = rtab[:, pm, :]
                    k.op(ACT, lambda: nc.scalar.activation(out=sq[:, 0:n], in_=ps[:, 0:n], func=AF.Square), [ps.res], [sq.res])
                    yield
                    k.op(DVE, lambda: nc.vector.tensor_reduce(out=ss[:, 0:H], in_=V(sq[:, 0:n], "p (h d) -> p h d", h=H), axis=AX.X, op=ALU.add),
                         [sq.res], [ss.res])
                    k.op(DVE, lambda: nc.vector.tensor_scalar(out=ss[:, 0:H], in0=ss[:, 0:H], scalar1=1.0 / 64, scalar2=EPS, op0=ALU.mult, op1=ALU.add),
                         [ss.res], [ss.res])
                    yield
                    rinv(None, ss[:, 0:H], ss.res)
                    yield
                    q3 = V(qn_t[:, 0:n], "p (h d) -> p h d", h=H)
                    k.op(DVE, lambda: nc.vector.tensor_tensor(out=q3, in0=V(ps[:, 0:n], "p (h d) -> p h d", h=H),
                                                              in1=ss[:, 0:H].unsqueeze(2).broadcast_to([128, H, 64]), op=ALU.mult),
                         [ps.res, ss.res], [qn_t.res])
                    yield
                    k.op(POOL, lambda: nc.gpsimd.tensor_tensor(out=q3, in0=q3, in1=qkg[:, gi * 64:(gi + 1) * 64].unsqueeze(1).broadcast_to([128, H, 64]),
                                                               op=ALU.mult), [qn_t.res, qkg.res], [qn_t.res])
                    k.op(POOL, lambda: nc.gpsimd.tensor_tensor(out=tA[:, 0:H, :], in0=q3[:, :, 0:16], in1=rt[:, 0:16].unsqueeze(1).broadcast_to([128, H, 16]),
                                                               op=ALU.mult), [qn_t.res, rtab.res], [tA.res])
                    k.op(POOL, lambda: nc.gpsimd.tensor_tensor(out=tB[:, 0:H, :], in0=q3[:, :, 0:16], in1=rt[:, 16:32].unsqueeze(1).broadcast_to([128, H, 16]),
                                                               op=ALU.mult), [qn_t.res, rtab.res], [tB.res])
                    yield
                    o3 = V(qo_t[:, 0:n], "p (h d) -> p h d", h=H)
                    k.op(DVE, lambda: nc.vector.tensor_sub(out=o3[:, :, 0:8], in0=tA[:, 0:H, 0:8], in1=tB[:, 0:H, 8:16]), [tA.res, tB.res], [qo_t.res])
                    k.op(DVE, lambda: nc.vector.tensor_add(out=o3[:, :, 8:16], in0=tA[:, 0:H, 8:16], in1=tB[:, 0:H, 0:8]), [tA.res, tB.res], [qo_t.res])
                    k.op(ACT, lambda: nc.scalar.copy(out=o3[:, :, 16:64], in_=q3[:, :, 16:64]), [qn_t.res], [qo_t.res])
                    yield

                def store_T(j, npair, dst_ap, dres_key):
                    qT, qo_t = qTs[j], qo[j]
                    transposes(qo_t, lambda c: qo_t[:, c * 128:(c + 1) * 128], npair, pst[j % 2], qT, qT[:, 0:npair, :], ACT)
                    yield
                    for e in range(2):
                        k.dma(SP, chq[j], V(dst_ap(e), "q p n -> p q n"), qT[e * 64:(e + 1) * 64, 0:npair, :], [qT.res], [k.dres(*dres_key)])

                def store_v(j, ps, H, c0, dst_ap, dres_key):
                    v_ = vo[j]
                    k.op(ACT, lambda: nc.scalar.copy(out=v_[:, 0:H, 0:64], in_=V(ps[:, c0:c0 + H * 64], "p (h d) -> p h d", h=H)), [ps.res], [v_.res])
                    yield
                    k.dma(SP, chv[j], dst_ap, V(v_[:, 0:H, :], "p h d -> p (h d)"), [v_.res], [k.dres(*dres_key)])

                def tile_A(j, w, kind, g, pm):
                    ps = psm[j]
                    proj_tm(w, 512, tok_cols(g, pm), ps)
                    yield
                    if kind < 2:
                        yield from qk_post(j, ps, 8, kind, g, pm)
                        if kind == 0:
                            yield from store_T(j, 4, lambda e: qTa[g][e:8:2, :, pm * 128:(pm + 1) * 128], ("qTa", g, pm))
                        else:
                            yield from store_T(j, 4, lambda e: kTa[g][e:8:2, :, 64 + pm * 128:64 + (pm + 1) * 128], ("kTa", g, pm))
                    else:
                        yield from store_v(j, ps, 8, 0, va[g][64 + pm * 128:64 + (pm + 1) * 128, :], ("va", g, pm))

                def tile_Bq(j, w, pm):
                    ps = psm[j]
                    proj_tm(w, 512, slice(pm * 128, (pm + 1) * 128), ps)
                    yield
                    yield from qk_post(j, ps, 8, 2, 0, pm)
                    yield from store_T(j, 4, lambda e: qTb[e:8:2, :, pm * 128:(pm + 1) * 128], ("qTb", pm))

                def tile_Bkv(j, w, pm):
                    ps = psm[j]
                    proj_tm(w, 256, slice(pm * 128, (pm + 1) * 128), ps)
                    yield
                    yield from qk_post(j, ps, 2, 3, 0, pm)
                    yield from store_T(j, 1, lambda e: kTb[e:e + 1, :, 128 + pm * 128:128 + (pm + 1) * 128], ("kTb", pm))
                    yield from store_v(j, ps, 2, 128, vbd[128 + pm * 128:128 + (pm + 1) * 128, :], ("vbd", pm))

                for kind in range(3):
                    for g in range(3):
                        w = load_w(kind * 1536 + g * 512, 512)
                        if kind < 2:
                            need_rope(g)
                        for p0 in range(0, NT, KP):
                            lockstep([tile_A(j, w, kind, g, p0 + j) for j in range(min(KP, NT - p0))])
                ckpt("P3")
                w = load_w(4608, 512)
                need_rope(0)
                for p0 in range(0, NT, KP):
                    lockstep([tile_Bq(j, w, p0 + j) for j in range(min(KP, NT - p0))])
                w = load_w(5120, 256)
                for p0 in range(0, NT, KP):
                    lockstep([tile_Bkv(j, w, p0 + j) for j in range(min(KP, NT - p0))])
                ckpt("P4")
                for kind in range(3):
                    w = load_w(5376 + kind * 512, 512)
                    for c4 in range(4):
                        for tr in range(T // 512):
                            i = cnt["i"]; cnt["i"] += 1
                            ps = psm[i % 3]
                            for c in range(8):
                                k.op(PE, lambda c=c, c4=c4, tr=tr, ps=ps: nc.tensor.matmul(ps[:, :], lhsT=w[:, c, c4 * 128:(c4 + 1) * 128],
                                                                                         rhs=hT[:, c, tr * 512:(tr + 1) * 512], start=(c == 0), stop=(c == 7)),
                                     [hT.res, w.res], [ps.res], inc=(c == 7))
                            f = fo[i % 2]
                            if i % 2 == 0:
                                k.op(ACT, lambda f=f, ps=ps: nc.scalar.copy(out=f[:], in_=ps[:]), [ps.res], [f.res])
                            else:
                                k.op(DVE, lambda f=f, ps=ps: nc.vector.tensor_copy(out=f[:], in_=ps[:]), [ps.res], [f.res])
                            cb = kind * 4 + c4
                            k.dma(SP, chf[i % 2], cx[cb * 128:(cb + 1) * 128, 2 + tr * 512:2 + (tr + 1) * 512], f[:], [f.res], [k.dres("cx", cb, tr // 4)])
                ckpt("P5")
                w = load_w(6912, 512)
                for pm in range(NT):
                    ps = psm[pm % 3]
                    proj_tm(w, 512, slice(pm * 128, (pm + 1) * 128), ps)
                    f = fo[pm % 2]
                    k.op(ACT, lambda f=f, ps=ps: nc.scalar.activation(out=f[:], in_=ps[:], func=AF.Silu), [ps.res], [f.res])
                    k.dma(SP, chf[pm % 2], zs[pm * 128:(pm + 1) * 128, :], f[:], [f.res], [k.dres("zs", pm)])
                ckpt("P6")
                w = load_w(7424, 32)
                for pm in range(NT):
                    ps = psm[pm % 3]
                    proj_tm(w, 32, slice(pm * 128, (pm + 1) * 128), ps)
                    f = fo[pm % 2]
                    s0, s1, s2 = sm
                    k.op(ACT, lambda f=f, ps=ps: nc.scalar.activation(out=f[:, 0:16], in_=ps[:, 0:16], func=AF.Sigmoid), [ps.res], [f.res])
                    k.op(DVE, lambda ps=ps: nc.vector.tensor_add(out=s0[:, 0:16], in0=ps[:, 16:32], in1=dtb[:]), [ps.res, dtb.res], [s0.res])
                    k.op(DVE, lambda: nc.vector.tensor_scalar(out=s1[:, 0:16], in0=s0[:, 0:16], scalar1=30.0, scalar2=None, op0=ALU.min), [s0.res], [s1.res])
                    k.op(ACT, lambda: nc.scalar.activation(out=s1[:, 0:16], in_=s1[:, 0:16], func=AF.Exp), [s1.res], [s1.res])
                    k.op(DVE, lambda: nc.vector.tensor_scalar(out=s1[:, 0:16], in0=s1[:, 0:16], scalar1=1.0, scalar2=None, op0=ALU.add), [s1.res], [s1.res])
                    k.op(ACT, lambda: nc.scalar.activation(out=s1[:, 0:16], in_=s1[:, 0:16], func=AF.Ln), [s1.res], [s1.res])
                    k.op(DVE, lambda: nc.vector.tensor_scalar(out=s2[:, 0:16], in0=s0[:, 0:16], scalar1=30.0, scalar2=-30.0, op0=ALU.max, op1=ALU.add), [s0.res], [s2.res])
                    k.op(DVE, lambda: nc.vector.tensor_add(out=s2[:, 0:16], in0=s2[:, 0:16], in1=s1[:, 0:16]), [s1.res, s2.res], [s2.res])
                    k.op(DVE, lambda f=f: nc.vector.tensor_mul(out=f[:, 16:32], in0=s2[:, 0:16], in1=negA[:]), [s2.res, negA.res], [f.res])
                    k.dma(SP, chf[pm % 2], bgd[pm * 128:(pm + 1) * 128, :], f[:, 0:32], [f.res], [k.dres("bgd", pm)])
                k.barrier()
            if dbg == "P":
                break

            with contextlib.ExitStack() as es:
                m4, m3, M4IDX, M3IDX = build_masks(es)
                qs = [sb(es, f"A_q{i}", [64, 8, 512], BF16) for i in range(2)]
                ks = [sb(es, f"A_k{i}", [64, 8, 768], BF16) for i in range(2)]
                vs = [sb(es, f"A_v{i}", [128, 6, 8 * HS], BF16) for i in range(2)]
                chq = [k.chan(f"A_q{i}") for i in range(2)]
                chk = [k.chan(f"A_k{i}") for i in range(2)]
                chv = [k.chan(f"A_v{i}") for i in range(2)]
                psS = [psb(es, f"A_pS{i}", [128, 512]) for i in range(4)]
                psO = [psb(es, f"A_pO{i}", [128, 512]) for i in range(4)]
                pt = [sb(es, f"A_pt{i}", [128, 512], BF16) for i in range(4)]
                pmk = [sb(es, f"A_pm{i}", [128, 512], BF16) for i in range(4)]
                osb = [sb(es, f"A_o{i}", [128, 8, 65]) for i in range(2)]
                cho = [k.chan(f"A_o{i}") for i in range(2)]
                obb = [sb(es, f"A_ob{i}", [128, 512], BF16) for i in range(2)]
                chg = k.chan("gen")
                esink = load_bc(es, "esink", sink[l:l + 1, :], 8, chg)
                k.op(ACT, lambda: nc.scalar.activation(out=esink[:], in_=esink[:], func=AF.Exp), [esink.res], [esink.res])
                den = sb(es, "A_den", [128, 8])
                NSL = 4

                def rolling(unit_iter, K):
                    active = []
                    it = iter(unit_iter)
                    done = False
                    while True:
                        nxt = []
                        for g_ in active:
                            try:
                                next(g_)
                                nxt.append(g_)
                            except StopIteration:
                                pass
                        active = nxt
                        if not done and len(active) < K:
                            try:
                                g_ = next(it)
                                next(g_)
                                active.append(g_)
                            except StopIteration:
                                done = True
                        if done and not active:
                            break

                uctr = [0]
                octr = [0]

                def loads_A(g, sbk):
                    a = sbk * 512
                    q_, k_, v_ = qs[sbk % 2], ks[sbk % 2], vs[sbk % 2]
                    k.dma(SP, chq[sbk % 2], q_[:], V(qTa[g][:, :, a:a + 512], "q p n -> p q n"),
                          [k.dres("qTa", g, sbk * 4 + j) for j in range(4)], [q_.res])
                    k.dma(SP, chk[sbk % 2], k_[:, :, 0:640], V(kTa[g][:, :, a:a + 640], "q p n -> p q n"),
                          [k.dres("kTa", g, j) for j in range(max(0, sbk * 4 - 1), min(NT, sbk * 4 + 5))], [k_.res])
                    k.dma(SP, chv[sbk % 2], v_[:, 0:5, :], V(va[g][a:a + 640, :], "(t p) c -> p t c", p=128),
                          [k.dres("va", g, j) for j in range(max(0, sbk * 4 - 1), min(NT, sbk * 4 + 5))], [v_.res])

                def loads_B(sbk):
                    a = sbk * 512
                    q_, k_, v_ = qs[sbk % 2], ks[sbk % 2], vs[sbk % 2]
                    k.dma(SP, chq[sbk % 2], q_[:], V(qTb[:, :, a:a + 512], "q p n -> p q n"), [k.dres("qTb", sbk * 4 + j) for j in range(4)], [q_.res])
                    rng = range(max(0, sbk * 4 - 1), min(NT, sbk * 4 + 5))
                    k.dma(SP, chk[sbk % 2], k_[:, 0:2, :], V(kTb[:, :, a:a + 768], "q p n -> p q n"), [k.dres("kTb", j) for j in rng], [k_.res])
                    k.dma(SP, chv[sbk % 2], v_[:, :, 0:2 * HS], V(vbd[a:a + 768, :], "(t p) c -> p t c", p=128), [k.dres("vbd", j) for j in rng], [v_.res])

                def unit_A(g, sbk, qi, hp, midx, oslot):
                    sl = uctr[0] % NSL
                    uctr[0] += 1
                    q_, k_, v_ = qs[sbk % 2], ks[sbk % 2], vs[sbk % 2]
                    pS, p_t, p_m = psS[sl], pt[sl], pmk[sl]
                    pO = (psO[oslot * 2], psO[oslot * 2 + 1])
                    for e in range(2):
                        for u in range(2):
                            k.op(PE, lambda e=e, u=u: nc.tensor.matmul(
                                pS[:, (2 * e + u) * 128:(2 * e + u + 1) * 128], lhsT=k_[:, 2 * hp + e, (qi + u) * 128:(qi + u + 1) * 128],
                                rhs=q_[:, 2 * hp + e, qi * 128:(qi + 1) * 128], start=True, stop=True),
                                [k_.res, q_.res], [pS.res], inc=(e == 1 and u == 1))
                    yield
                    k.op(ACT, lambda: nc.scalar.activation(out=p_t[:], in_=pS[:], func=AF.Exp, scale=0.125), [pS.res], [p_t.res])
                    yield
                    meng = DVE if sl % 2 == 0 else POOL
                    k.op(meng, lambda: meng.e.tensor_tensor(out=p_m[:], in0=p_t[:], in1=m4[:, midx, :], op=ALU.mult), [p_t.res, m4.res], [p_m.res])
                    yield
                    for e in range(2):
                        h = 2 * hp + e
                        po = pO[h // 4]
                        for u in range(2):
                            k.op(PE, lambda e=e, u=u, h=h, po=po: nc.tensor.matmul(
                                po[:, (h % 4) * HS:(h % 4) * HS + 65], lhsT=p_m[:, (2 * e + u) * 128:(2 * e + u + 1) * 128],
                                rhs=v_[:, qi + u, h * HS:h * HS + 65], start=(u == 0), stop=(u == 1)),
                                [p_m.res, v_.res], [po.res], inc=(e == 1 and u == 1))
                    if hp == 3:
                        yield
                        yield from fin_A(g, sbk * 4 + qi, oslot)

                def fin_A(g, pm, oslot):
                    d = DIL[g]
                    per = NT // d
                    o_ = osb[oslot]
                    pO = (psO[oslot * 2], psO[oslot * 2 + 1])
                    k.op(ACT, lambda: nc.scalar.copy(out=o_[:, 0:4, :], in_=V(pO[0][:, 0:4 * HS], "p (h d) -> p h d", h=4)[:, :, 0:65]), [pO[0].res], [o_.res])
                    k.op(DVE, lambda: nc.vector.tensor_copy(out=o_[:, 4:8, :], in_=V(pO[1][:, 0:4 * HS], "p (h d) -> p h d", h=4)[:, :, 0:65]), [pO[1].res], [o_.res])
                    yield
                    r, m = pm // per, pm % per
                    s0 = r + d * 128 * m
                    k.dma(SP, cho[oslot], oA[g][s0:s0 + d * 127 + 1:d, :], V(o_[:], "p h d -> p (h d)"), [o_.res],
                          [k.dres("oA", g, tt) for tt in range((s0 // 128), min(NT, (s0 + d * 127) // 128 + 1))])

                def units_A(g):
                    d = DIL[g]
                    seg = NT // (2 * d)
                    loads_A(g, 0)
                    for sbk in range(NT // 4):
                        for qi in range(4):
                            if qi == 2 and sbk + 1 < NT // 4:
                                loads_A(g, sbk + 1)
                            pm = sbk * 4 + qi
                            sidx, sp_ = pm // seg, pm % seg
                            lo_kind = "N" if sp_ != 0 else ("S" if sidx % 2 == 1 else "E")
                            hi_kind = "N" if sp_ != seg - 1 else ("S" if sidx % 2 == 0 else "E")
                            midx = M4IDX[(lo_kind, hi_kind)]
                            oslot = octr[0] % 2
                            octr[0] += 1
                            for hp in range(4):
                                yield unit_A(g, sbk, qi, hp, midx, oslot)

                def unit_B(sbk, qi, hq, midx, oslot):
                    sl = uctr[0] % NSL
                    uctr[0] += 1
                    q_, k_, v_ = qs[sbk % 2], ks[sbk % 2], vs[sbk % 2]
                    pS, p_t, p_m = psS[sl], pt[sl], pmk[sl]
                    pO = (psO[oslot * 2], psO[oslot * 2 + 1])
                    kvh = hq // 4
                    for u in range(3):
                        k.op(PE, lambda u=u: nc.tensor.matmul(
                            pS[:, u * 128:(u + 1) * 128], lhsT=k_[:, kvh, (qi + u) * 128:(qi + u + 1) * 128],
                            rhs=q_[:, hq, qi * 128:(qi + 1) * 128], start=True, stop=True),
                            [k_.res, q_.res], [pS.res], inc=(u == 2))
                    yield
                    k.op(ACT, lambda: nc.scalar.activation(out=p_t[:, 0:384], in_=pS[:, 0:384], func=AF.Exp, scale=0.125), [pS.res], [p_t.res])
                    yield
                    meng = DVE if sl % 2 == 0 else POOL
                    k.op(meng, lambda: meng.e.tensor_tensor(out=p_m[:, 0:384], in0=p_t[:, 0:384], in1=m3[:, midx, :], op=ALU.mult), [p_t.res, m3.res], [p_m.res])
                    yield
                    po = pO[hq // 4]
                    for u in range(3):
                        k.op(PE, lambda u=u: nc.tensor.matmul(
                            po[:, (hq % 4) * HS:(hq % 4) * HS + 65], lhsT=p_m[:, u * 128:(u + 1) * 128],
                            rhs=v_[:, qi + u, kvh * HS:kvh * HS + 65], start=(u == 0), stop=(u == 2)),
                            [p_m.res, v_.res], [po.res], inc=(u == 2))
                    if hq == 7:
                        yield
                        pm = sbk * 4 + qi
                        o_ = osb[oslot]
                        ob_ = obb[oslot]
                        k.op(ACT, lambda: nc.scalar.copy(out=o_[:, 0:4, :], in_=V(pO[0][:, 0:4 * HS], "p (h d) -> p h d", h=4)[:, :, 0:65]), [pO[0].res], [o_.res])
                        k.op(DVE, lambda: nc.vector.tensor_copy(out=o_[:, 4:8, :], in_=V(pO[1][:, 0:4 * HS], "p (h d) -> p h d", h=4)[:, :, 0:65]), [pO[1].res], [o_.res])
                        yield
                        dn = dens[oslot]
                        k.op(DVE, lambda: nc.vector.tensor_add(out=dn[:], in0=o_[:, :, 64], in1=esink[:]), [o_.res, esink.res], [dn.res])
                        k.op(DVE, lambda: nc.vector.reciprocal(out=dn[:], in_=dn[:]), [dn.res], [dn.res])
                        k.op(DVE, lambda: nc.vector.tensor_tensor(out=V(ob_[:], "p (h d) -> p h d", h=8), in0=o_[:, :, 0:64],
                                                                  in1=dn[:].unsqueeze(2).broadcast_to([128, 8, 64]), op=ALU.mult),
                             [o_.res, dn.res], [ob_.res])
                        yield
                        k.dma(SP, cho[oslot], obd[pm * 128:(pm + 1) * 128, :], ob_[:], [ob_.res], [k.dres("obd", pm)])

                def units_B():
                    loads_B(0)
                    for sbk in range(NT // 4):
                        for qi in range(4):
                            if qi == 2 and sbk + 1 < NT // 4:
                                loads_B(sbk + 1)
                            pm = sbk * 4 + qi
                            if pm == 0:
                                mk_ = ("Z", "N")
                            elif pm == NT - 1:
                                mk_ = ("N", "Z")
                            elif pm == NT // 2:
                                mk_ = ("L", "N")
                            elif pm == NT // 2 - 1:
                                mk_ = ("N", "L")
                            else:
                                mk_ = ("N", "N")
                            oslot = octr[0] % 2
                            octr[0] += 1
                            for hq in range(8):
                                yield unit_B(sbk, qi, hq, M3IDX[mk_], oslot)

                dens = [sb(es, f"A_den{i}", [128, 8]) for i in range(2)]
                for g in range(3):
                    rolling(units_A(g), NSL)
                rolling(units_B(), NSL)
                k.barrier()
            if dbg == "AB":
                break

            with contextlib.ExitStack() as es:
                NR = T // 2048
                cw = sb(es, "C1_cw", [128, 12, 5])
                chg = k.chan("gen")
                for j in range(5):
                    k.dma(SP, chg, cw[:, :, j], V(conv_w[l][j, :], "(b p) -> p b", p=128), [], [cw.res], slow=True)
                KC1 = 2
                xin = [sb(es, f"C1_x{i}", [128, 2052]) for i in range(KC1)]
                chx = [k.chan(f"C1_x{i}") for i in range(KC1)]
                yvs = [sb(es, f"C1_y{i}", [128, 2048]) for i in range(KC1)]
                ysls = [sb(es, f"C1_ys{i}", [128, 2048]) for i in range(KC1)]
                sqbs = [sb(es, f"C1_sq{i}", [128, 2048], BF16) for i in range(KC1)]
                ynb = [sb(es, f"C1_yn{i}", [128, 2048], BF16) for i in range(KC1)]
                chn = [k.chan(f"C1_n{i}") for i in range(KC1)]
                psn = [[psb(es, f"C1_pn{j}{i}", [128, 512]) for i in range(2)] for j in range(KC1)]
                rvs = [[sb(es, f"C1_rv{j}{i}", [128, 512]) for i in range(2)] for j in range(KC1)]
                pst = [[psb(es, f"C1_pt{j}{i}", [128, 1024], BF16) for i in range(2)] for j in range(KC1)]
                tks = [sb(es, f"C1_tk{i}", [128, 16, 128], BF16) for i in range(KC1)]
                cht = [k.chan(f"C1_t{i}") for i in range(KC1)]

                def gen_c1(j, cb, rg):
                    kind = cb // 4
                    t0 = rg * 2048
                    xi, yv, ysl, sqb, yn, tk = xin[j], yvs[j], ysls[j], sqbs[j], ynb[j], tks[j]
                    k.dma(POOL, chx[j], xi[:], cx[cb * 128:(cb + 1) * 128, t0:t0 + 2052], [k.dres("cx", cb, rg)] +
                          ([k.dres("cx", cb, rg - 1)] if rg > 0 else []) + ([k.dres("cx", cb, rg + 1)] if rg < NR - 1 else []), [xi.res])
                    yield
                    if t0 == T // 2:
                        k.op(DVE, lambda: nc.vector.tensor_scalar(out=xi[:, 0:2], in0=xi[:, 0:2], scalar1=linkc[:, 0:1], scalar2=None, op0=ALU.mult),
                             [xi.res, linkc.res], [xi.res])
                    if t0 + 2048 == T // 2:
                        k.op(DVE, lambda: nc.vector.tensor_scalar(out=xi[:, 2050:2052], in0=xi[:, 2050:2052], scalar1=linkc[:, 0:1], scalar2=None, op0=ALU.mult),
                             [xi.res, linkc.res], [xi.res])
                    k.op(DVE, lambda: nc.vector.tensor_scalar(out=yv[:], in0=xi[:, 0:2048], scalar1=cw[:, cb, 0:1], scalar2=None, op0=ALU.mult),
                         [xi.res, cw.res], [yv.res])
                    for jj in range(1, 5):
                        k.op(DVE, lambda jj=jj: nc.vector.scalar_tensor_tensor(out=yv[:], in0=xi[:, jj:jj + 2048], scalar=cw[:, cb, jj:jj + 1], in1=yv[:],
                                                                              op0=ALU.mult, op1=ALU.add), [xi.res, cw.res, yv.res], [yv.res])
                    yield
                    k.op(ACT, lambda: nc.scalar.activation(out=ysl[:], in_=yv[:], func=AF.Silu), [yv.res], [ysl.res])
                    if kind < 2:
                        k.op(ACT, lambda: nc.scalar.activation(out=sqb[:], in_=ysl[:], func=AF.Square), [ysl.res], [sqb.res])
                        yield
                        for s4 in range(4):
                            pn = psn[j][s4 % 2]; r_ = rvs[j][s4 % 2]
                            k.op(PE, lambda pn=pn, s4=s4: nc.tensor.matmul(pn[:], lhsT=cbf[:, BONES, :], rhs=sqb[:, s4 * 512:(s4 + 1) * 512], start=True, stop=True),
                                 [cbf.res, sqb.res], [pn.res])
                            yield
                            k.op(DVE, lambda pn=pn, r_=r_: nc.vector.tensor_scalar(out=r_[:], in0=pn[:], scalar1=EPS, scalar2=None, op0=ALU.add), [pn.res], [r_.res])
                            yield
                            rinv(None, r_[:], r_.res)
                            yield
                            k.op(DVE, lambda r_=r_, s4=s4: nc.vector.scalar_tensor_tensor(
                                out=yn[:, s4 * 512:(s4 + 1) * 512], in0=ysl[:, s4 * 512:(s4 + 1) * 512], scalar=(0.125 if kind == 0 else 1.0), in1=r_[:],
                                op0=ALU.mult, op1=ALU.mult), [ysl.res, r_.res], [yn.res])
                        yield
                        dstT = qTc if kind == 0 else kTc
                        for e in range(2):
                            k.dma(SP, chn[j], dstT[2 * (cb % 4) + e][:, t0:t0 + 2048], yn[e * 64:(e + 1) * 64, :], [yn.res], [k.dres("qkTc", kind, cb % 4, rg)])
                    else:
                        yield
                        k.op(ACT, lambda: nc.scalar.copy(out=yn[:], in_=ysl[:]), [ysl.res], [yn.res])
                        yield
                    if kind >= 1:
                        for hf in range(2):
                            for c in range(8):
                                k.op(PE, lambda c=c, hf=hf: nc.tensor.transpose(out=pst[j][hf][:, c * 128:(c + 1) * 128], in_=yn[:, (hf * 8 + c) * 128:(hf * 8 + c + 1) * 128],
                                                                              identity=ident), [yn.res, cbf.res], [pst[j][hf].res], inc=(c == 7))
                        yield
                        k.op(DVE, lambda: nc.vector.tensor_copy(out=tk[:, 0:8, :], in_=V(pst[j][0][:, :], "p (c n) -> p c n", c=8)), [pst[j][0].res], [tk.res])
                        k.op(ACT, lambda: nc.scalar.copy(out=tk[:, 8:16, :], in_=V(pst[j][1][:, :], "p (c n) -> p c n", c=8)), [pst[j][1].res], [tk.res])
                        yield
                        dsttok = ktok if kind == 1 else vtok
                        k.dma(SP, cht[j], V(dsttok[t0:t0 + 2048, (cb % 4) * 128:(cb % 4 + 1) * 128], "(t p) c -> p t c", p=128), tk[:], [tk.res],
                              [k.dres("kvtok", kind, cb % 4, rg)])

                def lockstep1(gens):
                    gens = list(gens)
                    while gens:
                        nxt = []
                        for g_ in gens:
                            try:
                                next(g_)
                                nxt.append(g_)
                            except StopIteration:
                                pass
                        gens = nxt

                work = [(cb, rg) for cb in range(12) for rg in range(NR)]
                for w0 in range(0, len(work), KC1):
                    lockstep1([gen_c1(j, *work[w0 + j]) for j in range(min(KC1, len(work) - w0))])
                k.barrier()

            if dbg == "C1":
                break

            with contextlib.ExitStack() as es:
                negm4 = sb(es, "C2_negm4", [128, 2, 4, 128])
                for d_ in range(2):
                    for q_i in range(4):
                        k.op(DVE, lambda d_=d_, q_i=q_i: nc.vector.tensor_copy(out=negm4[:, d_, q_i, :], in_=cst[:, NEGF + d_, :]), [cst.res], [negm4.res])

                def lockstep2(gens):
                    gens = list(gens)
                    while gens:
                        nxt = []
                        for g_ in gens:
                            try:
                                next(g_)
                                nxt.append(g_)
                            except StopIteration:
                                pass
                        gens = nxt

                def gen_dir(dr):
                    D_ = f"d{dr}"
                    kTt = [sb(es, f"C2_kT{D_}{i}", [64, 8, 128], BF16) for i in range(2)]
                    qTt = [sb(es, f"C2_qT{D_}{i}", [64, 8, 128], BF16) for i in range(2)]
                    ktk = [sb(es, f"C2_kt{D_}{i}", [128, 8, 64], BF16) for i in range(2)]
                    vtk = [sb(es, f"C2_vt{D_}{i}", [128, 8, 64], BF16) for i in range(2)]
                    bgt = [sb(es, f"C2_bg{D_}{i}", [128, 32]) for i in range(2)]
                    chl = [[k.chan(f"C2_l{D_}{j}{i}") for i in range(2)] for j in range(5)]
                    B0, B1, B2 = [psb(es, f"C2_b{D_}{i}", [128, 512]) for i in range(3)]
                    psT16 = psb(es, f"C2_pT16{D_}", [128, 1024], BF16)
                    gcm = sb(es, f"C2_gcm{D_}", [128, 4, 128])
                    gsum = sb(es, f"C2_gsum{D_}", [128, 16])
                    sc = sb(es, f"C2_sc{D_}", [128, 6, 8])
                    GL = sb(es, f"C2_GL{D_}", [64, 2, 8])
                    tmp = sb(es, f"C2_tmp{D_}", [128, 8, 128]); DT = sb(es, f"C2_DT{D_}", [128, 8, 128]); DTs = sb(es, f"C2_DTs{D_}", [128, 8, 128])
                    UK = sb(es, f"C2_UK{D_}", [128, 8, 128])
                    QKm = sb(es, f"C2_QKm{D_}", [128, 8, 128], BF16)
                    UR = [sb(es, f"C2_UR{D_}{i}", [128, 8, 2, 128], BF16) for i in range(2)]
                    Lm = [sb(es, f"C2_L{D_}{i}", [128, 8, 128], BF16) for i in range(2)]
                    XT = sb(es, f"C2_XT{D_}", [128, 8, 128], BF16)
                    XVb = sb(es, f"C2_XVb{D_}", [128, 8, 64])
                    kE = sb(es, f"C2_kE{D_}", [128, 8, 64], BF16); kdm = sb(es, f"C2_kd{D_}", [128, 8, 64], BF16)
                    wT = sb(es, f"C2_wT{D_}", [64, 8, 128], BF16)
                    S = sb(es, f"C2_S{D_}", [64, 8, 64]); Sb = sb(es, f"C2_Sb{D_}", [64, 8, 64], BF16)
                    vt_ = sb(es, f"C2_vtt{D_}", [128, 8, 64]); vnew = sb(es, f"C2_vn{D_}", [128, 8, 64], BF16)
                    ot_ = sb(es, f"C2_ot{D_}", [128, 8, 64])
                    osbs = [sb(es, f"C2_o{D_}{i}", [128, 8, 64]) for i in range(2)]
                    cho = [k.chan(f"C2_o{D_}{i}") for i in range(2)]
                    CMx = CMi if dr == 0 else CMTi
                    SMx = SMF if dr == 0 else SMB
                    odst = ofd if dr == 0 else obw
                    k.op(DVE, lambda: nc.vector.memset(S[:], 0.0), [], [S.res])
                    k.op(DVE, lambda: nc.vector.memset(Sb[:], 0.0), [], [Sb.res])
                    order = list(range(NT)) if dr == 0 else list(range(NT - 1, -1, -1))
                    for it, n in enumerate(order):
                        b2 = it % 2
                        kT_, qT_, kt_, vt_k, bg_ = kTt[b2], qTt[b2], ktk[b2], vtk[b2], bgt[b2]
                        cs_ = slice(n * 128, (n + 1) * 128)

                        def loads(it_l):
                            n_l = order[it_l]
                            bl = it_l % 2
                            csl = slice(n_l * 128, (n_l + 1) * 128)
                            rgl = n_l // 16
                            k.dma(SP, chl[0][bl], kTt[bl][:], V(kTc[:, :, csl], "q p n -> p q n"), [k.dres("qkTc", 1, j, rgl) for j in range(4)], [kTt[bl].res])
                            k.dma(SP, chl[1][bl], qTt[bl][:], V(qTc[:, :, csl], "q p n -> p q n"), [k.dres("qkTc", 0, j, rgl) for j in range(4)], [qTt[bl].res])
                            k.dma(SP, chl[2][bl], V(ktk[bl][:], "p h d -> p (h d)"), ktok[csl, :], [k.dres("kvtok", 1, j, rgl) for j in range(4)], [ktk[bl].res])
                            k.dma(SP, chl[3][bl], V(vtk[bl][:], "p h d -> p (h d)"), vtok[csl, :], [k.dres("kvtok", 2, j, rgl) for j in range(4)], [vtk[bl].res])
                            k.dma(SP, chl[4][bl], bgt[bl][:], bgd[csl, :], [k.dres("bgd", n_l)], [bgt[bl].res])

                        if it == 0:
                            loads(0)
                        if it + 1 < NT:
                            loads(it + 1)
                        gcol = bg_[:, 16 + dr * 8:24 + dr * 8]
                        bcol = bg_[:, dr * 8:dr * 8 + 8]
                        pG = B2
                        k.op(PE, lambda: nc.tensor.matmul(pG[:, 0:8], lhsT=cst[:, CMi, :], rhs=gcol, start=True, stop=True), [cst.res, bg_.res], [pG.res], inc=False)
                        k.op(PE, lambda: nc.tensor.matmul(pG[:, 8:16], lhsT=cst[:, CMTi, :], rhs=gcol, start=True, stop=True), [cst.res, bg_.res], [pG.res], inc=False)
                        for c in range(2):
                            k.op(PE, lambda c=c: nc.tensor.matmul(pG[0:64, 32 + c * 8:40 + c * 8], lhsT=cst[:, BONES, c * 64:(c + 1) * 64], rhs=gcol, start=True, stop=True),
                                 [cst.res, bg_.res], [pG.res], inc=(c == 1))
                        yield
                        k.op(ACT, lambda: nc.scalar.copy(out=gsum[:], in_=pG[:, 0:16]), [pG.res], [gsum.res])
                        k.op(ACT, lambda: nc.scalar.activation(out=V(GL[:], "p c h -> p (c h)"), in_=pG[0:64, 32:48], func=AF.Exp), [pG.res], [GL.res])
                        yield
                        own = gsum[:, 0:8] if dr == 0 else gsum[:, 8:16]
                        oth = gsum[:, 8:16] if dr == 0 else gsum[:, 0:8]
                        k.op(DVE, lambda: nc.vector.tensor_copy(out=sc[:, 0, :], in_=own), [gsum.res], [sc.res])
                        k.op(ACT, lambda: nc.scalar.activation(out=sc[:, 1, :], in_=own, func=AF.Exp), [gsum.res], [sc.res])
                        k.op(DVE, lambda: nc.vector.tensor_sub(out=sc[:, 4, :], in0=oth, in1=gcol), [gsum.res, bg_.res], [sc.res])
                        k.op(DVE, lambda: nc.vector.tensor_scalar(out=sc[:, 3, :], in0=bcol, scalar1=-1.0, scalar2=None, op0=ALU.mult), [bg_.res], [sc.res])
                        yield
                        k.op(ACT, lambda: nc.scalar.activation(out=sc[:, 2, :], in_=sc[:, 4, :], func=AF.Exp), [sc.res], [sc.res])
                        k.op(POOL, lambda: nc.gpsimd.tensor_tensor(out=kE[:], in0=kt_[:], in1=sc[:, 1, :].unsqueeze(2).broadcast_to([128, 8, 64]), op=ALU.mult),
                             [kt_.res, sc.res], [kE.res])
                        yield
                        k.op(POOL, lambda: nc.gpsimd.tensor_tensor(out=kdm[:], in0=kt_[:], in1=sc[:, 2, :].unsqueeze(2).broadcast_to([128, 8, 64]), op=ALU.mult),
                             [kt_.res, sc.res], [kdm.res])
                        U0 = UR[0]
                        for hf in range(2):
                            hs = slice(hf * 4, (hf + 1) * 4)
                            k.op(POOL, lambda hs=hs: nc.gpsimd.tensor_tensor(out=gcm[:], in0=cst[:, CMx, :].unsqueeze(1).broadcast_to([128, 4, 128]),
                                                                             in1=gcol[:, hs].unsqueeze(2).broadcast_to([128, 4, 128]), op=ALU.mult),
                                 [cst.res, bg_.res], [gcm.res])
                            yield
                            k.op(PE, lambda: nc.tensor.matmul(B2[:], lhsT=cst[:, ONES, :], rhs=V(gcm[:], "p h n -> p (h n)"), start=True, stop=True),
                                 [cst.res, gcm.res], [B2.res])
                            for hl in range(4):
                                h = hf * 4 + hl
                                k.op(PE, lambda h=h, hl=hl: nc.tensor.matmul(B0[:, hl * 128:(hl + 1) * 128], lhsT=kT_[:, h, :], rhs=kT_[:, h, :], start=True, stop=True),
                                     [kT_.res], [B0.res], inc=(hl == 3))
                            for hl in range(4):
                                h = hf * 4 + hl
                                k.op(PE, lambda h=h, hl=hl: nc.tensor.matmul(B1[:, hl * 128:(hl + 1) * 128], lhsT=kT_[:, h, :], rhs=qT_[:, h, :], start=True, stop=True),
                                     [kT_.res, qT_.res], [B1.res], inc=(hl == 3))
                            yield
                            k.op(DVE, lambda hs=hs: nc.vector.tensor_tensor(out=V(tmp[:, hs, :], "p h n -> p (h n)"), in0=B2[:],
                                                                            in1=V(negm4[:, dr, :, :], "p h n -> p (h n)"), op=ALU.add), [B2.res, negm4.res], [tmp.res])
                            k.op(POOL, lambda hs=hs: nc.gpsimd.tensor_tensor(out=tmp[:, hs, :], in0=tmp[:, hs, :], in1=sc[:, 0, hs].unsqueeze(2).broadcast_to([128, 4, 128]),
                                                                             op=ALU.subtract), [tmp.res, sc.res], [tmp.res])
                            yield
                            k.op(ACT, lambda hs=hs: nc.scalar.activation(out=DT[:, hs, :], in_=tmp[:, hs, :], func=AF.Exp), [tmp.res], [DT.res])
                            yield
                            k.op(POOL, lambda hs=hs: nc.gpsimd.tensor_tensor(out=DTs[:, hs, :], in0=DT[:, hs, :], in1=cst[:, SMx, :].unsqueeze(1).broadcast_to([128, 4, 128]),
                                                                             op=ALU.mult), [DT.res, cst.res], [DTs.res])
                            k.op(DVE, lambda hs=hs: nc.vector.tensor_tensor(out=QKm[:, hs, :], in0=V(B1[:], "p (h n) -> p h n", h=4), in1=DT[:, hs, :], op=ALU.mult),
                                 [B1.res, DT.res], [QKm.res])
                            yield
                            k.op(DVE, lambda hs=hs: nc.vector.tensor_tensor(out=UK[:, hs, :], in0=V(B0[:], "p (h n) -> p h n", h=4), in1=DTs[:, hs, :], op=ALU.mult),
                                 [B0.res, DTs.res], [UK.res])
                            yield
                            k.op(POOL, lambda hs=hs: nc.gpsimd.tensor_tensor(out=U0[:, hs, 0, :], in0=UK[:, hs, :], in1=bcol[:, hs].unsqueeze(2).broadcast_to([128, 4, 128]),
                                                                             op=ALU.mult), [UK.res, bg_.res], [U0.res])
                            yield
                            k.op(POOL, lambda hs=hs: nc.gpsimd.tensor_tensor(out=U0[:, hs, 1, :], in0=cbf[:, IDENT, :].unsqueeze(1).broadcast_to([128, 4, 128]),
                                                                             in1=U0[:, hs, 0, :], op=ALU.subtract), [U0.res, cbf.res], [U0.res])
                        yield
                        L0 = Lm[0]
                        for h in range(8):
                            k.op(PE, lambda h=h: nc.tensor.transpose(out=psT16[:, h * 128:(h + 1) * 128], in_=U0[:, h, 0, :], identity=ident),
                                 [U0.res, cbf.res], [psT16.res], inc=(h == 7))
                        yield
                        k.op(ACT, lambda: nc.scalar.copy(out=L0[:], in_=V(psT16[:, :], "p (h n) -> p h n", h=8)), [psT16.res], [L0.res])
                        yield
                        for hb in range(2):
                            hs = slice(hb * 4, (hb + 1) * 4)
                            pUR = (B0, B1); pL = B2
                            cur = 0
                            for lvl in range(6):
                                Uc, Lc = UR[cur], Lm[cur]
                                Un, Ln_ = UR[1 - cur], Lm[1 - cur]
                                for hl in range(4):
                                    h = hb * 4 + hl
                                    pu = pUR[hl // 2]
                                    off = (hl % 2) * 256
                                    if lvl == 0:
                                        k.op(PE, lambda h=h, pu=pu, off=off: nc.tensor.matmul(pu[:, off:off + 128], lhsT=Lc[:, h, :], rhs=Uc[:, h, 0, :], start=True, stop=True),
                                             [Uc.res, Lc.res], [pu.res], inc=False)
                                    elif lvl < 5:
                                        k.op(PE, lambda h=h, pu=pu, off=off: nc.tensor.matmul(pu[:, off:off + 256], lhsT=Lc[:, h, :],
                                                                                                rhs=V(Uc[:, h, :, :], "p a n -> p (a n)"), start=True, stop=True),
                                             [Uc.res, Lc.res], [pu.res], inc=False)
                                    else:
                                        k.op(PE, lambda h=h, pu=pu, off=off: nc.tensor.matmul(pu[:, off + 128:off + 256], lhsT=Lc[:, h, :], rhs=Uc[:, h, 1, :], start=True, stop=True),
                                             [Uc.res, Lc.res], [pu.res], inc=(hl == 3))
                                    if lvl < 5:
                                        k.op(PE, lambda h=h, hl=hl: nc.tensor.matmul(pL[:, hl * 128:(hl + 1) * 128], lhsT=Uc[:, h, 0, :], rhs=Lc[:, h, :], start=True, stop=True),
                                             [Uc.res, Lc.res], [pL.res], inc=(hl == 3))
                                yield
                                if lvl < 5:
                                    for hh in range(2):
                                        hs2 = slice(hb * 4 + hh * 2, hb * 4 + hh * 2 + 2)
                                        pv = V(pUR[hh][:], "p (h a n) -> p h a n", h=2, a=2)
                                        k.op(ACT, lambda hs2=hs2, pv=pv: nc.scalar.copy(out=Un[:, hs2, 0, :], in_=pv[:, :, 0, :]), [pUR[hh].res], [Un.res])
                                        if lvl == 0:
                                            k.op(POOL, lambda hs2=hs2: nc.gpsimd.tensor_copy(out=Un[:, hs2, 1, :], in_=Uc[:, hs2, 1, :]), [Uc.res], [Un.res])
                                        else:
                                            k.op(DVE, lambda hs2=hs2, pv=pv: nc.vector.tensor_tensor(out=Un[:, hs2, 1, :], in0=pv[:, :, 1, :], in1=Uc[:, hs2, 1, :], op=ALU.add),
                                                 [pUR[hh].res, Uc.res], [Un.res])
                                    k.op(ACT, lambda: nc.scalar.copy(out=Ln_[:, hs, :], in_=V(pL[:], "p (h n) -> p h n", h=4)), [pL.res], [Ln_.res])
                                    cur = 1 - cur
                                else:
                                    for hh in range(2):
                                        hs2 = slice(hb * 4 + hh * 2, hb * 4 + hh * 2 + 2)
                                        pv = V(pUR[hh][:], "p (h a n) -> p h a n", h=2, a=2)
                                        k.op(DVE, lambda hs2=hs2, pv=pv: nc.vector.tensor_tensor(out=XT[:, hs2, :], in0=pv[:, :, 1, :], in1=Uc[:, hs2, 1, :], op=ALU.add),
                                             [pUR[hh].res, Uc.res], [XT.res])
                                yield
                        pX = B0
                        for h in range(8):
                            k.op(PE, lambda h=h: nc.tensor.matmul(pX[:, h * 64:(h + 1) * 64], lhsT=XT[:, h, :], rhs=vt_k[:, h, :], start=True, stop=True),
                                 [XT.res, vt_k.res], [pX.res], inc=(h == 7))
                        pW = (B1, B2)
                        for h in range(8):
                            k.op(PE, lambda h=h: nc.tensor.matmul(pW[h // 4][0:64, (h % 4) * 128:(h % 4 + 1) * 128], lhsT=kE[:, h, :], rhs=XT[:, h, :], start=True, stop=True),
                                 [kE.res, XT.res], [pW[h // 4].res], inc=(h % 4 == 3))
                        yield
                        k.op(DVE, lambda: nc.vector.tensor_tensor(out=XVb[:], in0=V(pX[:], "p (h d) -> p h d", h=8), in1=bcol.unsqueeze(2).broadcast_to([128, 8, 64]), op=ALU.mult),
                             [pX.res, bg_.res], [XVb.res])
                        for hf in range(2):
                            k.op(ACT, lambda hf=hf: nc.scalar.copy(out=wT[:, hf * 4:(hf + 1) * 4, :], in_=V(pW[hf][0:64, :], "p (q n) -> p q n", q=4)), [pW[hf].res], [wT.res])
                        yield
                        if (dr == 0 and n == NT // 2) or (dr == 1 and n == NT // 2 - 1):
                            k.op(DVE, lambda: nc.vector.tensor_scalar(out=S[:], in0=S[:], scalar1=linkc[0:64, 0:1], scalar2=None, op0=ALU.mult), [S.res, linkc.res], [S.res])
                            k.op(ACT, lambda: nc.scalar.copy(out=Sb[:], in_=S[:]), [S.res], [Sb.res])
                            yield
                        o_ = osbs[b2]
                        pP, pI, pA, pD = B0, B1, B2, B0
                        for c in ((0, 1) if dr == 0 else (1, 0)):
                            cs = slice(c * 64, (c + 1) * 64)
                            for h in range(8):
                                k.op(PE, lambda h=h: nc.tensor.matmul(pP[cs, h * 64:(h + 1) * 64], lhsT=wT[:, h, cs], rhs=Sb[:, h, :], start=True, stop=True),
                                     [wT.res, Sb.res], [pP.res], inc=(h == 7))
                            for h in range(8):
                                k.op(PE, lambda h=h: nc.tensor.matmul(pI[cs, h * 64:(h + 1) * 64], lhsT=qT_[:, h, cs], rhs=Sb[:, h, :], start=True, stop=True),
                                     [qT_.res, Sb.res], [pI.res], inc=(h == 7))
                            yield
                            k.op(DVE, lambda: nc.vector.tensor_tensor(out=vt_[cs, :, :], in0=V(pP[cs, :], "p (h d) -> p h d", h=8),
                                                                      in1=sc[cs, 3, :].unsqueeze(2).broadcast_to([64, 8, 64]), op=ALU.mult), [pP.res, sc.res], [vt_.res])
                            k.op(DVE, lambda: nc.vector.tensor_add(out=vnew[cs, :, :], in0=vt_[cs, :, :], in1=XVb[cs, :, :]), [vt_.res, XVb.res], [vnew.res])
                            k.op(DVE, lambda: nc.vector.tensor_tensor(out=ot_[cs, :, :], in0=V(pI[cs, :], "p (h d) -> p h d", h=8),
                                                                      in1=sc[cs, 1, :].unsqueeze(2).broadcast_to([64, 8, 64]), op=ALU.mult), [pI.res, sc.res], [ot_.res])
                            k.op(POOL, lambda c=c: nc.gpsimd.tensor_tensor(out=S[:], in0=S[:], in1=GL[:, c, :].unsqueeze(2).broadcast_to([64, 8, 64]), op=ALU.mult),
                                 [S.res, GL.res], [S.res])
                            yield
                            for h in range(8):
                                k.op(PE, lambda h=h: nc.tensor.matmul(pD[0:64, h * 64:(h + 1) * 64], lhsT=kdm[cs, h, :], rhs=vnew[cs, h, :], start=True, stop=True),
                                     [kdm.res, vnew.res], [pD.res], inc=(h == 7))
                            for h in range(8):
                                k.op(PE, lambda h=h: nc.tensor.matmul(pA[cs, h * 64:(h + 1) * 64], lhsT=QKm[cs, h, cs], rhs=vnew[cs, h, :], start=True, stop=True),
                                     [QKm.res, vnew.res], [pA.res], inc=(h == 7))
                            yield
                            k.op(DVE, lambda: nc.vector.tensor_tensor(out=Sb[:], in0=S[:], in1=V(pD[0:64, :], "p (q d) -> p q d", q=8), op=ALU.add), [S.res, pD.res], [Sb.res])
                            k.op(DVE, lambda: nc.vector.tensor_tensor(out=S[:], in0=S[:], in1=V(pD[0:64, :], "p (q d) -> p q d", q=8), op=ALU.add), [S.res, pD.res], [S.res])
                            k.op(DVE, lambda: nc.vector.tensor_tensor(out=o_[cs, :, :], in0=V(pA[cs, :], "p (h d) -> p h d", h=8), in1=ot_[cs, :, :], op=ALU.add),
                                 [pA.res, ot_.res], [o_.res])
                            yield
                        k.dma(SP, cho[b2], odst[cs_, :], V(o_[:], "p h d -> p (h d)"), [o_.res], [k.dres("ofd" if dr == 0 else "obw", n)])

                g0_, g1_ = gen_dir(0), gen_dir(1)
                for _ in range(36):
                    next(g0_)
                lockstep2([g0_, g1_])
                k.barrier()

            if dbg == "C2":
                break

            with contextlib.ExitStack() as es:
                rstd = rstdP
                junk = sb(es, "M1_junk", [128, D])
                chg = k.chan("gen")
                gain1 = load_bc(es, "M1_gain", norm1[l:l + 1, :], D, chg)
                wg = sb(es, "M1_wg", [128, 8, 3 * D], BF16)
                wbr = [sb(es, f"M1_wbr{i}", [128, 4, D], BF16) for i in range(3)]
                wo = sb(es, "M1_wo", [128, 8, D], BF16)
                chw = k.chan("M1_w")
                for c in range(8):
                    k.dma(POOL, chw, wg[:, c, :], w_gate[l][c * 128:(c + 1) * 128, :], [], [wg.res])
                for i in range(3):
                    k.dma(POOL, chw, wbr[i][:], V(w_br[i][l], "(c p) n -> p c n", p=128), [], [wbr[i].res])
                k.dma(POOL, chw, wo[:], V(w_out[l], "(c p) n -> p c n", p=128), [], [wo.res])
                KM = 2
                xs = [sb(es, f"M1_x{i}", [128, D]) for i in range(KM)]
                chx = [k.chan(f"M1_x{i}") for i in range(KM)]
                hbs = [sb(es, f"M1_hb{i}", [128, D], BF16) for i in range(KM)]
                hTs = [sb(es, f"M1_hT{i}", [128, 8, 128], BF16) for i in range(KM)]
                oats = [[sb(es, f"M1_oa{j}{i}", [128, 8, 65]) for i in range(3)] for j in range(KM)]
                chas = [[k.chan(f"M1_a{j}{i}") for i in range(3)] for j in range(KM)]
                obts = [[sb(es, f"M1_ob{j}{i}", [128, 512], BF16) for i in range(3)] for j in range(KM)]
                chbs = [[k.chan(f"M1_b{j}{i}") for i in range(1)] for j in range(KM)]
                oTs = [[sb(es, f"M1_oT{j}{i}", [128, 4, 128], BF16) for i in range(3)] for j in range(KM)]
                dens = [sb(es, f"M1_den{i}", [128, 8]) for i in range(KM)]
                ogain = load_bc(es, "M1_ogain", o_gain[l:l + 1, :], 64, chg)
                ofl = [sb(es, f"M1_of{i}", [128, 8, 64]) for i in range(KM)]
                obl = [sb(es, f"M1_obw{i}", [128, 8, 64]) for i in range(KM)]
                zsl = [sb(es, f"M1_zs{i}", [128, 512]) for i in range(KM)]
                osq = [sb(es, f"M1_osq{i}", [128, 512]) for i in range(KM)]
                oss = [sb(es, f"M1_oss{i}", [128, 8]) for i in range(KM)]
                chc = [[k.chan(f"M1_c{j}{i}") for i in range(3)] for j in range(KM)]
                gsbs = [sb(es, f"M1_gs{i}", [128, 512]) for i in range(KM)]
                tms = [sb(es, f"M1_tm{i}", [128, 512]) for i in range(KM)]
                mgs = [sb(es, f"M1_mg{i}", [128, D]) for i in range(KM)]
                mgbs = [sb(es, f"M1_mgb{i}", [128, D], BF16) for i in range(KM)]
                mTs = [sb(es, f"M1_mT{i}", [128, 8, 128], BF16) for i in range(KM)]
                xo = [sb(es, f"M1_xo{i}", [128, D]) for i in range(KM)]
                cho = [k.chan(f"M1_o{i}") for i in range(KM)]
                pst = [psb(es, f"M1_pt{i}", [128, 1024], BF16) for i in range(KM)]
                psg = [psb(es, f"M1_pg{i}", [128, 512]) for i in range(KM)]
                psp = [psb(es, f"M1_pp{i}", [128, 512]) for i in range(KM)]
                pso = [psb(es, f"M1_po{i}", [128, 512]) for i in range(KM)]

                def gen_m1(j, t):
                    rows = slice(t * 128, (t + 1) * 128)
                    xt, hb, hTt, oat, obt, oT, den, gsb, tm, mg, mgb, mT, xo_ = (xs[j], hbs[j], hTs[j], oats[j], obts[j], oTs[j], dens[j], gsbs[j], tms[j],
                                                                                  mgs[j], mgbs[j], mTs[j], xo[j])
                    pt_, pg, pp, po = pst[j], psg[j], psp[j], pso[j]
                    k.dma(POOL, chx[j], xt[:], src_x[rows, :], [k.dres(src_x.tensor.name, t)], [xt.res])
                    for g in range(3):
                        k.dma(POOL, chas[j][g], V(oat[g][:], "p h d -> p (h d)"), oA[g][rows, :], [k.dres("oA", g, t)], [oat[g].res])
                    k.dma(POOL, chbs[j][0], obt[1][:], obd[rows, :], [k.dres("obd", t)], [obt[1].res])
                    of_, ob_, zs_, sq_, ss_ = ofl[j], obl[j], zsl[j], osq[j], oss[j]
                    k.dma(POOL, chc[j][0], V(of_[:], "p h d -> p (h d)"), ofd[rows, :], [k.dres("ofd", t)], [of_.res])
                    k.dma(POOL, chc[j][1], V(ob_[:], "p h d -> p (h d)"), obw[rows, :], [k.dres("obw", t)], [ob_.res])
                    k.dma(POOL, chc[j][2], zs_[:], zs[rows, :], [k.dres("zs", t)], [zs_.res])
                    def oc_chain():
                        k.op(DVE, lambda: nc.vector.tensor_add(out=of_[:], in0=of_[:], in1=ob_[:]), [of_.res, ob_.res], [of_.res])
                        yield
                        k.op(ACT, lambda: nc.scalar.activation(out=sq_[:], in_=V(of_[:], "p h d -> p (h d)"), func=AF.Square), [of_.res], [sq_.res])
                        yield
                        k.op(DVE, lambda: nc.vector.tensor_reduce(out=ss_[:], in_=V(sq_[:], "p (h d) -> p h d", h=8), axis=AX.X, op=ALU.add), [sq_.res], [ss_.res])
                        k.op(DVE, lambda: nc.vector.tensor_scalar(out=ss_[:], in0=ss_[:], scalar1=1.0 / 64, scalar2=EPS, op0=ALU.mult, op1=ALU.add), [ss_.res], [ss_.res])
                        yield
                        rinv(None, ss_[:], ss_.res)
                        yield
                        k.op(DVE, lambda: nc.vector.tensor_tensor(out=of_[:], in0=of_[:], in1=ss_[:].unsqueeze(2).broadcast_to([128, 8, 64]), op=ALU.mult),
                             [of_.res, ss_.res], [of_.res])
                        yield
                        k.op(POOL, lambda: nc.gpsimd.tensor_tensor(out=of_[:], in0=of_[:], in1=ogain[:].unsqueeze(1).broadcast_to([128, 8, 64]), op=ALU.mult),
                             [of_.res, ogain.res], [of_.res])
                        k.op(POOL, lambda: nc.gpsimd.tensor_tensor(out=obt[2][:], in0=V(of_[:], "p h d -> p (h d)"), in1=zs_[:], op=ALU.mult), [of_.res, zs_.res], [obt[2].res])
                        yield
                    oc = oc_chain()
                    next(oc, None)
                    yield
                    k.op(DVE, lambda: nc.vector.scalar_tensor_tensor(out=hb[:], in0=xt[:], scalar=rstd[:, t:t + 1], in1=gain1[:], op0=ALU.mult, op1=ALU.mult),
                         [xt.res, rstd.res, gain1.res], [hb.res])
                    k.op(POOL, lambda: nc.gpsimd.tensor_add(out=oat[0][:], in0=oat[0][:], in1=oat[1][:]), [oat[0].res, oat[1].res], [oat[0].res])
                    k.op(POOL, lambda: nc.gpsimd.tensor_add(out=oat[0][:], in0=oat[0][:], in1=oat[2][:]), [oat[0].res, oat[2].res], [oat[0].res])
                    next(oc, None)
                    yield
                    for c in range(8):
                        k.op(PE, lambda c=c: nc.tensor.transpose(out=pt_[:, c * 128:(c + 1) * 128], in_=hb[:, c * 128:(c + 1) * 128], identity=ident),
                             [hb.res, cbf.res], [pt_.res], inc=(c == 7))
                    k.op(DVE, lambda: nc.vector.reciprocal(out=den[:], in_=oat[0][:, :, 64]), [oat[0].res], [den.res])
                    k.op(DVE, lambda: nc.vector.tensor_tensor(out=V(obt[0][:], "p (h d) -> p h d", h=8), in0=oat[0][:, :, 0:64],
                                                              in1=den[:].unsqueeze(2).broadcast_to([128, 8, 64]), op=ALU.mult), [oat[0].res, den.res], [obt[0].res])
                    next(oc, None)
                    yield
                    k.op(ACT, lambda: nc.scalar.copy(out=hTt[:], in_=V(pt_[:, :], "p (c n) -> p c n", c=8)), [pt_.res], [hTt.res])
                    next(oc, None)
                    yield
                    for br in range(3):
                        if br == 2:
                            for _ in oc:
                                pass
                        else:
                            next(oc, None)
                        for c in range(4):
                            k.op(PE, lambda c=c, br=br: nc.tensor.transpose(out=pt_[:, c * 128:(c + 1) * 128], in_=obt[br][:, c * 128:(c + 1) * 128], identity=ident),
                                 [obt[br].res, cbf.res], [pt_.res], inc=(c == 3))
                        yield
                        if br % 2:
                            k.op(DVE, lambda br=br: nc.vector.tensor_copy(out=oT[br][:], in_=V(pt_[:, 0:512], "p (c n) -> p c n", c=4)), [pt_.res], [oT[br].res])
                        else:
                            k.op(ACT, lambda br=br: nc.scalar.copy(out=oT[br][:], in_=V(pt_[:, 0:512], "p (c n) -> p c n", c=4)), [pt_.res], [oT[br].res])
                        yield
                    for nb in range(2):
                        ns = slice(nb * 512, (nb + 1) * 512)
                        for br in range(3):
                            for c in range(8):
                                k.op(PE, lambda c=c, br=br, nb=nb: nc.tensor.matmul(pg[:], lhsT=hTt[:, c, :], rhs=wg[:, c, br * D + nb * 512:br * D + (nb + 1) * 512],
                                                                                  start=(c == 0), stop=(c == 7)), [hTt.res, wg.res], [pg.res], inc=(c == 7))
                            for c in range(4):
                                k.op(PE, lambda c=c, br=br, ns=ns: nc.tensor.matmul(pp[:], lhsT=oT[br][:, c, :], rhs=wbr[br][:, c, ns], start=(c == 0), stop=(c == 3)),
                                     [oT[br].res, wbr[br].res], [pp.res], inc=(c == 3))
                            yield
                            k.op(ACT, lambda: nc.scalar.activation(out=gsb[:], in_=pg[:], func=AF.Sigmoid), [pg.res], [gsb.res])
                            yield
                            if br == 0:
                                k.op(DVE, lambda ns=ns: nc.vector.tensor_tensor(out=mg[:, ns], in0=pp[:], in1=gsb[:], op=ALU.mult), [pp.res, gsb.res], [mg.res])
                            else:
                                k.op(DVE, lambda: nc.vector.tensor_tensor(out=tm[:], in0=pp[:], in1=gsb[:], op=ALU.mult), [pp.res, gsb.res], [tm.res])
                                k.op(POOL, lambda ns=ns: nc.gpsimd.tensor_add(out=mg[:, ns], in0=mg[:, ns], in1=tm[:]), [mg.res, tm.res], [mg.res])
                            yield
                    k.op(ACT, lambda: nc.scalar.copy(out=mgb[:], in_=mg[:]), [mg.res], [mgb.res])
                    yield
                    for c in range(8):
                        k.op(PE, lambda c=c: nc.tensor.transpose(out=pt_[:, c * 128:(c + 1) * 128], in_=mgb[:, c * 128:(c + 1) * 128], identity=ident),
                             [mgb.res, cbf.res], [pt_.res], inc=(c == 7))
                    yield
                    k.op(ACT, lambda: nc.scalar.copy(out=mT[:], in_=V(pt_[:, :], "p (c n) -> p c n", c=8)), [pt_.res], [mT.res])
                    yield
                    for nb in range(2):
                        ns = slice(nb * 512, (nb + 1) * 512)
                        for c in range(8):
                            k.op(PE, lambda c=c, ns=ns: nc.tensor.matmul(po[:], lhsT=mT[:, c, :], rhs=wo[:, c, ns], start=(c == 0), stop=(c == 7)),
                                 [mT.res, wo.res], [po.res], inc=(c == 7))
                        yield
                        k.op(DVE, lambda ns=ns: nc.vector.tensor_tensor(out=xo_[:, ns], in0=po[:], in1=xt[:, ns], op=ALU.add), [po.res, xt.res], [xo_.res])
                        yield
                    k.op(ACT, lambda: nc.scalar.activation(out=junk[:], in_=xo_[:], func=AF.Square, accum_out=rstd2[:, t:t + 1]),
                         [xo_.res], [junk.res, rstd2.res])
                    k.dma(SP, cho[j], x1d[rows, :], xo_[:], [xo_.res], [k.dres("x1d", t)])

                def lockstep3(gens):
                    gens = list(gens)
                    while gens:
                        nxt = []
                        for g_ in gens:
                            try:
                                next(g_)
                                nxt.append(g_)
                            except StopIteration:
                                pass
                        gens = nxt

                for t0 in range(0, NT, KM):
                    lockstep3([gen_m1(j, t0 + j) for j in range(KM)])
                k.op(DVE, lambda: nc.vector.tensor_scalar(out=rstd2[:], in0=rstd2[:], scalar1=1.0 / D, scalar2=EPS, op0=ALU.mult, op1=ALU.add), [rstd2.res], [rstd2.res])
                rinv(None, rstd2[:], rstd2.res)
                k.barrier()

            if dbg == "M1":
                break

            with contextlib.ExitStack() as es:
                rstd = rstd2
                chg = k.chan("gen")
                gain2 = load_bc(es, "M2_gain", norm2[l:l + 1, :], D, chg)
                wf1 = sb(es, "M2_w1", [128, 8, 2 * FH], BF16)
                wf2 = sb(es, "M2_w2", [128, 22, D], BF16)
                chw = k.chan("M2_w")
                for c in range(8):
                    for hh in range(2):
                        k.dma(POOL, chw, wf1[:, c, hh * FH:(hh + 1) * FH], w_f1[l][c * 128:(c + 1) * 128, hh * FH:(hh + 1) * FH], [], [wf1.res])
                for c0 in range(0, 22, 4):
                    c1 = min(22, c0 + 4)
                    k.dma(POOL, chw, wf2[:, c0:c1, :], V(w_f2[l][c0 * 128:c1 * 128, :], "(c p) n -> p c n", p=128), [], [wf2.res])
                K2 = 2
                junk2 = sb(es, "M2_junk", [128, D])
                xs = [sb(es, f"M2_x{i}", [128, D]) for i in range(K2)]
                chx = [k.chan(f"M2_x{i}") for i in range(K2)]
                hbs = [sb(es, f"M2_hb{i}", [128, D], BF16) for i in range(K2)]
                hTs = [sb(es, f"M2_hT{i}", [128, 8, 128], BF16) for i in range(K2)]
                sgs = [sb(es, f"M2_sg{i}", [128, 352]) for i in range(K2)]
                actbs = [sb(es, f"M2_act{i}", [128, FH], BF16) for i in range(K2)]
                actTs = [sb(es, f"M2_actT{i}", [128, 22, 128], BF16) for i in range(K2)]
                xo = [sb(es, f"M2_xo{i}", [128, D]) for i in range(K2)]
                cho = [k.chan(f"M2_o{i}") for i in range(K2)]
                pst = [psb(es, f"M2_pt{i}", [128, 1024], BF16) for i in range(K2)]
                psg = [psb(es, f"M2_pg{i}", [128, 512]) for i in range(K2)]
                psu = [psb(es, f"M2_pu{i}", [128, 512]) for i in range(K2)]
                pso = [psb(es, f"M2_po{i}", [128, 512]) for i in range(K2)]

                def gen_m2(j, t):
                    rows = slice(t * 128, (t + 1) * 128)
                    xt, hb, hTt, sg_, actb, actT, xo_ = xs[j], hbs[j], hTs[j], sgs[j], actbs[j], actTs[j], xo[j]
                    pt_, pg, pu, po = pst[j], psg[j], psu[j], pso[j]
                    k.dma(POOL, chx[j], xt[:], x1d[rows, :], [k.dres("x1d", t)], [xt.res])
                    yield
                    k.op(DVE, lambda: nc.vector.scalar_tensor_tensor(out=hb[:], in0=xt[:], scalar=rstd[:, t:t + 1], in1=gain2[:], op0=ALU.mult, op1=ALU.mult),
                         [xt.res, rstd.res, gain2.res], [hb.res])
                    yield
                    for c in range(8):
                        k.op(PE, lambda c=c: nc.tensor.transpose(out=pt_[:, c * 128:(c + 1) * 128], in_=hb[:, c * 128:(c + 1) * 128], identity=ident),
                             [hb.res, cbf.res], [pt_.res], inc=(c == 7))
                    yield
                    k.op(ACT, lambda: nc.scalar.copy(out=hTt[:], in_=V(pt_[:, :], "p (c n) -> p c n", c=8)), [pt_.res], [hTt.res])
                    yield
                    for blk in range(8):
                        for c in range(8):
                            k.op(PE, lambda c=c, blk=blk: nc.tensor.matmul(pg[:, 0:352], lhsT=hTt[:, c, :], rhs=wf1[:, c, blk * 352:(blk + 1) * 352], start=(c == 0), stop=(c == 7)),
                                 [hTt.res, wf1.res], [pg.res], inc=(c == 7))
                        for c in range(8):
                            k.op(PE, lambda c=c, blk=blk: nc.tensor.matmul(pu[:, 0:352], lhsT=hTt[:, c, :], rhs=wf1[:, c, FH + blk * 352:FH + (blk + 1) * 352], start=(c == 0), stop=(c == 7)),
                                 [hTt.res, wf1.res], [pu.res], inc=(c == 7))
                        yield
                        k.op(ACT, lambda: nc.scalar.activation(out=sg_[:], in_=pg[:, 0:352], func=AF.Silu), [pg.res], [sg_.res])
                        yield
                        k.op(DVE, lambda blk=blk: nc.vector.tensor_tensor(out=actb[:, blk * 352:(blk + 1) * 352], in0=pu[:, 0:352], in1=sg_[:], op=ALU.mult),
                             [pu.res, sg_.res], [actb.res])
                    yield
                    for jj, (c0, n) in enumerate(((0, 8), (8, 8), (16, 6))):
                        for c in range(n):
                            k.op(PE, lambda c=c, c0=c0: nc.tensor.transpose(out=pt_[:, c * 128:(c + 1) * 128], in_=actb[:, (c0 + c) * 128:(c0 + c + 1) * 128], identity=ident),
                                 [actb.res, cbf.res], [pt_.res], inc=(c == n - 1))
                        yield
                        if jj % 2:
                            k.op(ACT, lambda c0=c0, n=n: nc.scalar.copy(out=actT[:, c0:c0 + n, :], in_=V(pt_[:, 0:n * 128], "p (c n) -> p c n", c=n)), [pt_.res], [actT.res])
                        else:
                            k.op(DVE, lambda c0=c0, n=n: nc.vector.tensor_copy(out=actT[:, c0:c0 + n, :], in_=V(pt_[:, 0:n * 128], "p (c n) -> p c n", c=n)), [pt_.res], [actT.res])
                        yield
                    for nb in range(2):
                        ns = slice(nb * 512, (nb + 1) * 512)
                        for c in range(22):
                            k.op(PE, lambda c=c, ns=ns: nc.tensor.matmul(po[:], lhsT=actT[:, c, :], rhs=wf2[:, c, ns], start=(c == 0), stop=(c == 21)),
                                 [actT.res, wf2.res], [po.res], inc=(c == 21))
                        yield
                        k.op(DVE, lambda ns=ns: nc.vector.tensor_tensor(out=xo_[:, ns], in0=po[:], in1=xt[:, ns], op=ALU.add), [po.res, xt.res], [xo_.res])
                        yield
                    if l < NL - 1:
                        k.op(ACT, lambda: nc.scalar.activation(out=junk2[:], in_=xo_[:], func=AF.Square, accum_out=rstdP[:, t:t + 1]),
                             [xo_.res], [junk2.res, rstdP.res])
                    k.dma(SP, cho[j], dst_y[rows, :], xo_[:], [xo_.res], [k.dres(dst_y.tensor.name, t)])

                def lockstep4(gens):
                    gens = list(gens)
                    while gens:
                        nxt = []
                        for g_ in gens:
                            try:
                                next(g_)
                                nxt.append(g_)
                            except StopIteration:
                                pass
                        gens = nxt

                for t0 in range(0, NT, K2):
                    lockstep4([gen_m2(j, t0 + j) for j in range(K2)])
                if l < NL - 1:
                    k.op(DVE, lambda: nc.vector.tensor_scalar(out=rstdP[:], in0=rstdP[:], scalar1=1.0 / D, scalar2=EPS, op0=ALU.mult, op1=ALU.add), [rstdP.res], [rstdP.res])
                    rinv(None, rstdP[:], rstdP.res)
                k.barrier()

        try:
            run_layers()
        except _Stop:
            pass
        k.barrier()
    return nc


def make_consts():
    p = np.arange(128)[:, None]
    f = np.arange(128)[None, :]
    same = (p // 64) == (f // 64)
    c = np.zeros((NCONST, 128, 128), np.float32)
    c[0] = (p == f)
    c[1] = (p >= f)
    c[2] = (p <= f)
    c[3] = (p >= f) & (p >= 64)
    c[4] = (p <= f) & (p < 64)
    c[5] = (p <= f) & same
    c[6] = (p >= f) & same
    c[7] = np.where((f >= p) & same, 0.0, -1e30)
    c[8] = np.where((f <= p) & same, 0.0, -1e30)
    c[9] = (f > p) & same
    c[10] = (f < p) & same
    c[11] = 1.0
    c[12] = same
    return np.ascontiguousarray(c.transpose(1, 0, 2).reshape(128, NCONST * 128))


def make_rope(T, positions):
    inv = np.power(np.float32(500000.0), -np.arange(0, 16, 2, dtype=np.float32) / np.float32(16))
    ang = positions.astype(np.float32)[:, None] * inv[None, :]
    cos, sin = np.cos(ang).astype(np.float32), np.sin(ang).astype(np.float32)
    tab = np.concatenate([cos, cos, sin, sin], axis=1)
    out = np.zeros((3, T, 32), np.float32)
    for g, d in enumerate(DIL):
        perm = np.arange(T).reshape(T // d, d).T.reshape(-1)
        out[g] = tab[perm]
    return out


_PROG = {}


DBG = None


def run_cores(core_inputs, T, NL, weights):
    key = (T, NL)
    if key not in _PROG:
        _PROG[key] = build_program(T, NL, DBG)
    nc = _PROG[key]
    consts = make_consts()
    in_maps = []
    for (x, linked, pos) in core_inputs:
        m = {"x": np.ascontiguousarray(x, dtype=np.float32), "link": np.full((128, 1), 1.0 if linked else 0.0, np.float32),
             "rope": make_rope(T, pos), "consts": consts}
        m.update(weights)
        in_maps.append(m)
    res = run_bass_kernel_spmd(nc, in_maps, core_ids=list(range(len(in_maps))))
    return [r["y"] for r in res.results]


def kernel(x_prompt, x_sample, norm1, w_in, qk_gain, sink, conv_w, a_log, dt_bias, o_gain, w_gate,
           w_br_a, w_br_b, w_br_c, w_out, norm2, w_ffn_in, w_ffn_out):
    T = 8192
    NL = 2
    f = lambda a: np.ascontiguousarray(np.asarray(a, dtype=np.float32))
    weights = {"norm1": f(norm1), "w_in": f(w_in), "qk_gain": f(qk_gain), "sink": f(sink), "conv_w": f(conv_w),
               "a_log": f(a_log).reshape(NL, 16), "dt_bias": f(dt_bias).reshape(NL, 16), "o_gain": f(o_gain), "w_gate": f(w_gate),
               "w_br_a": f(w_br_a), "w_br_b": f(w_br_b), "w_br_c": f(w_br_c), "w_out": f(w_out), "norm2": f(norm2),
               "w_ffn_in": f(w_ffn_in), "w_ffn_out": f(w_ffn_out)}
    xp, xs = f(x_prompt), f(x_sample)
    pos_full = np.arange(T)
    pos_half = np.concatenate([np.arange(T // 2), np.arange(T // 2)])
    zeros = np.zeros((T // 2, D), np.float32)
    cores = [(xp[0], True, pos_full), (xp[1], True, pos_full),
             (np.concatenate([xs[0], xs[1]], 0), False, pos_half), (np.concatenate([xs[2], xs[3]], 0), False, pos_half),
             (np.concatenate([xs[4], zeros], 0), False, pos_half), (np.concatenate([xs[5], zeros], 0), False, pos_half),
             (np.concatenate([xs[6], zeros], 0), False, pos_half), (np.concatenate([xs[7], zeros], 0), False, pos_half)]
    ys = run_cores(cores, T, NL, weights)
    y_prompt = np.stack([ys[0], ys[1]], 0).astype(np.float32)
    h = T // 2
    y_sample = np.stack([ys[2][:h], ys[2][h:], ys[3][:h], ys[3][h:], ys[4][:h], ys[5][:h], ys[6][:h], ys[7][:h]], 0).astype(np.float32)
    return (y_prompt, y_sample)
```

```python
import contextlib
import numpy as np
import concourse.bass as bass
import concourse.mybir as mybir
from concourse.bass_utils import run_bass_kernel_spmd

F32 = mybir.dt.float32
BF16 = mybir.dt.bfloat16
AF = mybir.ActivationFunctionType
ALU = mybir.AluOpType
AX = mybir.AxisListType

D = 1024
INW = 7456
FH = 2816
EPS = 1e-6
DIL = (1, 4, 16)
NCONST = 13
HS = 72


class Res:
    __slots__ = ("w", "r")

    def __init__(self):
        self.w = None
        self.r = {}


class Chan:
    def __init__(self, sem):
        self.sem = sem
        self.cnt = 0


class Eng:
    def __init__(self, e, sem):
        self.e = e
        self.sem = sem
        self.cnt = 0
        self.waited = {}
        self.pend = []

    def wait(self, toks):
        need = {}
        for t in toks:
            if t is None:
                continue
            s, v = t
            if need.get(s, 0) < v:
                need[s] = v
        for s, v in need.items():
            if self.waited.get(s, 0) < v:
                self.e.wait_ge(s, v)
                self.waited[s] = v


def _deps(reads, writes):
    toks = []
    for b in reads:
        toks.append(b.w)
    for b in writes:
        toks.append(b.w)
        toks.extend(b.r.items())
    return toks


def _commit(tok, reads, writes):
    s, v = tok
    for b in reads:
        if b.r.get(s, 0) < v:
            b.r[s] = v
    for b in writes:
        b.w = tok
        b.r = {}


class K:
    def __init__(self, nc):
        self.nc = nc
        self.es = contextlib.ExitStack()
        self.nsem = 0
        self.pe = Eng(nc.tensor, self.sem("pe"))
        self.act = Eng(nc.scalar, self.sem("act"))
        self.dve = Eng(nc.vector, self.sem("dve"))
        self.pool = Eng(nc.gpsimd, self.sem("pool"))
        self.sp = Eng(nc.sync, None)
        self.stopped = False
        self.chans = []
        self.chan_by_name = {}
        self.dram = {}

    def sem(self, name):
        self.nsem += 1
        return self.es.enter_context(self.nc.semaphore(name))

    def chan(self, name):
        c = self.chan_by_name.get(name)
        if c is None:
            c = Chan(self.sem("c_" + name))
            self.chans.append(c)
            self.chan_by_name[name] = c
        return c

    def dres(self, *key):
        r = self.dram.get(key)
        if r is None:
            r = self.dram[key] = Res()
        return r

    def op(self, eng, fn, reads=(), writes=(), inc=True):
        if self.stopped:
            return
        eng.wait(_deps(reads, writes))
        ins = fn()
        if not inc:
            eng.pend.append((reads, writes))
            return
        eng.cnt += 1
        ins.then_inc(eng.sem, 1)
        tok = (eng.sem, eng.cnt)
        for (r, w) in eng.pend:
            _commit(tok, r, w)
        eng.pend = []
        _commit(tok, reads, writes)

    def dma(self, q, ch, out, in_, reads=(), writes=(), slow=False):
        if self.stopped:
            return
        toks = _deps(reads, writes)
        if ch.cnt:
            toks.append((ch.sem, ch.cnt))
        q.wait(toks)
        if slow:
            ins = q.e.dma_start(out=out, in_=in_, allow_slow_non_contiguous=True)
        else:
            ins = q.e.dma_start(out=out, in_=in_)
        ch.cnt += 16
        ins.then_inc(ch.sem, 16)
        _commit((ch.sem, ch.cnt), reads, writes)

    def barrier(self):
        toks = [(e.sem, e.cnt) for e in (self.pe, self.act, self.dve, self.pool) if e.cnt]
        toks += [(c.sem, c.cnt) for c in self.chans if c.cnt]
        for e in (self.pe, self.act, self.dve, self.pool, self.sp):
            e.wait(toks)


class _Stop(Exception):
    pass


class Tl:
    def __init__(self, t):
        self.t = t
        self.res = Res()

    def __getitem__(self, k):
        return self.t[k]


def build_program(T, NL, dbg=None):
    NT = T // 128
    nc = bass.Bass("TRN2", target_bir_lowering=False)
    k = K(nc)
    PE, ACT, DVE, POOL, SP = k.pe, k.act, k.dve, k.pool, k.sp

    def din(name, shape, dt=F32):
        return nc.dram_tensor(name, list(shape), dt, kind="ExternalInput").ap()

    def dscr(name, shape, dt=F32):
        return nc.dram_tensor(name, list(shape), dt, kind="Internal").ap()

    x_in = din("x", [T, D])
    link_in = din("link", [128, 1])
    rope_in = din("rope", [3, 128, NT * 32])
    const_in = din("consts", [128, NCONST * 128])
    norm1 = din("norm1", [NL, D]); w_in = din("w_in", [NL, D, INW]); qk_gain = din("qk_gain", [NL, 4, 64])
    sink = din("sink", [NL, 8]); conv_w = din("conv_w", [NL, 5, 1536]); a_log = din("a_log", [NL, 16])
    dt_bias = din("dt_bias", [NL, 16]); o_gain = din("o_gain", [NL, 64]); w_gate = din("w_gate", [NL, D, 3 * D])
    w_br = [din("w_br_a", [NL, 512, D]), din("w_br_b", [NL, 512, D]), din("w_br_c", [NL, 512, D])]
    w_out = din("w_out", [NL, D, D]); norm2 = din("norm2", [NL, D]); w_f1 = din("w_ffn_in", [NL, D, 2 * FH])
    w_f2 = din("w_ffn_out", [NL, FH, D])
    y_out = nc.dram_tensor("y", [T, D], F32, kind="ExternalOutput").ap()

    xmid = dscr("xmid", [T, D]); x1d = dscr("x1d", [T, D])
    qTa = dscr("qTa", [3, 8, 64, T], BF16); kTa = dscr("kTa", [3, 8, 64, T + 128], BF16)
    va = dscr("va", [3, T + 128, 8 * HS], BF16)
    qTb = dscr("qTb", [8, 64, T], BF16); kTb = dscr("kTb", [2, 64, T + 256], BF16); vbd = dscr("vbd", [T + 256, 2 * HS], BF16)
    cx = dscr("cx", [1536, T + 4]); zs = dscr("zs", [T, 512]); bgd = dscr("bgd", [T, 32])
    oA = dscr("oA", [3, T, 520]); obd = dscr("obd", [T, 512], BF16)
    kTc = dscr("kTc", [8, 64, T], BF16); qTc = dscr("qTc", [8, 64, T], BF16)
    ktok = dscr("ktok", [T, 512], BF16); vtok = dscr("vtok", [T, 512], BF16)
    ofd = dscr("ofd", [T, 512]); obw = dscr("obw", [T, 512]); ocd = dscr("ocd", [T, 512], BF16)

    st = contextlib.ExitStack()

    uid = [0]

    def sb(es, name, shape, dt=F32):
        uid[0] += 1
        return Tl(es.enter_context(nc.sbuf_tensor(f"{name}_{uid[0]}", list(shape), dt)))

    def psb(es, name, shape, dt=F32):
        uid[0] += 1
        return Tl(es.enter_context(nc.psum_tensor(f"{name}_{uid[0]}", list(shape), dt)))

    def V(ap, pat, **kw):
        return ap.rearrange(pat, **kw)

    with k.es, st:
        cst = sb(st, "cst", [128, NCONST, 128])
        cbf = sb(st, "cbf", [128, NCONST, 128], BF16)
        linkc = sb(st, "linkc", [128, 1])
        zero_t = sb(st, "zero_t", [128, 1024], BF16)
        zero_f = sb(st, "zero_f", [128, 4])
        rstdP = sb(st, "rstdP", [128, NT])
        rstd2 = sb(st, "rstd2", [128, NT])
        ch_c = k.chan("const")
        k.dma(SP, ch_c, cst[:], V(const_in, "p (c n) -> p c n", c=NCONST), writes=[cst.res])
        ch_c2 = k.chan("const2")
        k.dma(SP, ch_c2, linkc[:], link_in, writes=[linkc.res])
        k.op(DVE, lambda: nc.vector.tensor_copy(out=cbf[:], in_=cst[:]), [cst.res], [cbf.res])
        k.op(POOL, lambda: nc.gpsimd.memset(zero_t[:], 0.0), [], [zero_t.res])
        k.op(POOL, lambda: nc.gpsimd.memset(zero_f[:], 0.0), [], [zero_f.res])
        IDENT, MLO, MHI, MLOE, MHIE, CMi, CMTi, NEGF, NEGB, SMF, SMB, ONES, BONES = range(13)
        ident = cbf[:, IDENT, :]
        def build_masks(es_m):
            m4 = sb(es_m, "m4", [128, 9, 512], BF16)
            m3 = sb(es_m, "m3", [128, 5, 384], BF16)
            mk = sb(es_m, "mk", [128, 8, 128])
            for i, (full, edge) in enumerate(((MLO, MLOE), (MHI, MHIE))):
                k.op(DVE, lambda i=i, full=full, edge=edge: nc.vector.tensor_sub(out=mk[:, 4 + i, :], in0=cst[:, full, :], in1=cst[:, edge, :]),
                     [cst.res], [mk.res])
                k.op(DVE, lambda i=i, edge=edge: nc.vector.scalar_tensor_tensor(out=mk[:, i, :], in0=mk[:, 4 + i, :], scalar=linkc[:, 0:1],
                                                                                  in1=cst[:, edge, :], op0=ALU.mult, op1=ALU.add),
                     [mk.res, linkc.res, cst.res], [mk.res])
                k.op(DVE, lambda i=i, full=full: nc.vector.tensor_scalar(out=mk[:, 2 + i, :], in0=cst[:, full, :], scalar1=linkc[:, 0:1],
                                                                           scalar2=None, op0=ALU.mult),
                     [cst.res, linkc.res], [mk.res])
            lo_src = {"N": cst[:, MLO, :], "E": cst[:, MLOE, :], "S": mk[:, 0, :]}
            hi_src = {"N": cst[:, MHI, :], "E": cst[:, MHIE, :], "S": mk[:, 1, :]}
            M4IDX = {}
            for a_i, a in enumerate("NES"):
                for b_i, b in enumerate("NES"):
                    idx = a_i * 3 + b_i
                    M4IDX[(a, b)] = idx
                    for u in range(4):
                        src = lo_src[a] if u % 2 == 0 else hi_src[b]
                        k.op(DVE, lambda idx=idx, u=u, src=src: nc.vector.tensor_copy(out=m4[:, idx, u * 128:(u + 1) * 128], in_=src),
                             [cst.res, mk.res], [m4.res])
            lo3 = {"N": cst[:, MLO, :], "L": mk[:, 2, :], "Z": None}
            hi3 = {"N": cst[:, MHI, :], "L": mk[:, 3, :], "Z": None}
            M3IDX = {}
            for idx, (a, b) in enumerate((("Z", "N"), ("N", "N"), ("N", "L"), ("L", "N"), ("N", "Z"))):
                M3IDX[(a, b)] = idx
                for u, src in ((0, lo3[a]), (1, cst[:, ONES, :]), (2, hi3[b])):
                    if src is None:
                        k.op(DVE, lambda idx=idx, u=u: nc.vector.memset(m3[:, idx, u * 128:(u + 1) * 128], 0.0), [], [m3.res])
                    else:
                        k.op(DVE, lambda idx=idx, u=u, src=src: nc.vector.tensor_copy(out=m3[:, idx, u * 128:(u + 1) * 128], in_=src),
                             [cst.res, mk.res], [m3.res])
            return m4, m3, M4IDX, M3IDX

        ch_z = k.chan("zpad")
        for g in range(3):
            for e0 in (0, T + 64):
                k.dma(SP, ch_z, V(kTa[g][:, :, e0:e0 + 64], "q p n -> p q n"), V(zero_t[0:64, 0:512], "p (q n) -> p q n", q=8), [zero_t.res], [])
                k.dma(SP, ch_z, va[g][e0:e0 + 64, :], zero_t[0:64, 0:8 * HS], [zero_t.res], [])
        for e0 in (0, T + 128):
            k.dma(SP, ch_z, V(kTb[:, :, e0:e0 + 128], "q p n -> p q n"), V(zero_t[0:64, 0:256], "p (q n) -> p q n", q=2), [zero_t.res], [])
            k.dma(SP, ch_z, vbd[e0:e0 + 128, :], zero_t[:, 0:2 * HS], [zero_t.res], [])
        for cb in range(12):
            for e0 in (0, T + 2):
                k.dma(SP, ch_z, cx[cb * 128:(cb + 1) * 128, e0:e0 + 2], zero_f[:, 0:2], [zero_f.res], [])
        k.barrier()

        def rstd_prepass(es, src, rstd, tag):
            xs = [sb(es, f"rp_x{tag}{i}", [128, D]) for i in range(2)]
            chs = [k.chan(f"rp{i}") for i in range(2)]
            junk = sb(es, f"rp_j{tag}", [128, D])
            ssq = sb(es, f"rp_s{tag}", [128, NT])
            for t in range(NT):
                xt = xs[t % 2]
                k.dma(SP, chs[t % 2], xt[:], src[t * 128:(t + 1) * 128, :], [k.dres(src.tensor.name, t)], [xt.res])
                k.op(ACT, lambda xt=xt, t=t: nc.scalar.activation(out=junk[:], in_=xt[:], func=AF.Square, accum_out=ssq[:, t:t + 1]),
                     [xt.res], [junk.res, ssq.res])
            k.op(DVE, lambda: nc.vector.tensor_scalar(out=ssq[:], in0=ssq[:], scalar1=1.0 / D, scalar2=EPS, op0=ALU.mult, op1=ALU.add),
                 [ssq.res], [ssq.res])
            k.op(ACT, lambda: nc.scalar.activation(out=ssq[:], in_=ssq[:], func=AF.Ln), [ssq.res], [ssq.res])
            k.op(ACT, lambda: nc.scalar.activation(out=rstd[:], in_=ssq[:], func=AF.Exp, scale=-0.5), [ssq.res], [rstd.res])

        def rinv(eng_small, v, n_res):
            k.op(ACT, lambda: nc.scalar.activation(out=v, in_=v, func=AF.Ln), [n_res], [n_res])
            k.op(ACT, lambda: nc.scalar.activation(out=v, in_=v, func=AF.Exp, scale=-0.5), [n_res], [n_res])

        def transposes(src_tl, src_ap_fn, n, ps_tl, dst_tl, dst_ap, evac_eng, extra_reads=()):
            for c in range(n):
                k.op(PE, lambda c=c: nc.tensor.transpose(out=ps_tl[:, c * 128:(c + 1) * 128], in_=src_ap_fn(c), identity=ident),
                     [src_tl.res, cbf.res] + list(extra_reads), [ps_tl.res], inc=(c == n - 1))
            if evac_eng is ACT:
                k.op(ACT, lambda: nc.scalar.copy(out=dst_ap, in_=V(ps_tl[:, 0:n * 128], "p (c n) -> p c n", c=n)), [ps_tl.res], [dst_tl.res])
            else:
                k.op(DVE, lambda: nc.vector.tensor_copy(out=dst_ap, in_=V(ps_tl[:, 0:n * 128], "p (c n) -> p c n", c=n)), [ps_tl.res], [dst_tl.res])

        def load_bc(es, name, src_row, n, ch):
            t = sb(es, name, [128, n])
            k.dma(SP, ch, t[:], src_row.partition_broadcast(128), [], [t.res])
            return t

        def ckpt(name):
            if dbg == name:
                k.stopped = True

        def run_layers():
          for l in range(NL):
            src_x = x_in if l == 0 else xmid
            ckpt("P0")
            dst_y = xmid if l < NL - 1 else y_out

            with contextlib.ExitStack() as es:
                hT = sb(es, "hT", [128, 8, T], BF16)
                rstd = rstdP
                chg = k.chan("gen")
                qkg = load_bc(es, "qkg", V(qk_gain[l:l + 1], "o a d -> o (a d)"), 256, chg)
                alog = load_bc(es, "alog", a_log[l:l + 1, :], 16, chg)
                dtb = load_bc(es, "dtb", dt_bias[l:l + 1, :], 16, chg)
                negA = sb(es, "negA", [128, 16])
                k.op(ACT, lambda: nc.scalar.activation(out=negA[:], in_=alog[:], func=AF.Exp), [alog.res], [negA.res])
                k.op(DVE, lambda: nc.vector.tensor_scalar(out=negA[:], in0=negA[:], scalar1=-1.0, scalar2=None, op0=ALU.mult), [negA.res], [negA.res])
                if l == 0:
                    with contextlib.ExitStack() as es2:
                        rstd_prepass(es2, src_x, rstd, "P")
                        k.barrier()
                ckpt("P1")
                pst = [psb(es, f"P_pst{i}", [128, 1024], BF16) for i in range(2)]
                with contextlib.ExitStack() as es0:
                    gain1 = load_bc(es0, "gain1", norm1[l:l + 1, :], D, chg)
                    xs = [sb(es0, f"P_x{i}", [128, D]) for i in range(2)]
                    chx = [k.chan(f"P_x{i}") for i in range(2)]
                    hbs = [sb(es0, f"P_hb{i}", [128, D], BF16) for i in range(2)]
                    for t in range(NT):
                        xt, hb, pt = xs[t % 2], hbs[t % 2], pst[t % 2]
                        k.dma(SP, chx[t % 2], xt[:], src_x[t * 128:(t + 1) * 128, :], [k.dres(src_x.tensor.name, t)], [xt.res])
                        k.op(DVE, lambda xt=xt, hb=hb, t=t: nc.vector.scalar_tensor_tensor(out=hb[:], in0=xt[:], scalar=rstd[:, t:t + 1], in1=gain1[:],
                                                                                           op0=ALU.mult, op1=ALU.mult),
                             [xt.res, rstd.res, gain1.res], [hb.res])
                        transposes(hb, lambda c, hb=hb: hb[:, c * 128:(c + 1) * 128], 8, pt, hT, hT[:, :, t * 128:(t + 1) * 128], ACT)
                    k.barrier()
                ckpt("P2")
                KP = 4
                wb = [sb(es, f"P_w{i}", [128, 8, 512], BF16) for i in range(2)]
                chw = [k.chan(f"P_w{i}") for i in range(2)]
                psm = [psb(es, f"P_psm{i}", [128, 512]) for i in range(KP)]
                sqs = [sb(es, f"P_sq{i}", [128, 512]) for i in range(KP)]
                sss = [sb(es, f"P_ss{i}", [128, 8]) for i in range(KP)]
                qn = [sb(es, f"P_qn{i}", [128, 512]) for i in range(KP)]
                tAs = [sb(es, f"P_tA{i}", [128, 8, 16]) for i in range(KP)]
                tBs = [sb(es, f"P_tB{i}", [128, 8, 16]) for i in range(KP)]
                qo = [sb(es, f"P_qo{i}", [128, 512], BF16) for i in range(KP)]
                qTs = [sb(es, f"P_qT{i}", [128, 4, 128], BF16) for i in range(KP)]
                chq = [k.chan(f"P_q{i}") for i in range(KP)]
                rtab = sb(es, "P_rtab", [128, NT, 32])
                chr_ = k.chan("P_r0")
                rtab_g = [-1]

                def need_rope(g):
                    if rtab_g[0] != g:
                        k.dma(SP, chr_, rtab[:], V(rope_in[g], "p (t c) -> p t c", c=32), [], [rtab.res])
                        rtab_g[0] = g
                vo = [sb(es, f"P_vo{i}", [128, 8, HS], BF16) for i in range(KP)]
                chv = [k.chan(f"P_v{i}") for i in range(KP)]
                fo = [sb(es, f"P_fo{i}", [128, 512]) for i in range(2)]
                chf = [k.chan(f"P_f{i}") for i in range(2)]
                sm = [sb(es, f"P_sm{i}", [128, 32]) for i in range(3)]
                for v_ in vo:
                    k.op(POOL, lambda v_=v_: nc.gpsimd.memset(v_[:], 1.0), [], [v_.res])
                cnt = {"w": 0, "i": 0}

                def lockstep(gens):
                    gens = list(gens)
                    while gens:
                        nxt = []
                        for g_ in gens:
                            try:
                                next(g_)
                                nxt.append(g_)
                            except StopIteration:
                                pass
                        gens = nxt

                wplan = [(kind_ * 1536 + g_ * 512, 512) for kind_ in range(3) for g_ in range(3)] + [(4608, 512), (5120, 256)] + \
                        [(5376 + kind_ * 512, 512) for kind_ in range(3)] + [(6912, 512), (7424, 32)]

                def issue_w(i):
                    c0, ncols = wplan[i]
                    k.dma(POOL, chw[i % 2], wb[i % 2][:, :, 0:ncols], V(w_in[l][:, c0:c0 + ncols], "(c p) n -> p c n", p=128), [], [wb[i % 2].res])

                def load_w(c0, ncols):
                    i = cnt["w"]
                    assert wplan[i] == (c0, ncols), (i, wplan[i], c0, ncols)
                    if i == 0:
                        issue_w(0)
                    if i + 1 < len(wplan):
                        issue_w(i + 1)
                    cnt["w"] += 1
                    return wb[i % 2]

                def tok_cols(g, pm):
                    d = DIL[g]
                    per = NT // d
                    r, m = pm // per, pm % per
                    s0 = r + d * 128 * m
                    return slice(s0, s0 + d * 127 + 1, d)

                def proj_tm(w, ncols, cols, ps):
                    for c in range(8):
                        k.op(PE, lambda c=c: nc.tensor.matmul(ps[:, 0:ncols], lhsT=hT[:, c, cols], rhs=w[:, c, 0:ncols], start=(c == 0), stop=(c == 7)),
                             [hT.res, w.res], [ps.res], inc=(c == 7))

                def qk_post(j, ps, H, gi, g, pm):
                    n = H * 64
                    qn_t, sq, ss, tA, tB, qo_t = qn[j], sqs[j], sss[j], tAs[j], tBs[j], qo[j]
                    rt = rtab[:, pm, :]
                    k.op(ACT, lambda: nc.scalar.activation(out=sq[:, 0:n], in_=ps[:, 0:n], func=AF.Square), [ps.res], [sq.res])
                    yield
                    k.op(DVE, lambda: nc.vector.tensor_reduce(out=ss[:, 0:H], in_=V(sq[:, 0:n], "p (h d) -> p h d", h=H), axis=AX.X, op=ALU.add),
                         [sq.res], [ss.res])
                    k.op(DVE, lambda: nc.vector.tensor_scalar(out=ss[:, 0:H], in0=ss[:, 0:H], scalar1=1.0 / 64, scalar2=EPS, op0=ALU.mult, op1=ALU.add),
                         [ss.res], [ss.res])
                    yield
                    rinv(None, ss[:, 0:H], ss.res)
                    yield
                    q3 = V(qn_t[:, 0:n], "p (h d) -> p h d", h=H)
                    k.op(DVE, lambda: nc.vector.tensor_tensor(out=q3, in0=V(ps[:, 0:n], "p (h d) -> p h d", h=H),
                                                              in1=ss[:, 0:H].unsqueeze(2).broadcast_to([128, H, 64]), op=ALU.mult),
                         [ps.res, ss.res], [qn_t.res])
                    yield
                    k.op(POOL, lambda: nc.gpsimd.tensor_tensor(out=q3, in0=q3, in1=qkg[:, gi * 64:(gi + 1) * 64].unsqueeze(1).broadcast_to([128, H, 64]),
                                                               op=ALU.mult), [qn_t.res, qkg.res], [qn_t.res])
                    k.op(POOL, lambda: nc.gpsimd.tensor_tensor(out=tA[:, 0:H, :], in0=q3[:, :, 0:16], in1=rt[:, 0:16].unsqueeze(1).broadcast_to([128, H, 16]),
                                                               op=ALU.mult), [qn_t.res, rtab.res], [tA.res])
                    k.op(POOL, lambda: nc.gpsimd.tensor_tensor(out=tB[:, 0:H, :], in0=q3[:, :, 0:16], in1=rt[:, 16:32].unsqueeze(1).broadcast_to([128, H, 16]),
                                                               op=ALU.mult), [qn_t.res, rtab.res], [tB.res])
                    yield
                    o3 = V(qo_t[:, 0:n], "p (h d) -> p h d", h=H)
                    k.op(DVE, lambda: nc.vector.tensor_sub(out=o3[:, :, 0:8], in0=tA[:, 0:H, 0:8], in1=tB[:, 0:H, 8:16]), [tA.res, tB.res], [qo_t.res])
                    k.op(DVE, lambda: nc.vector.tensor_add(out=o3[:, :, 8:16], in0=tA[:, 0:H, 8:16], in1=tB[:, 0:H, 0:8]), [tA.res, tB.res], [qo_t.res])
                    k.op(ACT, lambda: nc.scalar.copy(out=o3[:, :, 16:64], in_=q3[:, :, 16:64]), [qn_t.res], [qo_t.res])
                    yield

                def store_T(j, npair, dst_ap, dres_key):
                    qT, qo_t = qTs[j], qo[j]
                    transposes(qo_t, lambda c: qo_t[:, c * 128:(c + 1) * 128], npair, pst[j % 2], qT, qT[:, 0:npair, :], ACT)
                    yield
                    for e in range(2):
                        k.dma(SP, chq[j], V(dst_ap(e), "q p n -> p q n"), qT[e * 64:(e + 1) * 64, 0:npair, :], [qT.res], [k.dres(*dres_key)])

                def store_v(j, ps, H, c0, dst_ap, dres_key):
                    v_ = vo[j]
                    k.op(ACT, lambda: nc.scalar.copy(out=v_[:, 0:H, 0:64], in_=V(ps[:, c0:c0 + H * 64], "p (h d) -> p h d", h=H)), [ps.res], [v_.res])
                    yield
                    k.dma(SP, chv[j], dst_ap, V(v_[:, 0:H, :], "p h d -> p (h d)"), [v_.res], [k.dres(*dres_key)])

                def tile_A(j, w, kind, g, pm):
                    ps = psm[j]
                    proj_tm(w, 512, tok_cols(g, pm), ps)
                    yield
                    if kind < 2:
                        yield from qk_post(j, ps, 8, kind, g, pm)
                        if kind == 0:
                            yield from store_T(j, 4, lambda e: qTa[g][e:8:2, :, pm * 128:(pm + 1) * 128], ("qTa", g, pm))
                        else:
                            yield from store_T(j, 4, lambda e: kTa[g][e:8:2, :, 64 + pm * 128:64 + (pm + 1) * 128], ("kTa", g, pm))
                    else:
                        yield from store_v(j, ps, 8, 0, va[g][64 + pm * 128:64 + (pm + 1) * 128, :], ("va", g, pm))

                def tile_Bq(j, w, pm):
                    ps = psm[j]
                    proj_tm(w, 512, slice(pm * 128, (pm + 1) * 128), ps)
                    yield
                    yield from qk_post(j, ps, 8, 2, 0, pm)
                    yield from store_T(j, 4, lambda e: qTb[e:8:2, :, pm * 128:(pm + 1) * 128], ("qTb", pm))

                def tile_Bkv(j, w, pm):
                    ps = psm[j]
                    proj_tm(w, 256, slice(pm * 128, (pm + 1) * 128), ps)
                    yield
                    yield from qk_post(j, ps, 2, 3, 0, pm)
                    yield from store_T(j, 1, lambda e: kTb[e:e + 1, :, 128 + pm * 128:128 + (pm + 1) * 128], ("kTb", pm))
                    yield from store_v(j, ps, 2, 128, vbd[128 + pm * 128:128 + (pm + 1) * 128, :], ("vbd", pm))

                for kind in range(3):
                    for g in range(3):
                        w = load_w(kind * 1536 + g * 512, 512)
                        if kind < 2:
                            need_rope(g)
                        for p0 in range(0, NT, KP):
                            lockstep([tile_A(j, w, kind, g, p0 + j) for j in range(min(KP, NT - p0))])
                ckpt("P3")
                w = load_w(4608, 512)
                need_rope(0)
                for p0 in range(0, NT, KP):
                    lockstep([tile_Bq(j, w, p0 + j) for j in range(min(KP, NT - p0))])
                w = load_w(5120, 256)
                for p0 in range(0, NT, KP):
                    lockstep([tile_Bkv(j, w, p0 + j) for j in range(min(KP, NT - p0))])
                ckpt("P4")
                for kind in range(3):
                    w = load_w(5376 + kind * 512, 512)
                    for c4 in range(4):
                        for tr in range(T // 512):
                            i = cnt["i"]; cnt["i"] += 1
                            ps = psm[i % 3]
                            for c in range(8):
                                k.op(PE, lambda c=c, c4=c4, tr=tr, ps=ps: nc.tensor.matmul(ps[:, :], lhsT=w[:, c, c4 * 128:(c4 + 1) * 128],
                                                                                         rhs=hT[:, c, tr * 512:(tr + 1) * 512], start=(c == 0), stop=(c == 7)),
                                     [hT.res, w.res], [ps.res], inc=(c == 7))
                            f = fo[i % 2]
                            if i % 2 == 0:
                                k.op(ACT, lambda f=f, ps=ps: nc.scalar.copy(out=f[:], in_=ps[:]), [ps.res], [f.res])
                            else:
                                k.op(DVE, lambda f=f, ps=ps: nc.vector.tensor_copy(out=f[:], in_=ps[:]), [ps.res], [f.res])
                            cb = kind * 4 + c4
                            k.dma(SP, chf[i % 2], cx[cb * 128:(cb + 1) * 128, 2 + tr * 512:2 + (tr + 1) * 512], f[:], [f.res], [k.dres("cx", cb, tr // 4)])
                ckpt("P5")
                w = load_w(6912, 512)
                for pm in range(NT):
                    ps = psm[pm % 3]
                    proj_tm(w, 512, slice(pm * 128, (pm + 1) * 128), ps)
                    f = fo[pm % 2]
                    k.op(ACT, lambda f=f, ps=ps: nc.scalar.activation(out=f[:], in_=ps[:], func=AF.Silu), [ps.res], [f.res])
                    k.dma(SP, chf[pm % 2], zs[pm * 128:(pm + 1) * 128, :], f[:], [f.res], [k.dres("zs", pm)])
                ckpt("P6")
                w = load_w(7424, 32)
                for pm in range(NT):
                    ps = psm[pm % 3]
                    proj_tm(w, 32, slice(pm * 128, (pm + 1) * 128), ps)
                    f = fo[pm % 2]
                    s0, s1, s2 = sm
                    k.op(ACT, lambda f=f, ps=ps: nc.scalar.activation(out=f[:, 0:16], in_=ps[:, 0:16], func=AF.Sigmoid), [ps.res], [f.res])
                    k.op(DVE, lambda ps=ps: nc.vector.tensor_add(out=s0[:, 0:16], in0=ps[:, 16:32], in1=dtb[:]), [ps.res, dtb.res], [s0.res])
                    k.op(DVE, lambda: nc.vector.tensor_scalar(out=s1[:, 0:16], in0=s0[:, 0:16], scalar1=30.0, scalar2=None, op0=ALU.min), [s0.res], [s1.res])
                    k.op(ACT, lambda: nc.scalar.activation(out=s1[:, 0:16], in_=s1[:, 0:16], func=AF.Exp), [s1.res], [s1.res])
                    k.op(DVE, lambda: nc.vector.tensor_scalar(out=s1[:, 0:16], in0=s1[:, 0:16], scalar1=1.0, scalar2=None, op0=ALU.add), [s1.res], [s1.res])
                    k.op(ACT, lambda: nc.scalar.activation(out=s1[:, 0:16], in_=s1[:, 0:16], func=AF.Ln), [s1.res], [s1.res])
                    k.op(DVE, lambda: nc.vector.tensor_scalar(out=s2[:, 0:16], in0=s0[:, 0:16], scalar1=30.0, scalar2=-30.0, op0=ALU.max, op1=ALU.add), [s0.res], [s2.res])
                    k.op(DVE, lambda: nc.vector.tensor_add(out=s2[:, 0:16], in0=s2[:, 0:16], in1=s1[:, 0:16]), [s1.res, s2.res], [s2.res])
                    k.op(DVE, lambda f=f: nc.vector.tensor_mul(out=f[:, 16:32], in0=s2[:, 0:16], in1=negA[:]), [s2.res, negA.res], [f.res])
                    k.dma(SP, chf[pm % 2], bgd[pm * 128:(pm + 1) * 128, :], f[:, 0:32], [f.res], [k.dres("bgd", pm)])
                k.barrier()
            if dbg == "P":
                break

            with contextlib.ExitStack() as es:
                m4, m3, M4IDX, M3IDX = build_masks(es)
                qs = [sb(es, f"A_q{i}", [64, 8, 512], BF16) for i in range(2)]
                ks = [sb(es, f"A_k{i}", [64, 8, 768], BF16) for i in range(2)]
                vs = [sb(es, f"A_v{i}", [128, 6, 8 * HS], BF16) for i in range(2)]
                chq = [k.chan(f"A_q{i}") for i in range(2)]
                chk = [k.chan(f"A_k{i}") for i in range(2)]
                chv = [k.chan(f"A_v{i}") for i in range(2)]
                psS = [psb(es, f"A_pS{i}", [128, 512]) for i in range(4)]
                psO = [psb(es, f"A_pO{i}", [128, 512]) for i in range(4)]
                pt = [sb(es, f"A_pt{i}", [128, 512], BF16) for i in range(4)]
                pmk = [sb(es, f"A_pm{i}", [128, 512], BF16) for i in range(4)]
                osb = [sb(es, f"A_o{i}", [128, 8, 65]) for i in range(2)]
                cho = [k.chan(f"A_o{i}") for i in range(2)]
                obb = [sb(es, f"A_ob{i}", [128, 512], BF16) for i in range(2)]
                chg = k.chan("gen")
                esink = load_bc(es, "esink", sink[l:l + 1, :], 8, chg)
                k.op(ACT, lambda: nc.scalar.activation(out=esink[:], in_=esink[:], func=AF.Exp), [esink.res], [esink.res])
                den = sb(es, "A_den", [128, 8])
                NSL = 4

                def rolling(unit_iter, K):
                    active = []
                    it = iter(unit_iter)
                    done = False
                    while True:
                        nxt = []
                        for g_ in active:
                            try:
                                next(g_)
                                nxt.append(g_)
                            except StopIteration:
                                pass
                        active = nxt
                        if not done and len(active) < K:
                            try:
                                g_ = next(it)
                                next(g_)
                                active.append(g_)
                            except StopIteration:
                                done = True
                        if done and not active:
                            break

                uctr = [0]
                octr = [0]

                def loads_A(g, sbk):
                    a = sbk * 512
                    q_, k_, v_ = qs[sbk % 2], ks[sbk % 2], vs[sbk % 2]
                    k.dma(SP, chq[sbk % 2], q_[:], V(qTa[g][:, :, a:a + 512], "q p n -> p q n"),
                          [k.dres("qTa", g, sbk * 4 + j) for j in range(4)], [q_.res])
                    k.dma(SP, chk[sbk % 2], k_[:, :, 0:640], V(kTa[g][:, :, a:a + 640], "q p n -> p q n"),
                          [k.dres("kTa", g, j) for j in range(max(0, sbk * 4 - 1), min(NT, sbk * 4 + 5))], [k_.res])
                    k.dma(SP, chv[sbk % 2], v_[:, 0:5, :], V(va[g][a:a + 640, :], "(t p) c -> p t c", p=128),
                          [k.dres("va", g, j) for j in range(max(0, sbk * 4 - 1), min(NT, sbk * 4 + 5))], [v_.res])

                def loads_B(sbk):
                    a = sbk * 512
                    q_, k_, v_ = qs[sbk % 2], ks[sbk % 2], vs[sbk % 2]
                    k.dma(SP, chq[sbk % 2], q_[:], V(qTb[:, :, a:a + 512], "q p n -> p q n"), [k.dres("qTb", sbk * 4 + j) for j in range(4)], [q_.res])
                    rng = range(max(0, sbk * 4 - 1), min(NT, sbk * 4 + 5))
                    k.dma(SP, chk[sbk % 2], k_[:, 0:2, :], V(kTb[:, :, a:a + 768], "q p n -> p q n"), [k.dres("kTb", j) for j in rng], [k_.res])
                    k.dma(SP, chv[sbk % 2], v_[:, :, 0:2 * HS], V(vbd[a:a + 768, :], "(t p) c -> p t c", p=128), [k.dres("vbd", j) for j in rng], [v_.res])

                def unit_A(g, sbk, qi, hp, midx, oslot):
                    sl = uctr[0] % NSL
                    uctr[0] += 1
                    q_, k_, v_ = qs[sbk % 2], ks[sbk % 2], vs[sbk % 2]
                    pS, p_t, p_m = psS[sl], pt[sl], pmk[sl]
                    pO = (psO[oslot * 2], psO[oslot * 2 + 1])
                    for e in range(2):
                        for u in range(2):
                            k.op(PE, lambda e=e, u=u: nc.tensor.matmul(
                                pS[:, (2 * e + u) * 128:(2 * e + u + 1) * 128], lhsT=k_[:, 2 * hp + e, (qi + u) * 128:(qi + u + 1) * 128],
                                rhs=q_[:, 2 * hp + e, qi * 128:(qi + 1) * 128], start=True, stop=True),
                                [k_.res, q_.res], [pS.res], inc=(e == 1 and u == 1))
                    yield
                    k.op(ACT, lambda: nc.scalar.activation(out=p_t[:], in_=pS[:], func=AF.Exp, scale=0.125), [pS.res], [p_t.res])
                    yield
                    meng = DVE if sl % 2 == 0 else POOL
                    k.op(meng, lambda: meng.e.tensor_tensor(out=p_m[:], in0=p_t[:], in1=m4[:, midx, :], op=ALU.mult), [p_t.res, m4.res], [p_m.res])
                    yield
                    for e in range(2):
                        h = 2 * hp + e
                        po = pO[h // 4]
                        for u in range(2):
                            k.op(PE, lambda e=e, u=u, h=h, po=po: nc.tensor.matmul(
                                po[:, (h % 4) * HS:(h % 4) * HS + 65], lhsT=p_m[:, (2 * e + u) * 128:(2 * e + u + 1) * 128],
                                rhs=v_[:, qi + u, h * HS:h * HS + 65], start=(u == 0), stop=(u == 1)),
                                [p_m.res, v_.res], [po.res], inc=(e == 1 and u == 1))
                    if hp == 3:
                        yield
                        yield from fin_A(g, sbk * 4 + qi, oslot)

                def fin_A(g, pm, oslot):
                    d = DIL[g]
                    per = NT // d
                    o_ = osb[oslot]
                    pO = (psO[oslot * 2], psO[oslot * 2 + 1])
                    k.op(ACT, lambda: nc.scalar.copy(out=o_[:, 0:4, :], in_=V(pO[0][:, 0:4 * HS], "p (h d) -> p h d", h=4)[:, :, 0:65]), [pO[0].res], [o_.res])
                    k.op(DVE, lambda: nc.vector.tensor_copy(out=o_[:, 4:8, :], in_=V(pO[1][:, 0:4 * HS], "p (h d) -> p h d", h=4)[:, :, 0:65]), [pO[1].res], [o_.res])
                    yield
                    r, m = pm // per, pm % per
                    s0 = r + d * 128 * m
                    k.dma(SP, cho[oslot], oA[g][s0:s0 + d * 127 + 1:d, :], V(o_[:], "p h d -> p (h d)"), [o_.res],
                          [k.dres("oA", g, tt) for tt in range((s0 // 128), min(NT, (s0 + d * 127) // 128 + 1))])

                def units_A(g):
                    d = DIL[g]
                    seg = NT // (2 * d)
                    loads_A(g, 0)
                    for sbk in range(NT // 4):
                        for qi in range(4):
                            if qi == 2 and sbk + 1 < NT // 4:
                                loads_A(g, sbk + 1)
                            pm = sbk * 4 + qi
                            sidx, sp_ = pm // seg, pm % seg
                            lo_kind = "N" if sp_ != 0 else ("S" if sidx % 2 == 1 else "E")
                            hi_kind = "N" if sp_ != seg - 1 else ("S" if sidx % 2 == 0 else "E")
                            midx = M4IDX[(lo_kind, hi_kind)]
                            oslot = octr[0] % 2
                            octr[0] += 1
                            for hp in range(4):
                                yield unit_A(g, sbk, qi, hp, midx, oslot)

                def unit_B(sbk, qi, hq, midx, oslot):
                    sl = uctr[0] % NSL
                    uctr[0] += 1
                    q_, k_, v_ = qs[sbk % 2], ks[sbk % 2], vs[sbk % 2]
                    pS, p_t, p_m = psS[sl], pt[sl], pmk[sl]
                    pO = (psO[oslot * 2], psO[oslot * 2 + 1])
                    kvh = hq // 4
                    for u in range(3):
                        k.op(PE, lambda u=u: nc.tensor.matmul(
                            pS[:, u * 128:(u + 1) * 128], lhsT=k_[:, kvh, (qi + u) * 128:(qi + u + 1) * 128],
                            rhs=q_[:, hq, qi * 128:(qi + 1) * 128], start=True, stop=True),
                            [k_.res, q_.res], [pS.res], inc=(u == 2))
                    yield
                    k.op(ACT, lambda: nc.scalar.activation(out=p_t[:, 0:384], in_=pS[:, 0:384], func=AF.Exp, scale=0.125), [pS.res], [p_t.res])
                    yield
                    meng = DVE if sl % 2 == 0 else POOL
                    k.op(meng, lambda: meng.e.tensor_tensor(out=p_m[:, 0:384], in0=p_t[:, 0:384], in1=m3[:, midx, :], op=ALU.mult), [p_t.res, m3.res], [p_m.res])
                    yield
                    po = pO[hq // 4]
                    for u in range(3):
                        k.op(PE, lambda u=u: nc.tensor.matmul(
                            po[:, (hq % 4) * HS:(hq % 4) * HS + 65], lhsT=p_m[:, u * 128:(u + 1) * 128],
                            rhs=v_[:, qi + u, kvh * HS:kvh * HS + 65], start=(u == 0), stop=(u == 2)),
                            [p_m.res, v_.res], [po.res], inc=(u == 2))
                    if hq == 7:
                        yield
                        pm = sbk * 4 + qi
                        o_ = osb[oslot]
                        ob_ = obb[oslot]
                        k.op(ACT, lambda: nc.scalar.copy(out=o_[:, 0:4, :], in_=V(pO[0][:, 0:4 * HS], "p (h d) -> p h d", h=4)[:, :, 0:65]), [pO[0].res], [o_.res])
                        k.op(DVE, lambda: nc.vector.tensor_copy(out=o_[:, 4:8, :], in_=V(pO[1][:, 0:4 * HS], "p (h d) -> p h d", h=4)[:, :, 0:65]), [pO[1].res], [o_.res])
                        yield
                        dn = dens[oslot]
                        k.op(DVE, lambda: nc.vector.tensor_add(out=dn[:], in0=o_[:, :, 64], in1=esink[:]), [o_.res, esink.res], [dn.res])
                        k.op(DVE, lambda: nc.vector.reciprocal(out=dn[:], in_=dn[:]), [dn.res], [dn.res])
                        k.op(DVE, lambda: nc.vector.tensor_tensor(out=V(ob_[:], "p (h d) -> p h d", h=8), in0=o_[:, :, 0:64],
                                                                  in1=dn[:].unsqueeze(2).broadcast_to([128, 8, 64]), op=ALU.mult),
                             [o_.res, dn.res], [ob_.res])
                        yield
                        k.dma(SP, cho[oslot], obd[pm * 128:(pm + 1) * 128, :], ob_[:], [ob_.res], [k.dres("obd", pm)])

                def units_B():
                    loads_B(0)
                    for sbk in range(NT // 4):
                        for qi in range(4):
                            if qi == 2 and sbk + 1 < NT // 4:
                                loads_B(sbk + 1)
                            pm = sbk * 4 + qi
                            if pm == 0:
                                mk_ = ("Z", "N")
                            elif pm == NT - 1:
                                mk_ = ("N", "Z")
                            elif pm == NT // 2:
                                mk_ = ("L", "N")
                            elif pm == NT // 2 - 1:
                                mk_ = ("N", "L")
                            else:
                                mk_ = ("N", "N")
                            oslot = octr[0] % 2
                            octr[0] += 1
                            for hq in range(8):
                                yield unit_B(sbk, qi, hq, M3IDX[mk_], oslot)

                dens = [sb(es, f"A_den{i}", [128, 8]) for i in range(2)]
                for g in range(3):
                    rolling(units_A(g), NSL)
                rolling(units_B(), NSL)
                k.barrier()
            if dbg == "AB":
                break

            with contextlib.ExitStack() as es:
                NR = T // 2048
                cw = sb(es, "C1_cw", [128, 12, 5])
                chg = k.chan("gen")
                for j in range(5):
                    k.dma(SP, chg, cw[:, :, j], V(conv_w[l][j, :], "(b p) -> p b", p=128), [], [cw.res], slow=True)
                KC1 = 2
                xin = [sb(es, f"C1_x{i}", [128, 2052]) for i in range(KC1)]
                chx = [k.chan(f"C1_x{i}") for i in range(KC1)]
                yvs = [sb(es, f"C1_y{i}", [128, 2048]) for i in range(KC1)]
                ysls = [sb(es, f"C1_ys{i}", [128, 2048]) for i in range(KC1)]
                sqbs = [sb(es, f"C1_sq{i}", [128, 2048], BF16) for i in range(KC1)]
                ynb = [sb(es, f"C1_yn{i}", [128, 2048], BF16) for i in range(KC1)]
                chn = [k.chan(f"C1_n{i}") for i in range(KC1)]
                psn = [[psb(es, f"C1_pn{j}{i}", [128, 512]) for i in range(2)] for j in range(KC1)]
                rvs = [[sb(es, f"C1_rv{j}{i}", [128, 512]) for i in range(2)] for j in range(KC1)]
                pst = [[psb(es, f"C1_pt{j}{i}", [128, 1024], BF16) for i in range(2)] for j in range(KC1)]
                tks = [sb(es, f"C1_tk{i}", [128, 16, 128], BF16) for i in range(KC1)]
                cht = [k.chan(f"C1_t{i}") for i in range(KC1)]

                def gen_c1(j, cb, rg):
                    kind = cb // 4
                    t0 = rg * 2048
                    xi, yv, ysl, sqb, yn, tk = xin[j], yvs[j], ysls[j], sqbs[j], ynb[j], tks[j]
                    k.dma(POOL, chx[j], xi[:], cx[cb * 128:(cb + 1) * 128, t0:t0 + 2052], [k.dres("cx", cb, rg)] +
                          ([k.dres("cx", cb, rg - 1)] if rg > 0 else []) + ([k.dres("cx", cb, rg + 1)] if rg < NR - 1 else []), [xi.res])
                    yield
                    if t0 == T // 2:
                        k.op(DVE, lambda: nc.vector.tensor_scalar(out=xi[:, 0:2], in0=xi[:, 0:2], scalar1=linkc[:, 0:1], scalar2=None, op0=ALU.mult),
                             [xi.res, linkc.res], [xi.res])
                    if t0 + 2048 == T // 2:
                        k.op(DVE, lambda: nc.vector.tensor_scalar(out=xi[:, 2050:2052], in0=xi[:, 2050:2052], scalar1=linkc[:, 0:1], scalar2=None, op0=ALU.mult),
                             [xi.res, linkc.res], [xi.res])
                    k.op(DVE, lambda: nc.vector.tensor_scalar(out=yv[:], in0=xi[:, 0:2048], scalar1=cw[:, cb, 0:1], scalar2=None, op0=ALU.mult),
                         [xi.res, cw.res], [yv.res])
                    for jj in range(1, 5):
                        k.op(DVE, lambda jj=jj: nc.vector.scalar_tensor_tensor(out=yv[:], in0=xi[:, jj:jj + 2048], scalar=cw[:, cb, jj:jj + 1], in1=yv[:],
                                                                              op0=ALU.mult, op1=ALU.add), [xi.res, cw.res, yv.res], [yv.res])
                    yield
                    k.op(ACT, lambda: nc.scalar.activation(out=ysl[:], in_=yv[:], func=AF.Silu), [yv.res], [ysl.res])
                    if kind < 2:
                        k.op(ACT, lambda: nc.scalar.activation(out=sqb[:], in_=ysl[:], func=AF.Square), [ysl.res], [sqb.res])
                        yield
                        for s4 in range(4):
                            pn = psn[j][s4 % 2]; r_ = rvs[j][s4 % 2]
                            k.op(PE, lambda pn=pn, s4=s4: nc.tensor.matmul(pn[:], lhsT=cbf[:, BONES, :], rhs=sqb[:, s4 * 512:(s4 + 1) * 512], start=True, stop=True),
                                 [cbf.res, sqb.res], [pn.res])
                            yield
                            k.op(DVE, lambda pn=pn, r_=r_: nc.vector.tensor_scalar(out=r_[:], in0=pn[:], scalar1=EPS, scalar2=None, op0=ALU.add), [pn.res], [r_.res])
                            yield
                            rinv(None, r_[:], r_.res)
                            yield
                            k.op(DVE, lambda r_=r_, s4=s4: nc.vector.scalar_tensor_tensor(
                                out=yn[:, s4 * 512:(s4 + 1) * 512], in0=ysl[:, s4 * 512:(s4 + 1) * 512], scalar=(0.125 if kind == 0 else 1.0), in1=r_[:],
                                op0=ALU.mult, op1=ALU.mult), [ysl.res, r_.res], [yn.res])
                        yield
                        dstT = qTc if kind == 0 else kTc
                        for e in range(2):
                            k.dma(SP, chn[j], dstT[2 * (cb % 4) + e][:, t0:t0 + 2048], yn[e * 64:(e + 1) * 64, :], [yn.res], [k.dres("qkTc", kind, cb % 4, rg)])
                    else:
                        yield
                        k.op(ACT, lambda: nc.scalar.copy(out=yn[:], in_=ysl[:]), [ysl.res], [yn.res])
                        yield
                    if kind >= 1:
                        for hf in range(2):
                            for c in range(8):
                                k.op(PE, lambda c=c, hf=hf: nc.tensor.transpose(out=pst[j][hf][:, c * 128:(c + 1) * 128], in_=yn[:, (hf * 8 + c) * 128:(hf * 8 + c + 1) * 128],
                                                                              identity=ident), [yn.res, cbf.res], [pst[j][hf].res], inc=(c == 7))
                        yield
                        k.op(DVE, lambda: nc.vector.tensor_copy(out=tk[:, 0:8, :], in_=V(pst[j][0][:, :], "p (c n) -> p c n", c=8)), [pst[j][0].res], [tk.res])
                        k.op(ACT, lambda: nc.scalar.copy(out=tk[:, 8:16, :], in_=V(pst[j][1][:, :], "p (c n) -> p c n", c=8)), [pst[j][1].res], [tk.res])
                        yield
                        dsttok = ktok if kind == 1 else vtok
                        k.dma(SP, cht[j], V(dsttok[t0:t0 + 2048, (cb % 4) * 128:(cb % 4 + 1) * 128], "(t p) c -> p t c", p=128), tk[:], [tk.res],
                              [k.dres("kvtok", kind, cb % 4, rg)])

                def lockstep1(gens):
                    gens = list(gens)
                    while gens:
                        nxt = []
                        for g_ in gens:
                            try:
                                next(g_)
                                nxt.append(g_)
                            except StopIteration:
                                pass
                        gens = nxt

                work = [(cb, rg) for cb in range(12) for rg in range(NR)]
                for w0 in range(0, len(work), KC1):
                    lockstep1([gen_c1(j, *work[w0 + j]) for j in range(min(KC1, len(work) - w0))])
                k.barrier()

            if dbg == "C1":
                break

            with contextlib.ExitStack() as es:
                negm4 = sb(es, "C2_negm4", [128, 2, 4, 128])
                for d_ in range(2):
                    for q_i in range(4):
                        k.op(DVE, lambda d_=d_, q_i=q_i: nc.vector.tensor_copy(out=negm4[:, d_, q_i, :], in_=cst[:, NEGF + d_, :]), [cst.res], [negm4.res])

                def lockstep2(gens):
                    gens = list(gens)
                    while gens:
                        nxt = []
                        for g_ in gens:
                            try:
                                next(g_)
                                nxt.append(g_)
                            except StopIteration:
                                pass
                        gens = nxt

                def gen_dir(dr):
                    D_ = f"d{dr}"
                    kTt = [sb(es, f"C2_kT{D_}{i}", [64, 8, 128], BF16) for i in range(2)]
                    qTt = [sb(es, f"C2_qT{D_}{i}", [64, 8, 128], BF16) for i in range(2)]
                    ktk = [sb(es, f"C2_kt{D_}{i}", [128, 8, 64], BF16) for i in range(2)]
                    vtk = [sb(es, f"C2_vt{D_}{i}", [128, 8, 64], BF16) for i in range(2)]
                    bgt = [sb(es, f"C2_bg{D_}{i}", [128, 32]) for i in range(2)]
                    chl = [[k.chan(f"C2_l{D_}{j}{i}") for i in range(2)] for j in range(5)]
                    B0, B1, B2 = [psb(es, f"C2_b{D_}{i}", [128, 512]) for i in range(3)]
                    psT16 = psb(es, f"C2_pT16{D_}", [128, 1024], BF16)
                    gcm = sb(es, f"C2_gcm{D_}", [128, 4, 128])
                    gsum = sb(es, f"C2_gsum{D_}", [128, 16])
                    sc = sb(es, f"C2_sc{D_}", [128, 6, 8])
                    GL = sb(es, f"C2_GL{D_}", [64, 2, 8])
                    tmp = sb(es, f"C2_tmp{D_}", [128, 8, 128]); DT = sb(es, f"C2_DT{D_}", [128, 8, 128]); DTs = sb(es, f"C2_DTs{D_}", [128, 8, 128])
                    UK = sb(es, f"C2_UK{D_}", [128, 8, 128])
                    QKm = sb(es, f"C2_QKm{D_}", [128, 8, 128], BF16)
                    UR = [sb(es, f"C2_UR{D_}{i}", [128, 8, 2, 128], BF16) for i in range(2)]
                    Lm = [sb(es, f"C2_L{D_}{i}", [128, 8, 128], BF16) for i in range(2)]
                    XT = sb(es, f"C2_XT{D_}", [128, 8, 128], BF16)
                    XVb = sb(es, f"C2_XVb{D_}", [128, 8, 64])
                    kE = sb(es, f"C2_kE{D_}", [128, 8, 64], BF16); kdm = sb(es, f"C2_kd{D_}", [128, 8, 64], BF16)
                    wT = sb(es, f"C2_wT{D_}", [64, 8, 128], BF16)
                    S = sb(es, f"C2_S{D_}", [64, 8, 64]); Sb = sb(es, f"C2_Sb{D_}", [64, 8, 64], BF16)
                    vt_ = sb(es, f"C2_vtt{D_}", [128, 8, 64]); vnew = sb(es, f"C2_vn{D_}", [128, 8, 64], BF16)
                    ot_ = sb(es, f"C2_ot{D_}", [128, 8, 64])
                    osbs = [sb(es, f"C2_o{D_}{i}", [128, 8, 64]) for i in range(2)]
                    cho = [k.chan(f"C2_o{D_}{i}") for i in range(2)]
                    CMx = CMi if dr == 0 else CMTi
                    SMx = SMF if dr == 0 else SMB
                    odst = ofd if dr == 0 else obw
                    k.op(DVE, lambda: nc.vector.memset(S[:], 0.0), [], [S.res])
                    k.op(DVE, lambda: nc.vector.memset(Sb[:], 0.0), [], [Sb.res])
                    order = list(range(NT)) if dr == 0 else list(range(NT - 1, -1, -1))
                    for it, n in enumerate(order):
                        b2 = it % 2
                        kT_, qT_, kt_, vt_k, bg_ = kTt[b2], qTt[b2], ktk[b2], vtk[b2], bgt[b2]
                        cs_ = slice(n * 128, (n + 1) * 128)

                        def loads(it_l):
                            n_l = order[it_l]
                            bl = it_l % 2
                            csl = slice(n_l * 128, (n_l + 1) * 128)
                            rgl = n_l // 16
                            k.dma(SP, chl[0][bl], kTt[bl][:], V(kTc[:, :, csl], "q p n -> p q n"), [k.dres("qkTc", 1, j, rgl) for j in range(4)], [kTt[bl].res])
                            k.dma(SP, chl[1][bl], qTt[bl][:], V(qTc[:, :, csl], "q p n -> p q n"), [k.dres("qkTc", 0, j, rgl) for j in range(4)], [qTt[bl].res])
                            k.dma(SP, chl[2][bl], V(ktk[bl][:], "p h d -> p (h d)"), ktok[csl, :], [k.dres("kvtok", 1, j, rgl) for j in range(4)], [ktk[bl].res])
                            k.dma(SP, chl[3][bl], V(vtk[bl][:], "p h d -> p (h d)"), vtok[csl, :], [k.dres("kvtok", 2, j, rgl) for j in range(4)], [vtk[bl].res])
                            k.dma(SP, chl[4][bl], bgt[bl][:], bgd[csl, :], [k.dres("bgd", n_l)], [bgt[bl].res])

                        if it == 0:
                            loads(0)
                        if it + 1 < NT:
                            loads(it + 1)
                        gcol = bg_[:, 16 + dr * 8:24 + dr * 8]
                        bcol = bg_[:, dr * 8:dr * 8 + 8]
                        pG = B2
                        k.op(PE, lambda: nc.tensor.matmul(pG[:, 0:8], lhsT=cst[:, CMi, :], rhs=gcol, start=True, stop=True), [cst.res, bg_.res], [pG.res], inc=False)
                        k.op(PE, lambda: nc.tensor.matmul(pG[:, 8:16], lhsT=cst[:, CMTi, :], rhs=gcol, start=True, stop=True), [cst.res, bg_.res], [pG.res], inc=False)
                        for c in range(2):
                            k.op(PE, lambda c=c: nc.tensor.matmul(pG[0:64, 32 + c * 8:40 + c * 8], lhsT=cst[:, BONES, c * 64:(c + 1) * 64], rhs=gcol, start=True, stop=True),
                                 [cst.res, bg_.res], [pG.res], inc=(c == 1))
                        yield
                        k.op(ACT, lambda: nc.scalar.copy(out=gsum[:], in_=pG[:, 0:16]), [pG.res], [gsum.res])
                        k.op(ACT, lambda: nc.scalar.activation(out=V(GL[:], "p c h -> p (c h)"), in_=pG[0:64, 32:48], func=AF.Exp), [pG.res], [GL.res])
                        yield
                        own = gsum[:, 0:8] if dr == 0 else gsum[:, 8:16]
                        oth = gsum[:, 8:16] if dr == 0 else gsum[:, 0:8]
                        k.op(DVE, lambda: nc.vector.tensor_copy(out=sc[:, 0, :], in_=own), [gsum.res], [sc.res])
                        k.op(ACT, lambda: nc.scalar.activation(out=sc[:, 1, :], in_=own, func=AF.Exp), [gsum.res], [sc.res])
                        k.op(DVE, lambda: nc.vector.tensor_sub(out=sc[:, 4, :], in0=oth, in1=gcol), [gsum.res, bg_.res], [sc.res])
                        k.op(DVE, lambda: nc.vector.tensor_scalar(out=sc[:, 3, :], in0=bcol, scalar1=-1.0, scalar2=None, op0=ALU.mult), [bg_.res], [sc.res])
                        yield
                        k.op(ACT, lambda: nc.scalar.activation(out=sc[:, 2, :], in_=sc[:, 4, :], func=AF.Exp), [sc.res], [sc.res])
                        k.op(POOL, lambda: nc.gpsimd.tensor_tensor(out=kE[:], in0=kt_[:], in1=sc[:, 1, :].unsqueeze(2).broadcast_to([128, 8, 64]), op=ALU.mult),
                             [kt_.res, sc.res], [kE.res])
                        yield
                        k.op(POOL, lambda: nc.gpsimd.tensor_tensor(out=kdm[:], in0=kt_[:], in1=sc[:, 2, :].unsqueeze(2).broadcast_to([128, 8, 64]), op=ALU.mult),
                             [kt_.res, sc.res], [kdm.res])
                        U0 = UR[0]
                        for hf in range(2):
                            hs = slice(hf * 4, (hf + 1) * 4)
                            k.op(POOL, lambda hs=hs: nc.gpsimd.tensor_tensor(out=gcm[:], in0=cst[:, CMx, :].unsqueeze(1).broadcast_to([128, 4, 128]),
                                                                             in1=gcol[:, hs].unsqueeze(2).broadcast_to([128, 4, 128]), op=ALU.mult),
                                 [cst.res, bg_.res], [gcm.res])
                            yield
                            k.op(PE, lambda: nc.tensor.matmul(B2[:], lhsT=cst[:, ONES, :], rhs=V(gcm[:], "p h n -> p (h n)"), start=True, stop=True),
                                 [cst.res, gcm.res], [B2.res])
                            for hl in range(4):
                                h = hf * 4 + hl
                                k.op(PE, lambda h=h, hl=hl: nc.tensor.matmul(B0[:, hl * 128:(hl + 1) * 128], lhsT=kT_[:, h, :], rhs=kT_[:, h, :], start=True, stop=True),
                                     [kT_.res], [B0.res], inc=(hl == 3))
                            for hl in range(4):
                                h = hf * 4 + hl
                                k.op(PE, lambda h=h, hl=hl: nc.tensor.matmul(B1[:, hl * 128:(hl + 1) * 128], lhsT=kT_[:, h, :], rhs=qT_[:, h, :], start=True, stop=True),
                                     [kT_.res, qT_.res], [B1.res], inc=(hl == 3))
                            yield
                            k.op(DVE, lambda hs=hs: nc.vector.tensor_tensor(out=V(tmp[:, hs, :], "p h n -> p (h n)"), in0=B2[:],
                                                                            in1=V(negm4[:, dr, :, :], "p h n -> p (h n)"), op=ALU.add), [B2.res, negm4.res], [tmp.res])
                            k.op(POOL, lambda hs=hs: nc.gpsimd.tensor_tensor(out=tmp[:, hs, :], in0=tmp[:, hs, :], in1=sc[:, 0, hs].unsqueeze(2).broadcast_to([128, 4, 128]),
                                                                             op=ALU.subtract), [tmp.res, sc.res], [tmp.res])
                            yield
                            k.op(ACT, lambda hs=hs: nc.scalar.activation(out=DT[:, hs, :], in_=tmp[:, hs, :], func=AF.Exp), [tmp.res], [DT.res])
                            yield
                            k.op(POOL, lambda hs=hs: nc.gpsimd.tensor_tensor(out=DTs[:, hs, :], in0=DT[:, hs, :], in1=cst[:, SMx, :].unsqueeze(1).broadcast_to([128, 4, 128]),
                                                                             op=ALU.mult), [DT.res, cst.res], [DTs.res])
                            k.op(DVE, lambda hs=hs: nc.vector.tensor_tensor(out=QKm[:, hs, :], in0=V(B1[:], "p (h n) -> p h n", h=4), in1=DT[:, hs, :], op=ALU.mult),
                                 [B1.res, DT.res], [QKm.res])
                            yield
                            for hl in range(4):
                                h = hf * 4 + hl
                                k.op(DVE, lambda h=h, hl=hl: nc.vector.scalar_tensor_tensor(out=U0[:, h, 0, :], in0=B0[:, hl * 128:(hl + 1) * 128], scalar=bcol[:, h:h + 1],
                                                                                            in1=DTs[:, h, :], op0=ALU.mult, op1=ALU.mult),
                                     [B0.res, bg_.res, DTs.res], [U0.res])
                            yield
                            k.op(POOL, lambda hs=hs: nc.gpsimd.tensor_tensor(out=U0[:, hs, 1, :], in0=cbf[:, IDENT, :].unsqueeze(1).broadcast_to([128, 4, 128]),
                                                                             in1=U0[:, hs, 0, :], op=ALU.subtract), [U0.res, cbf.res], [U0.res])
                        yield
                        L0 = Lm[0]
                        for h in range(8):
                            k.op(PE, lambda h=h: nc.tensor.transpose(out=psT16[:, h * 128:(h + 1) * 128], in_=U0[:, h, 0, :], identity=ident),
                                 [U0.res, cbf.res], [psT16.res], inc=(h == 7))
                        yield
                        k.op(ACT, lambda: nc.scalar.copy(out=L0[:], in_=V(psT16[:, :], "p (h n) -> p h n", h=8)), [psT16.res], [L0.res])
                        yield
                        for hb in range(2):
                            hs = slice(hb * 4, (hb + 1) * 4)
                            pUR = (B0, B1); pL = B2
                            cur = 0
                            for lvl in range(6):
                                Uc, Lc = UR[cur], Lm[cur]
                                Un, Ln_ = UR[1 - cur], Lm[1 - cur]
                                for hl in range(4):
                                    h = hb * 4 + hl
                                    pu = pUR[hl // 2]
                                    off = (hl % 2) * 256
                                    if lvl == 0:
                                        k.op(PE, lambda h=h, pu=pu, off=off: nc.tensor.matmul(pu[:, off:off + 128], lhsT=Lc[:, h, :], rhs=Uc[:, h, 0, :], start=True, stop=True),
                                             [Uc.res, Lc.res], [pu.res], inc=False)
                                    elif lvl < 5:
                                        k.op(PE, lambda h=h, pu=pu, off=off: nc.tensor.matmul(pu[:, off:off + 256], lhsT=Lc[:, h, :],
                                                                                                rhs=V(Uc[:, h, :, :], "p a n -> p (a n)"), start=True, stop=True),
                                             [Uc.res, Lc.res], [pu.res], inc=False)
                                    else:
                                        k.op(PE, lambda h=h, pu=pu, off=off: nc.tensor.matmul(pu[:, off + 128:off + 256], lhsT=Lc[:, h, :], rhs=Uc[:, h, 1, :], start=True, stop=True),
                                             [Uc.res, Lc.res], [pu.res], inc=(hl == 3))
                                    if lvl < 5:
                                        k.op(PE, lambda h=h, hl=hl: nc.tensor.matmul(pL[:, hl * 128:(hl + 1) * 128], lhsT=Uc[:, h, 0, :], rhs=Lc[:, h, :], start=True, stop=True),
                                             [Uc.res, Lc.res], [pL.res], inc=(hl == 3))
                                yield
                                if lvl < 5:
                                    for hh in range(2):
                                        hs2 = slice(hb * 4 + hh * 2, hb * 4 + hh * 2 + 2)
                                        pv = V(pUR[hh][:], "p (h a n) -> p h a n", h=2, a=2)
                                        k.op(ACT, lambda hs2=hs2, pv=pv: nc.scalar.copy(out=Un[:, hs2, 0, :], in_=pv[:, :, 0, :]), [pUR[hh].res], [Un.res])
                                        if lvl == 0:
                                            k.op(POOL, lambda hs2=hs2: nc.gpsimd.tensor_copy(out=Un[:, hs2, 1, :], in_=Uc[:, hs2, 1, :]), [Uc.res], [Un.res])
                                        else:
                                            k.op(DVE, lambda hs2=hs2, pv=pv: nc.vector.tensor_tensor(out=Un[:, hs2, 1, :], in0=pv[:, :, 1, :], in1=Uc[:, hs2, 1, :], op=ALU.add),
                                                 [pUR[hh].res, Uc.res], [Un.res])
                                    k.op(ACT, lambda: nc.scalar.copy(out=Ln_[:, hs, :], in_=V(pL[:], "p (h n) -> p h n", h=4)), [pL.res], [Ln_.res])
                                    cur = 1 - cur
                                else:
                                    for hh in range(2):
                                        hs2 = slice(hb * 4 + hh * 2, hb * 4 + hh * 2 + 2)
                                        pv = V(pUR[hh][:], "p (h a n) -> p h a n", h=2, a=2)
                                        k.op(DVE, lambda hs2=hs2, pv=pv: nc.vector.tensor_tensor(out=XT[:, hs2, :], in0=pv[:, :, 1, :], in1=Uc[:, hs2, 1, :], op=ALU.add),
                                             [pUR[hh].res, Uc.res], [XT.res])
                                yield
                        pX = B0
                        for h in range(8):
                            k.op(PE, lambda h=h: nc.tensor.matmul(pX[:, h * 64:(h + 1) * 64], lhsT=XT[:, h, :], rhs=vt_k[:, h, :], start=True, stop=True),
                                 [XT.res, vt_k.res], [pX.res], inc=(h == 7))
                        pW = (B1, B2)
                        for h in range(8):
                            k.op(PE, lambda h=h: nc.tensor.matmul(pW[h // 4][0:64, (h % 4) * 128:(h % 4 + 1) * 128], lhsT=kE[:, h, :], rhs=XT[:, h, :], start=True, stop=True),
                                 [kE.res, XT.res], [pW[h // 4].res], inc=(h % 4 == 3))
                        yield
                        k.op(DVE, lambda: nc.vector.tensor_tensor(out=XVb[:], in0=V(pX[:], "p (h d) -> p h d", h=8), in1=bcol.unsqueeze(2).broadcast_to([128, 8, 64]), op=ALU.mult),
                             [pX.res, bg_.res], [XVb.res])
                        for hf in range(2):
                            k.op(ACT, lambda hf=hf: nc.scalar.copy(out=wT[:, hf * 4:(hf + 1) * 4, :], in_=V(pW[hf][0:64, :], "p (q n) -> p q n", q=4)), [pW[hf].res], [wT.res])
                        yield
                        if (dr == 0 and n == NT // 2) or (dr == 1 and n == NT // 2 - 1):
                            k.op(DVE, lambda: nc.vector.tensor_scalar(out=S[:], in0=S[:], scalar1=linkc[0:64, 0:1], scalar2=None, op0=ALU.mult), [S.res, linkc.res], [S.res])
                            k.op(ACT, lambda: nc.scalar.copy(out=Sb[:], in_=S[:]), [S.res], [Sb.res])
                            yield
                        o_ = osbs[b2]
                        pP, pI, pA, pD = B0, B1, B2, B0
                        for c in ((0, 1) if dr == 0 else (1, 0)):
                            cs = slice(c * 64, (c + 1) * 64)
                            for h in range(8):
                                k.op(PE, lambda h=h: nc.tensor.matmul(pP[cs, h * 64:(h + 1) * 64], lhsT=wT[:, h, cs], rhs=Sb[:, h, :], start=True, stop=True),
                                     [wT.res, Sb.res], [pP.res], inc=(h == 7))
                            for h in range(8):
                                k.op(PE, lambda h=h: nc.tensor.matmul(pI[cs, h * 64:(h + 1) * 64], lhsT=qT_[:, h, cs], rhs=Sb[:, h, :], start=True, stop=True),
                                     [qT_.res, Sb.res], [pI.res], inc=(h == 7))
                            yield
                            k.op(DVE, lambda: nc.vector.tensor_tensor(out=vt_[cs, :, :], in0=V(pP[cs, :], "p (h d) -> p h d", h=8),
                                                                      in1=sc[cs, 3, :].unsqueeze(2).broadcast_to([64, 8, 64]), op=ALU.mult), [pP.res, sc.res], [vt_.res])
                            k.op(POOL, lambda: nc.gpsimd.tensor_add(out=vnew[cs, :, :], in0=vt_[cs, :, :], in1=XVb[cs, :, :]), [vt_.res, XVb.res], [vnew.res])
                            k.op(DVE, lambda: nc.vector.tensor_tensor(out=ot_[cs, :, :], in0=V(pI[cs, :], "p (h d) -> p h d", h=8),
                                                                      in1=sc[cs, 1, :].unsqueeze(2).broadcast_to([64, 8, 64]), op=ALU.mult), [pI.res, sc.res], [ot_.res])
                            k.op(POOL, lambda c=c: nc.gpsimd.tensor_tensor(out=S[:], in0=S[:], in1=GL[:, c, :].unsqueeze(2).broadcast_to([64, 8, 64]), op=ALU.mult),
                                 [S.res, GL.res], [S.res])
                            yield
                            for h in range(8):
                                k.op(PE, lambda h=h: nc.tensor.matmul(pD[0:64, h * 64:(h + 1) * 64], lhsT=kdm[cs, h, :], rhs=vnew[cs, h, :], start=True, stop=True),
                                     [kdm.res, vnew.res], [pD.res], inc=(h == 7))
                            for h in range(8):
                                k.op(PE, lambda h=h: nc.tensor.matmul(pA[cs, h * 64:(h + 1) * 64], lhsT=QKm[cs, h, cs], rhs=vnew[cs, h, :], start=True, stop=True),
                                     [QKm.res, vnew.res], [pA.res], inc=(h == 7))
                            yield
                            k.op(DVE, lambda: nc.vector.tensor_tensor(out=Sb[:], in0=S[:], in1=V(pD[0:64, :], "p (q d) -> p q d", q=8), op=ALU.add), [S.res, pD.res], [Sb.res])
                            k.op(DVE, lambda: nc.vector.tensor_tensor(out=S[:], in0=S[:], in1=V(pD[0:64, :], "p (q d) -> p q d", q=8), op=ALU.add), [S.res, pD.res], [S.res])
                            k.op(DVE, lambda: nc.vector.tensor_tensor(out=o_[cs, :, :], in0=V(pA[cs, :], "p (h d) -> p h d", h=8), in1=ot_[cs, :, :], op=ALU.add),
                                 [pA.res, ot_.res], [o_.res])
                            yield
                        k.dma(SP, cho[b2], odst[cs_, :], V(o_[:], "p h d -> p (h d)"), [o_.res], [k.dres("ofd" if dr == 0 else "obw", n)])

                lockstep2([gen_dir(0), gen_dir(1)])
                k.barrier()

            if dbg == "C2":
                break

            with contextlib.ExitStack() as es:
                rstd = rstdP
                junk = sb(es, "M1_junk", [128, D])
                chg = k.chan("gen")
                gain1 = load_bc(es, "M1_gain", norm1[l:l + 1, :], D, chg)
                wg = sb(es, "M1_wg", [128, 8, 3 * D], BF16)
                wbr = [sb(es, f"M1_wbr{i}", [128, 4, D], BF16) for i in range(3)]
                wo = sb(es, "M1_wo", [128, 8, D], BF16)
                chw = k.chan("M1_w")
                for c in range(8):
                    k.dma(POOL, chw, wg[:, c, :], w_gate[l][c * 128:(c + 1) * 128, :], [], [wg.res])
                for i in range(3):
                    k.dma(POOL, chw, wbr[i][:], V(w_br[i][l], "(c p) n -> p c n", p=128), [], [wbr[i].res])
                k.dma(POOL, chw, wo[:], V(w_out[l], "(c p) n -> p c n", p=128), [], [wo.res])
                KM = 2
                xs = [sb(es, f"M1_x{i}", [128, D]) for i in range(KM)]
                chx = [k.chan(f"M1_x{i}") for i in range(KM)]
                hbs = [sb(es, f"M1_hb{i}", [128, D], BF16) for i in range(KM)]
                hTs = [sb(es, f"M1_hT{i}", [128, 8, 128], BF16) for i in range(KM)]
                oats = [[sb(es, f"M1_oa{j}{i}", [128, 8, 65]) for i in range(3)] for j in range(KM)]
                chas = [[k.chan(f"M1_a{j}{i}") for i in range(3)] for j in range(KM)]
                obts = [[sb(es, f"M1_ob{j}{i}", [128, 512], BF16) for i in range(3)] for j in range(KM)]
                chbs = [[k.chan(f"M1_b{j}{i}") for i in range(1)] for j in range(KM)]
                oTs = [[sb(es, f"M1_oT{j}{i}", [128, 4, 128], BF16) for i in range(3)] for j in range(KM)]
                dens = [sb(es, f"M1_den{i}", [128, 8]) for i in range(KM)]
                ogain = load_bc(es, "M1_ogain", o_gain[l:l + 1, :], 64, chg)
                ofl = [sb(es, f"M1_of{i}", [128, 8, 64]) for i in range(KM)]
                obl = [sb(es, f"M1_obw{i}", [128, 8, 64]) for i in range(KM)]
                zsl = [sb(es, f"M1_zs{i}", [128, 512]) for i in range(KM)]
                osq = [sb(es, f"M1_osq{i}", [128, 512]) for i in range(KM)]
                oss = [sb(es, f"M1_oss{i}", [128, 8]) for i in range(KM)]
                chc = [[k.chan(f"M1_c{j}{i}") for i in range(3)] for j in range(KM)]
                gsbs = [sb(es, f"M1_gs{i}", [128, 512]) for i in range(KM)]
                tms = [sb(es, f"M1_tm{i}", [128, 512]) for i in range(KM)]
                mgs = [sb(es, f"M1_mg{i}", [128, D]) for i in range(KM)]
                mgbs = [sb(es, f"M1_mgb{i}", [128, D], BF16) for i in range(KM)]
                mTs = [sb(es, f"M1_mT{i}", [128, 8, 128], BF16) for i in range(KM)]
                xo = [sb(es, f"M1_xo{i}", [128, D]) for i in range(KM)]
                cho = [k.chan(f"M1_o{i}") for i in range(KM)]
                pst = [psb(es, f"M1_pt{i}", [128, 1024], BF16) for i in range(KM)]
                psg = [psb(es, f"M1_pg{i}", [128, 512]) for i in range(KM)]
                psp = [psb(es, f"M1_pp{i}", [128, 512]) for i in range(KM)]
                pso = [psb(es, f"M1_po{i}", [128, 512]) for i in range(KM)]

                def gen_m1(j, t):
                    rows = slice(t * 128, (t + 1) * 128)
                    xt, hb, hTt, oat, obt, oT, den, gsb, tm, mg, mgb, mT, xo_ = (xs[j], hbs[j], hTs[j], oats[j], obts[j], oTs[j], dens[j], gsbs[j], tms[j],
                                                                                  mgs[j], mgbs[j], mTs[j], xo[j])
                    pt_, pg, pp, po = pst[j], psg[j], psp[j], pso[j]
                    k.dma(POOL, chx[j], xt[:], src_x[rows, :], [k.dres(src_x.tensor.name, t)], [xt.res])
                    for g in range(3):
                        k.dma(POOL, chas[j][g], V(oat[g][:], "p h d -> p (h d)"), oA[g][rows, :], [k.dres("oA", g, t)], [oat[g].res])
                    k.dma(POOL, chbs[j][0], obt[1][:], obd[rows, :], [k.dres("obd", t)], [obt[1].res])
                    of_, ob_, zs_, sq_, ss_ = ofl[j], obl[j], zsl[j], osq[j], oss[j]
                    k.dma(POOL, chc[j][0], V(of_[:], "p h d -> p (h d)"), ofd[rows, :], [k.dres("ofd", t)], [of_.res])
                    k.dma(POOL, chc[j][1], V(ob_[:], "p h d -> p (h d)"), obw[rows, :], [k.dres("obw", t)], [ob_.res])
                    k.dma(POOL, chc[j][2], zs_[:], zs[rows, :], [k.dres("zs", t)], [zs_.res])
                    def oc_chain():
                        k.op(DVE, lambda: nc.vector.tensor_add(out=of_[:], in0=of_[:], in1=ob_[:]), [of_.res, ob_.res], [of_.res])
                        yield
                        k.op(ACT, lambda: nc.scalar.activation(out=sq_[:], in_=V(of_[:], "p h d -> p (h d)"), func=AF.Square), [of_.res], [sq_.res])
                        yield
                        k.op(DVE, lambda: nc.vector.tensor_reduce(out=ss_[:], in_=V(sq_[:], "p (h d) -> p h d", h=8), axis=AX.X, op=ALU.add), [sq_.res], [ss_.res])
                        k.op(DVE, lambda: nc.vector.tensor_scalar(out=ss_[:], in0=ss_[:], scalar1=1.0 / 64, scalar2=EPS, op0=ALU.mult, op1=ALU.add), [ss_.res], [ss_.res])
                        yield
                        rinv(None, ss_[:], ss_.res)
                        yield
                        k.op(DVE, lambda: nc.vector.tensor_tensor(out=of_[:], in0=of_[:], in1=ss_[:].unsqueeze(2).broadcast_to([128, 8, 64]), op=ALU.mult),
                             [of_.res, ss_.res], [of_.res])
                        yield
                        k.op(POOL, lambda: nc.gpsimd.tensor_tensor(out=of_[:], in0=of_[:], in1=ogain[:].unsqueeze(1).broadcast_to([128, 8, 64]), op=ALU.mult),
                             [of_.res, ogain.res], [of_.res])
                        k.op(POOL, lambda: nc.gpsimd.tensor_tensor(out=obt[2][:], in0=V(of_[:], "p h d -> p (h d)"), in1=zs_[:], op=ALU.mult), [of_.res, zs_.res], [obt[2].res])
                        yield
                    oc = oc_chain()
                    next(oc, None)
                    yield
                    k.op(DVE, lambda: nc.vector.scalar_tensor_tensor(out=hb[:], in0=xt[:], scalar=rstd[:, t:t + 1], in1=gain1[:], op0=ALU.mult, op1=ALU.mult),
                         [xt.res, rstd.res, gain1.res], [hb.res])
                    k.op(POOL, lambda: nc.gpsimd.tensor_add(out=oat[0][:], in0=oat[0][:], in1=oat[1][:]), [oat[0].res, oat[1].res], [oat[0].res])
                    k.op(POOL, lambda: nc.gpsimd.tensor_add(out=oat[0][:], in0=oat[0][:], in1=oat[2][:]), [oat[0].res, oat[2].res], [oat[0].res])
                    next(oc, None)
                    yield
                    for c in range(8):
                        k.op(PE, lambda c=c: nc.tensor.transpose(out=pt_[:, c * 128:(c + 1) * 128], in_=hb[:, c * 128:(c + 1) * 128], identity=ident),
                             [hb.res, cbf.res], [pt_.res], inc=(c == 7))
                    k.op(DVE, lambda: nc.vector.reciprocal(out=den[:], in_=oat[0][:, :, 64]), [oat[0].res], [den.res])
                    k.op(DVE, lambda: nc.vector.tensor_tensor(out=V(obt[0][:], "p (h d) -> p h d", h=8), in0=oat[0][:, :, 0:64],
                                                              in1=den[:].unsqueeze(2).broadcast_to([128, 8, 64]), op=ALU.mult), [oat[0].res, den.res], [obt[0].res])
                    next(oc, None)
                    yield
                    k.op(ACT, lambda: nc.scalar.copy(out=hTt[:], in_=V(pt_[:, :], "p (c n) -> p c n", c=8)), [pt_.res], [hTt.res])
                    next(oc, None)
                    yield
                    for br in range(3):
                        if br == 2:
                            for _ in oc:
                                pass
                        else:
                            next(oc, None)
                        for c in range(4):
                            k.op(PE, lambda c=c, br=br: nc.tensor.transpose(out=pt_[:, c * 128:(c + 1) * 128], in_=obt[br][:, c * 128:(c + 1) * 128], identity=ident),
                                 [obt[br].res, cbf.res], [pt_.res], inc=(c == 3))
                        yield
                        if br % 2:
                            k.op(DVE, lambda br=br: nc.vector.tensor_copy(out=oT[br][:], in_=V(pt_[:, 0:512], "p (c n) -> p c n", c=4)), [pt_.res], [oT[br].res])
                        else:
                            k.op(ACT, lambda br=br: nc.scalar.copy(out=oT[br][:], in_=V(pt_[:, 0:512], "p (c n) -> p c n", c=4)), [pt_.res], [oT[br].res])
                        yield
                    for nb in range(2):
                        ns = slice(nb * 512, (nb + 1) * 512)
                        for br in range(3):
                            for c in range(8):
                                k.op(PE, lambda c=c, br=br, nb=nb: nc.tensor.matmul(pg[:], lhsT=hTt[:, c, :], rhs=wg[:, c, br * D + nb * 512:br * D + (nb + 1) * 512],
                                                                                  start=(c == 0), stop=(c == 7)), [hTt.res, wg.res], [pg.res], inc=(c == 7))
                            for c in range(4):
                                k.op(PE, lambda c=c, br=br, ns=ns: nc.tensor.matmul(pp[:], lhsT=oT[br][:, c, :], rhs=wbr[br][:, c, ns], start=(c == 0), stop=(c == 3)),
                                     [oT[br].res, wbr[br].res], [pp.res], inc=(c == 3))
                            yield
                            k.op(ACT, lambda: nc.scalar.activation(out=gsb[:], in_=pg[:], func=AF.Sigmoid), [pg.res], [gsb.res])
                            yield
                            if br == 0:
                                k.op(DVE, lambda ns=ns: nc.vector.tensor_tensor(out=mg[:, ns], in0=pp[:], in1=gsb[:], op=ALU.mult), [pp.res, gsb.res], [mg.res])
                            else:
                                k.op(DVE, lambda: nc.vector.tensor_tensor(out=tm[:], in0=pp[:], in1=gsb[:], op=ALU.mult), [pp.res, gsb.res], [tm.res])
                                k.op(POOL, lambda ns=ns: nc.gpsimd.tensor_add(out=mg[:, ns], in0=mg[:, ns], in1=tm[:]), [mg.res, tm.res], [mg.res])
                            yield
                    k.op(ACT, lambda: nc.scalar.copy(out=mgb[:], in_=mg[:]), [mg.res], [mgb.res])
                    yield
                    for c in range(8):
                        k.op(PE, lambda c=c: nc.tensor.transpose(out=pt_[:, c * 128:(c + 1) * 128], in_=mgb[:, c * 128:(c + 1) * 128], identity=ident),
                             [mgb.res, cbf.res], [pt_.res], inc=(c == 7))
                    yield
                    k.op(ACT, lambda: nc.scalar.copy(out=mT[:], in_=V(pt_[:, :], "p (c n) -> p c n", c=8)), [pt_.res], [mT.res])
                    yield
                    for nb in range(2):
                        ns = slice(nb * 512, (nb + 1) * 512)
                        for c in range(8):
                            k.op(PE, lambda c=c, ns=ns: nc.tensor.matmul(po[:], lhsT=mT[:, c, :], rhs=wo[:, c, ns], start=(c == 0), stop=(c == 7)),
                                 [mT.res, wo.res], [po.res], inc=(c == 7))
                        yield
                        k.op(DVE, lambda ns=ns: nc.vector.tensor_tensor(out=xo_[:, ns], in0=po[:], in1=xt[:, ns], op=ALU.add), [po.res, xt.res], [xo_.res])
                        yield
                    k.op(ACT, lambda: nc.scalar.activation(out=junk[:], in_=xo_[:], func=AF.Square, accum_out=rstd2[:, t:t + 1]),
                         [xo_.res], [junk.res, rstd2.res])
                    k.dma(SP, cho[j], x1d[rows, :], xo_[:], [xo_.res], [k.dres("x1d", t)])

                def lockstep3(gens):
                    gens = list(gens)
                    while gens:
                        nxt = []
                        for g_ in gens:
                            try:
                                next(g_)
                                nxt.append(g_)
                            except StopIteration:
                                pass
                        gens = nxt

                for t0 in range(0, NT, KM):
                    lockstep3([gen_m1(j, t0 + j) for j in range(KM)])
                k.op(DVE, lambda: nc.vector.tensor_scalar(out=rstd2[:], in0=rstd2[:], scalar1=1.0 / D, scalar2=EPS, op0=ALU.mult, op1=ALU.add), [rstd2.res], [rstd2.res])
                rinv(None, rstd2[:], rstd2.res)
                k.barrier()

            if dbg == "M1":
                break

            with contextlib.ExitStack() as es:
                rstd = rstd2
                chg = k.chan("gen")
                gain2 = load_bc(es, "M2_gain", norm2[l:l + 1, :], D, chg)
                wf1 = sb(es, "M2_w1", [128, 8, 2 * FH], BF16)
                wf2 = sb(es, "M2_w2", [128, 22, D], BF16)
                chw = k.chan("M2_w")
                for c in range(8):
                    for hh in range(2):
                        k.dma(POOL, chw, wf1[:, c, hh * FH:(hh + 1) * FH], w_f1[l][c * 128:(c + 1) * 128, hh * FH:(hh + 1) * FH], [], [wf1.res])
                for c0 in range(0, 22, 4):
                    c1 = min(22, c0 + 4)
                    k.dma(POOL, chw, wf2[:, c0:c1, :], V(w_f2[l][c0 * 128:c1 * 128, :], "(c p) n -> p c n", p=128), [], [wf2.res])
                K2 = 2
                junk2 = sb(es, "M2_junk", [128, D])
                xs = [sb(es, f"M2_x{i}", [128, D]) for i in range(K2)]
                chx = [k.chan(f"M2_x{i}") for i in range(K2)]
                hbs = [sb(es, f"M2_hb{i}", [128, D], BF16) for i in range(K2)]
                hTs = [sb(es, f"M2_hT{i}", [128, 8, 128], BF16) for i in range(K2)]
                sgs = [sb(es, f"M2_sg{i}", [128, 352]) for i in range(K2)]
                actbs = [sb(es, f"M2_act{i}", [128, FH], BF16) for i in range(K2)]
                actTs = [sb(es, f"M2_actT{i}", [128, 22, 128], BF16) for i in range(K2)]
                xo = [sb(es, f"M2_xo{i}", [128, D]) for i in range(K2)]
                cho = [k.chan(f"M2_o{i}") for i in range(K2)]
                pst = [psb(es, f"M2_pt{i}", [128, 1024], BF16) for i in range(K2)]
                psg = [psb(es, f"M2_pg{i}", [128, 512]) for i in range(K2)]
                psu = [psb(es, f"M2_pu{i}", [128, 512]) for i in range(K2)]
                pso = [psb(es, f"M2_po{i}", [128, 512]) for i in range(K2)]

                def gen_m2(j, t):
                    rows = slice(t * 128, (t + 1) * 128)
                    xt, hb, hTt, sg_, actb, actT, xo_ = xs[j], hbs[j], hTs[j], sgs[j], actbs[j], actTs[j], xo[j]
                    pt_, pg, pu, po = pst[j], psg[j], psu[j], pso[j]
                    k.dma(POOL, chx[j], xt[:], x1d[rows, :], [k.dres("x1d", t)], [xt.res])
                    yield
                    k.op(DVE, lambda: nc.vector.scalar_tensor_tensor(out=hb[:], in0=xt[:], scalar=rstd[:, t:t + 1], in1=gain2[:], op0=ALU.mult, op1=ALU.mult),
                         [xt.res, rstd.res, gain2.res], [hb.res])
                    yield
                    for c in range(8):
                        k.op(PE, lambda c=c: nc.tensor.transpose(out=pt_[:, c * 128:(c + 1) * 128], in_=hb[:, c * 128:(c + 1) * 128], identity=ident),
                             [hb.res, cbf.res], [pt_.res], inc=(c == 7))
                    yield
                    k.op(ACT, lambda: nc.scalar.copy(out=hTt[:], in_=V(pt_[:, :], "p (c n) -> p c n", c=8)), [pt_.res], [hTt.res])
                    yield
                    for blk in range(8):
                        for c in range(8):
                            k.op(PE, lambda c=c, blk=blk: nc.tensor.matmul(pg[:, 0:352], lhsT=hTt[:, c, :], rhs=wf1[:, c, blk * 352:(blk + 1) * 352], start=(c == 0), stop=(c == 7)),
                                 [hTt.res, wf1.res], [pg.res], inc=(c == 7))
                        for c in range(8):
                            k.op(PE, lambda c=c, blk=blk: nc.tensor.matmul(pu[:, 0:352], lhsT=hTt[:, c, :], rhs=wf1[:, c, FH + blk * 352:FH + (blk + 1) * 352], start=(c == 0), stop=(c == 7)),
                                 [hTt.res, wf1.res], [pu.res], inc=(c == 7))
                        yield
                        k.op(ACT, lambda: nc.scalar.activation(out=sg_[:], in_=pg[:, 0:352], func=AF.Silu), [pg.res], [sg_.res])
                        yield
                        k.op(DVE, lambda blk=blk: nc.vector.tensor_tensor(out=actb[:, blk * 352:(blk + 1) * 352], in0=pu[:, 0:352], in1=sg_[:], op=ALU.mult),
                             [pu.res, sg_.res], [actb.res])
                    yield
                    for jj, (c0, n) in enumerate(((0, 8), (8, 8), (16, 6))):
                        for c in range(n):
                            k.op(PE, lambda c=c, c0=c0: nc.tensor.transpose(out=pt_[:, c * 128:(c + 1) * 128], in_=actb[:, (c0 + c) * 128:(c0 + c + 1) * 128], identity=ident),
                                 [actb.res, cbf.res], [pt_.res], inc=(c == n - 1))
                        yield
                        if jj % 2:
                            k.op(ACT, lambda c0=c0, n=n: nc.scalar.copy(out=actT[:, c0:c0 + n, :], in_=V(pt_[:, 0:n * 128], "p (c n) -> p c n", c=n)), [pt_.res], [actT.res])
                        else:
                            k.op(DVE, lambda c0=c0, n=n: nc.vector.tensor_copy(out=actT[:, c0:c0 + n, :], in_=V(pt_[:, 0:n * 128], "p (c n) -> p c n", c=n)), [pt_.res], [actT.res])
                        yield
                    for nb in range(2):
                        ns = slice(nb * 512, (nb + 1) * 512)
                        for c in range(22):
                            k.op(PE, lambda c=c, ns=ns: nc.tensor.matmul(po[:], lhsT=actT[:, c, :], rhs=wf2[:, c, ns], start=(c == 0), stop=(c == 21)),
                                 [actT.res, wf2.res], [po.res], inc=(c == 21))
                        yield
                        k.op(DVE, lambda ns=ns: nc.vector.tensor_tensor(out=xo_[:, ns], in0=po[:], in1=xt[:, ns], op=ALU.add), [po.res, xt.res], [xo_.res])
                        yield
                    if l < NL - 1:
                        k.op(ACT, lambda: nc.scalar.activation(out=junk2[:], in_=xo_[:], func=AF.Square, accum_out=rstdP[:, t:t + 1]),
                             [xo_.res], [junk2.res, rstdP.res])
                    k.dma(SP, cho[j], dst_y[rows, :], xo_[:], [xo_.res], [k.dres(dst_y.tensor.name, t)])

                def lockstep4(gens):
                    gens = list(gens)
                    while gens:
                        nxt = []
                        for g_ in gens:
                            try:
                                next(g_)
                                nxt.append(g_)
                            except StopIteration:
                                pass
                        gens = nxt

                for t0 in range(0, NT, K2):
                    lockstep4([gen_m2(j, t0 + j) for j in range(K2)])
                if l < NL - 1:
                    k.op(DVE, lambda: nc.vector.tensor_scalar(out=rstdP[:], in0=rstdP[:], scalar1=1.0 / D, scalar2=EPS, op0=ALU.mult, op1=ALU.add), [rstdP.res], [rstdP.res])
                    rinv(None, rstdP[:], rstdP.res)
                k.barrier()

        try:
            run_layers()
        except _Stop:
            pass
        k.barrier()
    return nc


def make_consts():
    p = np.arange(128)[:, None]
    f = np.arange(128)[None, :]
    same = (p // 64) == (f // 64)
    c = np.zeros((NCONST, 128, 128), np.float32)
    c[0] = (p == f)
    c[1] = (p >= f)
    c[2] = (p <= f)
    c[3] = (p >= f) & (p >= 64)
    c[4] = (p <= f) & (p < 64)
    c[5] = (p <= f) & same
    c[6] = (p >= f) & same
    c[7] = np.where((f >= p) & same, 0.0, -1e30)
    c[8] = np.where((f <= p) & same, 0.0, -1e30)
    c[9] = (f > p) & same
    c[10] = (f < p) & same
    c[11] = 1.0
    c[12] = same
    return np.ascontiguousarray(c.transpose(1, 0, 2).reshape(128, NCONST * 128))


def make_rope(T, positions):
    inv = np.power(np.float32(500000.0), -np.arange(0, 16, 2, dtype=np.float32) / np.float32(16))
    ang = positions.astype(np.float32)[:, None] * inv[None, :]
    cos, sin = np.cos(ang).astype(np.float32), np.sin(ang).astype(np.float32)
    tab = np.concatenate([cos, cos, sin, sin], axis=1)
    out = np.zeros((3, T, 32), np.float32)
    for g, d in enumerate(DIL):
        perm = np.arange(T).reshape(T // d, d).T.reshape(-1)
        out[g] = tab[perm]
    return np.ascontiguousarray(out.reshape(3, T // 128, 128, 32).transpose(0, 2, 1, 3).reshape(3, 128, (T // 128) * 32))


_PROG = {}


DBG = None


def run_cores(core_inputs, T, NL, weights):
    key = (T, NL)
    if key not in _PROG:
        _PROG[key] = build_program(T, NL, DBG)
    nc = _PROG[key]
    consts = make_consts()
    in_maps = []
    for (x, linked, pos) in core_inputs:
        m = {"x": np.ascontiguousarray(x, dtype=np.float32), "link": np.full((128, 1), 1.0 if linked else 0.0, np.float32),
             "rope": make_rope(T, pos), "consts": consts}
        m.update(weights)
        in_maps.append(m)
    res = run_bass_kernel_spmd(nc, in_maps, core_ids=list(range(len(in_maps))))
    return [r["y"] for r in res.results]


def kernel(x_prompt, x_sample, norm1, w_in, qk_gain, sink, conv_w, a_log, dt_bias, o_gain, w_gate,
           w_br_a, w_br_b, w_br_c, w_out, norm2, w_ffn_in, w_ffn_out):
    T = 8192
    NL = 2
    f = lambda a: np.ascontiguousarray(np.asarray(a, dtype=np.float32))
    weights = {"norm1": f(norm1), "w_in": f(w_in), "qk_gain": f(qk_gain), "sink": f(sink), "conv_w": f(conv_w),
               "a_log": f(a_log).reshape(NL, 16), "dt_bias": f(dt_bias).reshape(NL, 16), "o_gain": f(o_gain), "w_gate": f(w_gate),
               "w_br_a": f(w_br_a), "w_br_b": f(w_br_b), "w_br_c": f(w_br_c), "w_out": f(w_out), "norm2": f(norm2),
               "w_ffn_in": f(w_ffn_in), "w_ffn_out": f(w_ffn_out)}
    xp, xs = f(x_prompt), f(x_sample)
    pos_full = np.arange(T)
    pos_half = np.concatenate([np.arange(T // 2), np.arange(T // 2)])
    zeros = np.zeros((T // 2, D), np.float32)
    cores = [(xp[0], True, pos_full), (xp[1], True, pos_full),
             (np.concatenate([xs[0], xs[1]], 0), False, pos_half), (np.concatenate([xs[2], xs[3]], 0), False, pos_half),
             (np.concatenate([xs[4], zeros], 0), False, pos_half), (np.concatenate([xs[5], zeros], 0), False, pos_half),
             (np.concatenate([xs[6], zeros], 0), False, pos_half), (np.concatenate([xs[7], zeros], 0), False, pos_half)]
    ys = run_cores(cores, T, NL, weights)
    y_prompt = np.stack([ys[0], ys[1]], 0).astype(np.float32)
    h = T // 2
    y_sample = np.stack([ys[2][:h], ys[2][h:], ys[3][:h], ys[3][h:], ys[4][:h], ys[5][:h], ys[6][:h], ys[7][:h]], 0).astype(np.float32)
    return (y_prompt, y_sample)
```
